# Optimizing a Trainium2 kernel written in Bass

```python
import jax, jax.numpy as jnp
from jax import lax
import numpy as np

D_MODEL = 1024
BATCH = 32
SEQ = 256
DEPTH = 2
DEC_BATCH = 4
DEC_SEQ = 2048
PAST_LEN = 256

GRID_W = 64
D_MIX = D_MODEL
HEAD_A = 64
D_A = D_MIX // 2
H_A = D_A // HEAD_A
D_B = D_MIX // 4
H_B = 4
BLK_B = D_B // H_B
D_C = D_MIX - D_A - D_B
HEAD_C = 64
H_C = D_C // HEAD_C
LORA_W = 64
LORA_A = 64
LORA_G = 128
CONV_B = 4
CONV_F = 3
RG_C = 8.0
CHUNK_C = 64
D_FF = ((8 * D_MODEL // 3 + 255) // 256) * 256
N_DIR = 2
N_MOD = 6
RMS_EPS = 1e-6
GN_EPS = 64e-5
SPLIT_SIZES = (D_A, D_A, D_A, LORA_W, LORA_W, LORA_A, LORA_A, LORA_G,
               D_B, D_B,
               D_C, D_C, D_C, D_C, D_C)
IN_COLS = sum(SPLIT_SIZES)

kernel_name = 'hybrid_prefix_diffusion_step'


def rms_norm(x, g):
    xf = x.astype(jnp.float32)
    y = xf * lax.rsqrt(jnp.mean(xf * xf, axis=-1, keepdims=True) + RMS_EPS)
    return (y * g.astype(jnp.float32)).astype(x.dtype)


def dwconv(x, w, b, pad):
    y = lax.conv_general_dilated(x, w[:, None, :].astype(x.dtype), window_strides=(1,), padding=[pad],
                                 dimension_numbers=('NWC', 'WIO', 'NWC'), feature_group_count=x.shape[-1])
    return y + b.astype(x.dtype)


def modulation(cond, ada_w, ada_b):
    m = jax.nn.silu(cond) @ ada_w + ada_b
    return jnp.split(m[:, None, :], N_MOD, axis=-1)


def rwkv7_recurrence(r, w, k, v, kk, a, s0, reverse):
    def step(S, inp):
        r_t, w_t, k_t, v_t, kk_t, a_t = inp
        s_kk = jnp.einsum('bhvk,bhk->bhv', S, kk_t)
        S = (S * w_t[:, :, None, :] - s_kk[..., None] * (kk_t * a_t)[:, :, None, :]
             + v_t[..., None] * k_t[:, :, None, :])
        return S, jnp.einsum('bhvk,bhk->bhv', S, r_t)
    xs = tuple(jnp.moveaxis(t, 1, 0) for t in (r, w, k, v, kk, a))
    s_fin, o = lax.scan(step, s0, xs, reverse=reverse)
    return jnp.moveaxis(o, 0, 1), s_fin


def rwkv7_mixer(r, k, v, wd, ad, gd, s0, p):
    Bn, T, _ = r.shape
    heads = lambda t: t.reshape(Bn, T, H_A, HEAD_A)
    kk = heads(k * p['rwkv_k_k'])
    kk = kk * lax.rsqrt(jnp.sum(kk * kk, axis=-1, keepdims=True) + 1e-12)
    rh, vh = heads(r), heads(v)
    outs, finals = [], []
    for d in range(N_DIR):
        wl = p['rwkv_w0'][d] + jnp.tanh(wd[d]) @ p['rwkv_w_up'][d]
        decay = jnp.exp(-jnp.exp(-jax.nn.softplus(-wl) - 0.5))
        a = jax.nn.sigmoid(p['rwkv_a0'][d] + ad[d] @ p['rwkv_a_up'][d])
        kd = k * (1.0 + (a - 1.0) * p['rwkv_k_a'])
        od, sd = rwkv7_recurrence(rh, heads(decay), heads(kd), vh, kk, heads(a), s0[:, d], reverse=(d == 1))
        outs.append(od)
        finals.append(sd)
    o = outs[0] + outs[1]
    mu = jnp.mean(o, axis=-1, keepdims=True)
    var = jnp.mean(jnp.square(o - mu), axis=-1, keepdims=True)
    o = ((o - mu) * lax.rsqrt(var + GN_EPS)).reshape(Bn, T, D_A) * p['rwkv_ln_w'] + p['rwkv_ln_b']
    bonus = jnp.sum(rh * heads(k) * p['rwkv_r_k'], axis=-1, keepdims=True) * vh
    o = (o + bonus.reshape(Bn, T, D_A)) * (jax.nn.sigmoid(gd) @ p['rwkv_g_up'])
    return o, jnp.stack(finals, axis=1)


def diag_linear_scan(a, b, h0, reverse):
    def combine(e, l):
        return (e[0] * l[0], l[0] * e[1] + l[1])
    a_cum, b_cum = lax.associative_scan(combine, (a, b), axis=1, reverse=reverse)
    h = a_cum * h0[:, None, :] + b_cum
    return h, (h[:, 0] if reverse else h[:, -1])


def rglru_mixer(xb, gb, h0, p):
    Bn, T, _ = xb.shape
    u = dwconv(xb, p['lru_conv_w'], p['lru_conv_b'], (CONV_B // 2, CONV_B - 1 - CONV_B // 2))
    ub = u.reshape(Bn, T, H_B, BLK_B)
    hs, finals = [], []
    for d in range(N_DIR):
        rg = jax.nn.sigmoid(jnp.einsum('bthi,hij->bthj', ub, p['lru_wa'][d]).reshape(Bn, T, D_B) + p['lru_ba'][d])
        ig = jax.nn.sigmoid(jnp.einsum('bthi,hij->bthj', ub, p['lru_wx'][d]).reshape(Bn, T, D_B) + p['lru_bx'][d])
        log_a = -RG_C * rg * jax.nn.softplus(-p['lru_lambda'][d])
        h, h_fin = diag_linear_scan(jnp.exp(log_a), jnp.sqrt(-jnp.expm1(2.0 * log_a)) * (ig * u),
                                    h0[:, d], reverse=(d == 1))
        hs.append(h)
        finals.append(h_fin)
    y = (hs[0] + hs[1]) * jax.nn.gelu(gb)
    return y, jnp.stack(finals, axis=1)


def hgrn2_chunk_scan(q, log_f, k, v, s0):
    Bn, T, H, _ = q.shape
    n_chunks = T // CHUNK_C
    chunks = lambda t: jnp.moveaxis(t.reshape(Bn, n_chunks, CHUNK_C, H, t.shape[-1]), 1, 0)
    causal = jnp.tril(jnp.ones((CHUNK_C, CHUNK_C), dtype=bool))[None, :, :, None, None]

    def step(S, inp):
        qc, lfc, kc, vc = inp
        b = jnp.cumsum(lfc, axis=1)
        o_inter = jnp.einsum('blhk,bhkv->blhv', qc * jnp.exp(b), S)
        rel = jnp.exp(jnp.where(causal, b[:, :, None] - b[:, None, :], -jnp.inf))
        att = jnp.einsum('bthk,bshk,btshk->bhts', qc, kc, rel)
        o_intra = jnp.einsum('bhts,bshv->bthv', att, vc)
        b_last = b[:, -1]
        S = (jnp.exp(b_last)[..., None] * S
             + jnp.einsum('bshk,bshv->bhkv', kc * jnp.exp(b_last[:, None] - b), vc))
        return S, o_inter + o_intra

    s_fin, o = lax.scan(step, s0, tuple(chunks(t) for t in (q, log_f, k, v)))
    return jnp.moveaxis(o, 0, 1).reshape(Bn, T, H, v.shape[-1]), s_fin


def hgrn2_mixer(q, f_raw, iv, og, s0, p):
    Bn, T, _ = q.shape
    heads = lambda t: t.reshape(Bn, T, H_C, HEAD_C)
    qh, vh = heads(jax.nn.silu(q)), heads(iv)
    lb = p['hgrn_lb']
    outs, finals = [], []
    for d in range(N_DIR):
        f = lb[d] + (1.0 - lb[d]) * jax.nn.sigmoid(f_raw[d])
        args = (qh, heads(jnp.log(f)), heads(1.0 - f), vh)
        if d == 1:
            args = tuple(jnp.flip(t, axis=1) for t in args)
        od, sd = hgrn2_chunk_scan(*args, s0[:, d])
        outs.append(jnp.flip(od, axis=1) if d == 1 else od)
        finals.append(sd)
    o = outs[0] + outs[1]
    o = o * lax.rsqrt(jnp.mean(o * o, axis=-1, keepdims=True) + RMS_EPS)
    o = o.reshape(Bn, T, D_C) * p['hgrn_norm_g'] * jax.nn.silu(og)
    return o, jnp.stack(finals, axis=1)


def conv_ffn(h, p, on_grid):
    g = h @ p['ffn_w_gate']
    Bn, T, F = g.shape
    if on_grid:
        rows = T // GRID_W
        g = dwconv(g.reshape(Bn * rows, GRID_W, F), p['ffn_conv_w'], p['ffn_conv_b'], (1, 1)).reshape(Bn, T, F)
    else:
        g = dwconv(g, p['ffn_conv_w'], p['ffn_conv_b'], (1, 1))
    return (jax.nn.silu(g) * (h @ p['ffn_w_up'])) @ p['ffn_w_down']


def trunk_layer(x, mods, s_rwkv, s_lru, s_hgrn, p, on_grid):
    shift1, scale1, gate1, shift2, scale2, gate2 = mods
    h = rms_norm(x, p['norm_mix_g']) * (1.0 + scale1) + shift1
    idx = np.cumsum(SPLIT_SIZES)[:-1].tolist()
    (r, k, v, wd_f, wd_b, ad_f, ad_b, gd, xb, gb, q, ff, fb, iv, og) = jnp.split(
        (h @ p['w_in']).astype(jnp.float32), idx, axis=-1)
    o_a, s_a = rwkv7_mixer(r, k, v, (wd_f, wd_b), (ad_f, ad_b), gd, s_rwkv, p)
    o_b, s_b = rglru_mixer(xb, gb, s_lru, p)
    o_c, s_c = hgrn2_mixer(q, (ff, fb), iv, og, s_hgrn, p)
    mix = jnp.concatenate([o_a, o_b, o_c], axis=-1).astype(x.dtype) @ p['w_out']
    x = x + gate1 * mix
    h = rms_norm(x, p['norm_ffn_g']) * (1.0 + scale2) + shift2
    x = x + gate2 * conv_ffn(h, p, on_grid)
    return x, s_a, s_b, s_c


def setup_inputs(seed: int = 0) -> dict:
    key = jax.random.key(seed)
    ks = iter(jax.random.split(key, 64))
    nrm = lambda shape, scale: scale * jax.random.normal(next(ks), shape, jnp.float32)
    unif = lambda shape, lo, hi: jax.random.uniform(next(ks), shape, jnp.float32, lo, hi)
    L, D = DEPTH, D_MODEL
    return {
        'x_prompt': nrm((BATCH, SEQ, D), 1.0),
        'x_sample': nrm((DEC_BATCH, DEC_SEQ, D), 1.0),
        'state_rwkv': nrm((DEC_BATCH, DEPTH, N_DIR, H_A, HEAD_A, HEAD_A), 0.5),
        'state_rglru': nrm((DEC_BATCH, DEPTH, N_DIR, D_B), 0.5),
        'state_hgrn': nrm((DEC_BATCH, DEPTH, N_DIR, H_C, HEAD_C, HEAD_C), 0.5),
        'c': nrm((DEC_BATCH, D), 1.0),
        'c_ctx': nrm((D,), 1.0),
        'ada_w': nrm((L, D, N_MOD * D), 0.5 * D ** -0.5),
        'ada_b': nrm((L, N_MOD * D), 0.02),
        'norm_mix_g': 1.0 + nrm((L, D), 0.05),
        'norm_ffn_g': 1.0 + nrm((L, D), 0.05),
        'w_in': nrm((L, D, IN_COLS), D ** -0.5),
        'w_out': nrm((L, D_MIX, D), D_MIX ** -0.5),
        'rwkv_w0': unif((L, N_DIR, D_A), -6.0, 0.0),
        'rwkv_w_up': nrm((L, N_DIR, LORA_W, D_A), 0.1 * LORA_W ** -0.5),
        'rwkv_a0': nrm((L, N_DIR, D_A), 0.1),
        'rwkv_a_up': nrm((L, N_DIR, LORA_A, D_A), 0.1 * LORA_A ** -0.5),
        'rwkv_g_up': nrm((L, LORA_G, D_A), LORA_G ** -0.5),
        'rwkv_k_k': 0.85 + nrm((L, D_A), 0.05),
        'rwkv_k_a': 1.0 + nrm((L, D_A), 0.05),
        'rwkv_r_k': nrm((L, H_A, HEAD_A), 0.1),
        'rwkv_ln_w': 1.0 + nrm((L, D_A), 0.05),
        'rwkv_ln_b': nrm((L, D_A), 0.02),
        'lru_conv_w': nrm((L, CONV_B, D_B), CONV_B ** -0.5),
        'lru_conv_b': nrm((L, D_B), 0.02),
        'lru_wa': nrm((L, N_DIR, H_B, BLK_B, BLK_B), BLK_B ** -0.5),
        'lru_ba': nrm((L, N_DIR, D_B), 0.02),
        'lru_wx': nrm((L, N_DIR, H_B, BLK_B, BLK_B), BLK_B ** -0.5),
        'lru_bx': nrm((L, N_DIR, D_B), 0.02),
        'lru_lambda': unif((L, N_DIR, D_B), 4.3, 9.0),
        'hgrn_lb_logits': nrm((N_DIR, L, D_C), 1.0),
        'hgrn_norm_g': 1.0 + nrm((L, D_C), 0.05),
        'ffn_w_gate': nrm((L, D, D_FF), D ** -0.5),
        'ffn_w_up': nrm((L, D, D_FF), D ** -0.5),
        'ffn_conv_w': nrm((L, CONV_F, D_FF), CONV_F ** -0.5),
        'ffn_conv_b': nrm((L, D_FF), 0.02),
        'ffn_w_down': nrm((L, D_FF, D), D_FF ** -0.5),
        'final_g': 1.0 + nrm((D,), 0.05),
    }


def reference(x_prompt, x_sample, state_rwkv, state_rglru, state_hgrn, c, c_ctx, ada_w, ada_b,
              norm_mix_g, norm_ffn_g, w_in, w_out, rwkv_w0, rwkv_w_up, rwkv_a0, rwkv_a_up, rwkv_g_up,
              rwkv_k_k, rwkv_k_a, rwkv_r_k, rwkv_ln_w, rwkv_ln_b, lru_conv_w, lru_conv_b, lru_wa, lru_ba,
              lru_wx, lru_bx, lru_lambda, hgrn_lb_logits, hgrn_norm_g, ffn_w_gate, ffn_w_up, ffn_conv_w,
              ffn_conv_b, ffn_w_down, final_g):
    f32 = jnp.float32
    lb_all = jnp.cumsum(jax.nn.softmax(hgrn_lb_logits.astype(f32), axis=1), axis=1)
    lb_all = lb_all - lb_all[:, :1]
    n_ctx = x_prompt.shape[0]
    z_rwkv = jnp.zeros((n_ctx, N_DIR, H_A, HEAD_A, HEAD_A), f32)
    z_lru = jnp.zeros((n_ctx, N_DIR, D_B), f32)
    z_hgrn = jnp.zeros((n_ctx, N_DIR, H_C, HEAD_C, HEAD_C), f32)
    xp, xs = x_prompt, x_sample
    new_rwkv, new_lru, new_hgrn = [], [], []
    for l in range(DEPTH):
        p = {
            'norm_mix_g': norm_mix_g[l], 'norm_ffn_g': norm_ffn_g[l], 'w_in': w_in[l], 'w_out': w_out[l],
            'rwkv_w0': rwkv_w0[l], 'rwkv_w_up': rwkv_w_up[l], 'rwkv_a0': rwkv_a0[l], 'rwkv_a_up': rwkv_a_up[l],
            'rwkv_g_up': rwkv_g_up[l], 'rwkv_k_k': rwkv_k_k[l], 'rwkv_k_a': rwkv_k_a[l], 'rwkv_r_k': rwkv_r_k[l],
            'rwkv_ln_w': rwkv_ln_w[l], 'rwkv_ln_b': rwkv_ln_b[l],
            'lru_conv_w': lru_conv_w[l], 'lru_conv_b': lru_conv_b[l], 'lru_wa': lru_wa[l], 'lru_ba': lru_ba[l],
            'lru_wx': lru_wx[l], 'lru_bx': lru_bx[l], 'lru_lambda': lru_lambda[l],
            'hgrn_lb': lb_all[:, l], 'hgrn_norm_g': hgrn_norm_g[l],
            'ffn_w_gate': ffn_w_gate[l], 'ffn_w_up': ffn_w_up[l], 'ffn_conv_w': ffn_conv_w[l],
            'ffn_conv_b': ffn_conv_b[l], 'ffn_w_down': ffn_w_down[l],
        }
        xp, s_a, s_b, s_c = trunk_layer(xp, modulation(c_ctx[None, :], ada_w[l], ada_b[l]),
                                        z_rwkv, z_lru, z_hgrn, p, on_grid=False)
        new_rwkv.append(s_a)
        new_lru.append(s_b)
        new_hgrn.append(s_c)
        xs, _, _, _ = trunk_layer(xs, modulation(c, ada_w[l], ada_b[l]),
                                  state_rwkv[:, l].astype(f32), state_rglru[:, l].astype(f32),
                                  state_hgrn[:, l].astype(f32), p, on_grid=True)
    y_prompt = rms_norm(xp, final_g)
    y_sample = rms_norm(xs, final_g)
    new_rwkv_state = jnp.stack(new_rwkv, axis=1).astype(x_prompt.dtype)
    new_rglru_state = jnp.stack(new_lru, axis=1).astype(x_prompt.dtype)
    new_hgrn_state = jnp.stack(new_hgrn, axis=1).astype(x_prompt.dtype)
    return (y_prompt, y_sample, new_rwkv_state, new_rglru_state, new_hgrn_state)
```

```python
import contextlib, math
import numpy as np
import concourse.bass as bass
import concourse.mybir as mybir
from concourse.bass_utils import run_bass_kernel_spmd

F32 = mybir.dt.float32
BF16 = mybir.dt.bfloat16
AF = mybir.ActivationFunctionType
ALU = mybir.AluOpType

T = 2048
D = 1024
NPV = 228
NCONST = 640
DFF = 2816
NFC = 22
CW = -math.exp(-0.5)
PV_RW, PV_LRU, PV_HG, PV_FW, PV_FB = 72, 108, 130, 140, 206
DBG = {}


class Dep:
    __slots__ = ("w", "r")

    def __init__(s):
        s.w = None
        s.r = {}


class Tl:
    def __init__(s, ap):
        s.ap = ap
        s.d = Dep()


class Rot:
    def __init__(s, tl):
        s.tl = tl
        s.i = 0

    def next(s):
        x = s.tl[s.i % len(s.tl)]
        s.i += 1
        return x


class B:
    def __init__(s, nc, es):
        s.nc = nc
        s.eng = {}
        for name, obj in (("pe", nc.tensor), ("dve", nc.vector), ("act", nc.scalar),
                          ("pool", nc.gpsimd), ("sp", nc.sync)):
            sem = es.enter_context(nc.semaphore("sem_" + name))
            s.eng[name] = dict(obj=obj, sem=sem, cnt=0, known={})
        s.dsemq = {q: [[es.enter_context(nc.semaphore("dsem%s%d" % (q, i))), 0] for i in range(24)] for q in ("sp", "pool")}
        s.dsem = s.dsemq["sp"] + s.dsemq["pool"]
        s.di = {"sp": 0, "pool": 0}
        s.nins = 0

    def _wait(s, e, evs):
        best = {}
        for (sem, val) in evs:
            k = id(sem)
            if k not in best or best[k][1] < val:
                best[k] = (sem, val)
        for k, (sem, val) in best.items():
            if e["known"].get(k, 0) < val:
                e["obj"].wait_ge(sem, val)
                e["known"][k] = val

    def _collect(s, en, r, w):
        evs = []
        for t in r:
            if t.w is not None:
                evs.append(t.w[:2])
        for t in w:
            if t.w is not None and not (t.w[2] == en and en == "pe"):
                evs.append(t.w[:2])
            for rd in t.r.values():
                if not (rd[2] == en and en == "pe"):
                    evs.append(rd[:2])
        return evs

    def _update(s, ev, r, w):
        for t in r:
            t.r[id(ev[0])] = ev
        for t in w:
            t.w = ev
            t.r = {}

    @staticmethod
    def _flat(lst):
        out = []
        for x in lst:
            if isinstance(x, (list, tuple)):
                out += B._flat(x)
            else:
                out.append(x.d if isinstance(x, Tl) else x)
        return out

    def op(s, en, fn, r=(), w=()):
        r = s._flat(r)
        w = s._flat(w)
        e = s.eng[en]
        s._wait(e, s._collect(en, r, w))
        ins = fn(e["obj"])
        e["cnt"] += 1
        ins.then_inc(e["sem"], 1)
        s.nins += 1
        ev = (e["sem"], e["cnt"], en)
        s._update(ev, r, w)
        return ev

    def dma(s, qn, out, in_, r=(), w=()):
        r = s._flat(r)
        w = s._flat(w)
        e = s.eng[qn]
        slot = s.dsemq[qn][s.di[qn] % 24]
        s.di[qn] += 1
        evs = s._collect(None, r, w)
        if slot[1] > 0:
            evs.append((slot[0], slot[1]))
        s._wait(e, evs)
        ins = e["obj"].dma_start(out=out, in_=in_)
        slot[1] += 16
        ins.then_inc(slot[0], 16)
        s.nins += 1
        ev = (slot[0], slot[1], "dma")
        s._update(ev, r, w)
        return ev

    def barrier(s):
        for en, e in s.eng.items():
            evs = []
            for on, o in s.eng.items():
                if on != en and o["cnt"] > 0:
                    evs.append((o["sem"], o["cnt"]))
            for sl in s.dsem:
                if sl[1] > 0:
                    evs.append((sl[0], sl[1]))
            s._wait(e, evs)


class StopBuild(Exception):
    pass


def build_nc(stop=None):
    nc = bass.Bass("TRN2", target_bir_lowering=False)
    try:
        _build(nc, stop)
    except StopBuild:
        pass
    return nc


def _build(nc, stop):

    def din(name, shape):
        return nc.dram_tensor(name, list(shape), F32, kind="ExternalInput").ap()

    def dout(name, shape):
        return nc.dram_tensor(name, list(shape), F32, kind="ExternalOutput").ap()

    x_d = din("x", [T, D])
    cond_d = din("cond", [128, 8])
    carry_d = din("carry", [128, 1])
    fmask_d = din("fmask", [128, 7])
    consts_d = din("consts", [128, NCONST])
    pv_d = din("pv", [2, 128, NPV])
    irw_d = din("irw", [2, 2, 8, 4, 128, 64])
    ilru_d = din("ilru", [2, 2, 128, 16])
    ihg_d = din("ihg", [2, 2, 8, 2, 128, 64])
    ada_w_d = din("ada_w", [2, D, 6 * D])
    w_in_d = din("w_in", [2, D, 3712])
    w_out_d = din("w_out", [2, D, D])
    w_up_d = din("rwkv_w_up", [2, 128, 512])
    a_up_d = din("rwkv_a_up", [2, 128, 512])
    g_up_d = din("rwkv_g_up", [2, 128, 512])
    lwa_d = din("lru_wa", [2, 2, 4, 64, 64])
    lwx_d = din("lru_wx", [2, 2, 4, 64, 64])
    wg_d = din("ffn_w_gate", [2, D, DFF])
    wu_d = din("ffn_w_up", [2, D, DFF])
    wd_d = din("ffn_w_down", [2, DFF, D])
    y_d = dout("y", [T, D])
    frw_d = dout("frw", [2, 2, 8, 4, 128, 64])
    flru_d = dout("flru", [2, 2, 128, 16])
    fhg_d = dout("fhg", [2, 2, 8, 2, 128, 64])
    xscr = nc.dram_tensor("xscr", [128, 8, T], F32, kind="Internal").ap()
    mixscr = nc.dram_tensor("mixscr", [8, 128, T], BF16, kind="Internal").ap()
    dbg_d = dout("dbg", [128, 8 * T]) if stop is not None else None

    with contextlib.ExitStack() as es:
        b = B(nc, es)

        uid = [0]

        def sb(name, shape, dt=F32, st=es):
            uid[0] += 1
            return st.enter_context(nc.sbuf_tensor("%s_%d" % (name, uid[0]), list(shape), dt))

        def tl(name, shape, dt=F32, st=es):
            return Tl(sb(name, shape, dt, st))

        def rot(name, n, shape, dt=F32, st=es):
            return Rot([tl("%s%d" % (name, i), shape, dt, st) for i in range(n)])

        psA = Rot([Tl(es.enter_context(nc.psum_tensor("psA%d" % i, [128, 512], F32))) for i in range(8)])
        psB = psA

        cst = sb("cst", [128, NCONST])
        identb = sb("identb", [128, 128], BF16)
        bonesb = sb("bonesb", [128, 128], BF16)
        onesb = sb("onesb", [128, 128], BF16)
        pv = sb("pv", [128, 2, NPV])
        carry = sb("carry", [128, 1])
        fmask = sb("fmask", [128, 7])
        condt = sb("condt", [128, 8])
        scb = sb("scb", [128, 8], BF16)
        modt = sb("modt", [128, 2, 48])
        gs = sb("gs", [128, 2, 16])
        der = sb("der", [128, 2, 32])
        hT = sb("hT", [128, 8, T], BF16)
        flru = sb("flru", [128, 2, 2, 16])
        ilru = sb("ilru", [128, 2, 2, 16])
        d_hT = [Dep() for _ in range(4)]
        d_mix = [Dep() for _ in range(8)]
        d_x = [[Dep() for _ in range(8)] for _ in range(4)]
        d_flru = Dep()
        wtile = rot("wt", 3, [128, 8, 128], BF16)
        ident = cst[:, 0:128]
        m1 = cst[0:64, 256:384]
        m2 = cst[0:64, 384:512]
        m3 = cst[0:64, 512:576]
        mh = cst[0:32, 576:608]

        def ck(n, src=None):
            if stop is not None and stop == n:
                b.barrier()
                if src is not None:
                    b.dma("pool", dbg_d[0:src.shape[0], 0:src.shape[1]], src)
                b.barrier()
                DBG["nins"] = b.nins
                raise StopBuild()


        d0 = Dep()
        b.dma("sp", cst[:], consts_d[:, :], w=[d0])
        for l in range(2):
            b.dma("sp", pv[:, l, :], pv_d[l], w=[d0])
        b.dma("sp", carry[:], carry_d[:, :], w=[d0])
        b.dma("sp", fmask[:], fmask_d[:, :], w=[d0])
        b.dma("sp", condt[:], cond_d[:, :], w=[d0])
        for l in range(2):
            b.dma("sp", ilru[:, l, :, :], ilru_d[l].rearrange("j p s -> p j s"), w=[d0])
        b.op("dve", lambda e: e.tensor_copy(out=identb[:], in_=cst[:, 0:128]), r=[d0], w=[d0])
        b.op("dve", lambda e: e.tensor_copy(out=bonesb[:], in_=cst[:, 128:256]), r=[d0], w=[d0])
        b.op("dve", lambda e: e.memset(onesb[:], 1.0), w=[d0])
        b.op("dve", lambda e: e.memset(flru[:], 0.0), w=[d0])
        b.barrier()
        b.op("act", lambda e: e.activation(out=scb[:], in_=condt[:], func=AF.Silu), w=[d0])
        for l in range(2):
            for j in range(4):
                c = PV_RW + j * 9 + 5
                b.op("dve", lambda e, c=c, j=j: e.tensor_scalar(out=der[:, l, j:j + 1], in0=pv[:, l, c:c + 1],
                                                               scalar1=-1.0, scalar2=1.0, op0=ALU.mult, op1=ALU.add), w=[d0])
            for j in range(2):
                for d in range(2):
                    c = PV_LRU + j * 11 + 9 + d
                    o = 4 + j * 2 + d
                    b.op("act", lambda e, c=c, o=o: e.activation(out=der[:, l, o:o + 1], in_=pv[:, l, c:c + 1],
                                                                 func=AF.Exp, scale=-1.0), w=[d0])
                    b.op("act", lambda e, o=o: e.activation(out=der[:, l, o:o + 1], in_=der[:, l, o:o + 1],
                                                            func=AF.Ln, bias=1.0), r=[d0], w=[d0])
                    b.op("dve", lambda e, o=o: e.tensor_scalar(out=der[:, l, o + 4:o + 5], in0=der[:, l, o:o + 1],
                                                              scalar1=-16.0, scalar2=None, op0=ALU.mult), r=[d0], w=[d0])
                    b.op("dve", lambda e, o=o: e.tensor_scalar(out=der[:, l, o:o + 1], in0=der[:, l, o:o + 1],
                                                              scalar1=-8.0, scalar2=None, op0=ALU.mult), r=[d0], w=[d0])
                    o2 = 12 + j * 2 + d
                    if l == 0:
                        b.op("dve", lambda e, o2=o2: e.memset(der[:, l, o2:o2 + 1], 0.0), w=[d0])
                        b.op("dve", lambda e, o2=o2: e.memset(der[:, l, o2 + 4:o2 + 5], 1.0), w=[d0])
                    else:
                        ch = PV_HG + j * 5 + d * 2
                        b.op("dve", lambda e, o2=o2, ch=ch: e.tensor_tensor(out=der[:, l, o2:o2 + 1], in0=pv[:, l, ch + 1:ch + 2],
                                                                            in1=pv[:, l, ch:ch + 1], op=ALU.subtract), w=[d0])
                        b.op("act", lambda e, o2=o2: e.activation(out=der[:, l, o2:o2 + 1], in_=der[:, l, o2:o2 + 1],
                                                                  func=AF.Sigmoid), r=[d0], w=[d0])
                        b.op("dve", lambda e, o2=o2: e.tensor_scalar(out=der[:, l, o2 + 4:o2 + 5], in0=der[:, l, o2:o2 + 1],
                                                                    scalar1=-1.0, scalar2=1.0, op0=ALU.mult, op1=ALU.add), r=[d0], w=[d0])
        b.barrier()
        ck(1, der[:, 0, :])

        with contextlib.ExitStack() as ph:
            apc = rot("apc", 2, [128, 8, 512], BF16, ph)
            for l in range(2):
                ps = psA.next()
                for pc in range(12):
                    wt = apc.next()
                    b.dma("pool", wt.ap[:], ada_w_d[l].rearrange("(j p) c -> p j c", p=128)[:, :, pc * 512:(pc + 1) * 512], w=[wt])
                    for mm in range(4):
                        m = pc * 4 + mm
                        for j in range(8):
                            b.op("pe", lambda e, j=j, m=m, mm=mm, wt=wt, ps=ps: e.matmul(
                                ps.ap[:, m:m + 1], lhsT=wt.ap[:, j, mm * 128:(mm + 1) * 128], rhs=scb[:, j:j + 1],
                                start=(j == 0), stop=(j == 7)), r=[wt], w=[ps])
                b.op("dve", lambda e, ps=ps, l=l: e.tensor_tensor(out=modt[:, l, :], in0=ps.ap[:, 0:48], in1=pv[:, l, 16:64], op=ALU.add),
                     r=[ps], w=[d0])
                b.op("dve", lambda e, l=l: e.scalar_tensor_tensor(out=gs[:, l, 0:8], in0=modt[:, l, 8:16], scalar=1.0, in1=pv[:, l, 0:8],
                                                                  op0=ALU.add, op1=ALU.mult), r=[d0], w=[d0])
                b.op("dve", lambda e, l=l: e.scalar_tensor_tensor(out=gs[:, l, 8:16], in0=modt[:, l, 32:40], scalar=1.0, in1=pv[:, l, 8:16],
                                                                  op0=ALU.add, op1=ALU.mult), r=[d0], w=[d0])
        b.barrier()
        ck(2, modt[:, 0, :])

        def blk(bi):
            return slice(bi * 512, (bi + 1) * 512)

        def norm_phase(xT, ph, gfn, sfn, outfn):
            sqb = rot("nsq", 3, [128, 512], BF16, ph)
            rsb = rot("nrs", 2, [128, 512], F32, ph)
            tmb = rot("ntm", 3, [128, 512], F32, ph)
            for bi in range(4):
                ps = psA.next()
                for j in range(8):
                    sq = sqb.next()
                    b.op("act", lambda e, j=j, sq=sq: e.activation(out=sq.ap[:], in_=xT[:, j, blk(bi)], func=AF.Square),
                         r=[d_x[bi][j]], w=[sq])
                    b.op("pe", lambda e, j=j, sq=sq, ps=ps: e.matmul(ps.ap[:, :], lhsT=onesb[:], rhs=sq.ap[:], start=(j == 0), stop=(j == 7)),
                         r=[sq], w=[ps])
                rs = rsb.next()
                b.op("act", lambda e, rs=rs, ps=ps: e.activation(out=rs.ap[:], in_=ps.ap[:, :], func=AF.Ln, scale=1.0 / D, bias=1e-6),
                     r=[ps], w=[rs])
                b.op("act", lambda e, rs=rs: e.activation(out=rs.ap[:], in_=rs.ap[:], func=AF.Exp, scale=-0.5), r=[rs], w=[rs])
                for j in range(8):
                    tm = tmb.next()
                    b.op("dve", lambda e, j=j, tm=tm, rs=rs: e.tensor_tensor(out=tm.ap[:], in0=xT[:, j, blk(bi)], in1=rs.ap[:], op=ALU.mult),
                         r=[d_x[bi][j], rs], w=[tm])
                    outfn(bi, j, tm)

        def to_hT(l, which):
            def f(bi, j, tm):
                g = gs[:, l, which * 8 + j:which * 8 + j + 1]
                sh = modt[:, l, which * 24 + j:which * 24 + j + 1]
                b.op("act", lambda e: e.activation(out=hT[:, j, blk(bi)], in_=tm.ap[:], func=AF.Identity, scale=g, bias=sh),
                     r=[tm], w=[d_hT[bi]])
            return f

        def load_w(src, cols):
            wt = wtile.next()
            n = cols.stop - cols.start
            b.dma("pool", wt.ap[:, :, 0:n], src.rearrange("(j p) c -> p j c", p=128)[:, :, cols], w=[wt])
            return wt

        def proj(l, col0, ncols, evac):
            wt = load_w(w_in_d[l], slice(col0, col0 + ncols))
            for bi in range(4):
                ps = psA.next()
                for j in range(8):
                    b.op("pe", lambda e, j=j, ps=ps: e.matmul(ps.ap[0:ncols, :], lhsT=wt.ap[:, j, 0:ncols], rhs=hT[:, j, blk(bi)],
                                                              start=(j == 0), stop=(j == 7)), r=[wt, d_hT[bi]], w=[ps])
                evac(bi, ps)

        def act_evac(dst, func=AF.Copy, **kw):
            def f(bi, ps):
                b.op("act", lambda e: e.activation(out=dst.ap[:, blk(bi)], in_=ps.ap[:, :], func=func, **kw), r=[ps], w=[dst])
            return f

        def headsum(src, consume, ph_rot):
            for bi in range(4):
                ps = psA.next()
                b.op("pe", lambda e, ps=ps: e.matmul(ps.ap[:, :], lhsT=bonesb[:], rhs=src.ap[:, blk(bi)], start=True, stop=True),
                     r=[src], w=[ps])
                consume(bi, ps)

        def rv(ap, d):
            return ap[:, ::-1] if d else ap[:, :]

        for l in range(2):
            pvl = lambda c: pv[:, l, c:c + 1]
            if l == 0:
                xs = contextlib.ExitStack()
                xTt = sb("xT", [128, 8, T], F32, xs)
            if l == 0:
                with contextlib.ExitStack() as ph:
                    xin = rot("xin", 3, [128, D], F32, ph)
                    for tt in range(16):
                        xt = xin.next()
                        b.dma("sp", xt.ap[:], x_d[tt * 128:(tt + 1) * 128, :], w=[xt])
                        for half in range(2):
                            ps = psA.next()
                            for q in range(4):
                                j = half * 4 + q
                                b.op("pe", lambda e, j=j, q=q, ps=ps, xt=xt: e.transpose(ps.ap[:, q * 128:(q + 1) * 128], xt.ap[:, j * 128:(j + 1) * 128], ident),
                                     r=[xt], w=[ps])
                            b.op("act" if half else "dve",
                                 (lambda e, ps=ps, half=half, tt=tt: e.activation(out=xTt[:, half * 4:half * 4 + 4, tt * 128:(tt + 1) * 128],
                                                                                 in_=ps.ap[:, :].rearrange("p (q t) -> p q t", q=4), func=AF.Copy)) if half else
                                 (lambda e, ps=ps, half=half, tt=tt: e.tensor_copy(out=xTt[:, half * 4:half * 4 + 4, tt * 128:(tt + 1) * 128],
                                                                                  in_=ps.ap[:, :].rearrange("p (q t) -> p q t", q=4))),
                                 r=[ps], w=[d_x[tt // 4][half * 4:half * 4 + 4]])
            with contextlib.ExitStack() as ph:
                norm_phase(xTt, ph, None, None, to_hT(l, 0))
            if l == 0:
                for bi in range(4):
                    b.dma("sp", xscr[:, :, blk(bi)], xTt[:, :, blk(bi)], r=[d_x[bi]])
            b.barrier()
            if l == 0:
                ck(3, hT[:, 0, :])
            xs.close()

            with contextlib.ExitStack() as mp:
                lora_w = tl("lora_w", [128, T], BF16, mp)
                lora_a = tl("lora_a", [128, T], BF16, mp)
                lora_g = tl("lora_g", [128, T], BF16, mp)
                wup = tl("wup", [128, 512], BF16, mp)
                aup = tl("aup", [128, 512], BF16, mp)
                gup = tl("gup", [128, 512], BF16, mp)
                b.dma("pool", wup.ap[:], w_up_d[l], w=[wup])
                b.dma("pool", aup.ap[:], a_up_d[l], w=[aup])
                b.dma("pool", gup.ap[:], g_up_d[l], w=[gup])
                proj(l, 1536, 128, act_evac(lora_w, AF.Tanh))
                proj(l, 1664, 128, act_evac(lora_a))
                proj(l, 1792, 128, act_evac(lora_g, AF.Sigmoid))
                if l == 0:
                    ck(4, lora_w.ap[:, :])

                with contextlib.ExitStack() as rp:
                    rst64 = sb("rst64", [128, T], BF16, rp)
                    b.op("dve", lambda e: e.memset(rst64[:], 1.0), w=[d0])
                    b.op("dve", lambda e: e.memset(rst64[:].rearrange("p (c i) -> p c i", i=64)[:, :, 0:1], 0.0), w=[d0])
                    r_bf = tl("r_bf", [128, T], BF16, rp)
                    k32 = tl("k32", [128, T], BF16, rp)
                    v_bf = tl("v_bf", [128, T], BF16, rp)
                    t1 = tl("t1", [128, T], F32, rp)
                    kk_bf = tl("kk_bf", [128, T], BF16, rp)
                    sg32 = tl("sg32", [128, T], F32, rp)
                    bs32 = tl("bs32", [128, T], F32, rp)
                    a_bf = tl("a_bf", [128, T], BF16, rp)
                    b_bf = tl("b_bf", [128, T], BF16, rp)
                    kd_bf = tl("kd_bf", [128, T], BF16, rp)
                    s4 = b_bf
                    bon = kk_bf
                    s4p = kd_bf
                    E = a_bf
                    mst = a_bf
                    vdir = [v_bf, tl("vrev", [128, T], BF16, rp)]
                    KR = [tl("KR%d" % d, [128, 32, 2, 64], BF16, rp) for d in range(2)]
                    KH = [tl("KH%d" % d, [128, T], BF16, rp) for d in range(2)]
                    BH = [tl("BH%d" % d, [128, T], BF16, rp) for d in range(2)]
                    KB = [tl("KB%d" % d, [128, T], BF16, rp) for d in range(2)]
                    BB = [tl("BB%d" % d, [128, T], BF16, rp) for d in range(2)]
                    WL = [tl("WL%d" % d, [128, 32], F32, rp) for d in range(2)]
                    o32 = sb("o32", [128, T], F32, rp)
                    P32 = [tl("P32%d" % d, [128, 64], F32, rp) for d in range(2)]
                    Pbf = [tl("Pbf%d" % d, [128, 64], BF16, rp) for d in range(2)]
                    pin = rot("pin", 2, [128, 64], F32, rp)
                    pout = rot("pout", 2, [128, 64], F32, rp)
                    t1b = t1.ap[:].bitcast(BF16)
                    sgb = sg32.ap[:].bitcast(BF16)
                    bsb = bs32.ap[:].bitcast(BF16)
                    bbb = b_bf.ap[:]
                    tok = [[tl("tok%d%d" % (d, p), [64, 3, 2, 2, 64], BF16, rp) for p in range(2)] for d in range(2)]
                    Pbd = [tl("Pbd%d" % d, [128, 128], BF16, rp) for d in range(2)]
                    for t_ in tok[0] + tok[1] + Pbd:
                        b.op("dve", lambda e, t_=t_: e.memset(t_.ap[:], 0.0), w=[t_])
                    for d_ in range(2):
                        tok[d_].append(Tl(sgb[0:64, 2304 + d_ * 768:3072 + d_ * 768].rearrange("p (q a b c) -> p q a b c", q=3, a=2, b=2)))
                        tok[d_].append(Tl(bsb[0:64, 2304 + d_ * 768:3072 + d_ * 768].rearrange("p (q a b c) -> p q a b c", q=3, a=2, b=2)))
                    KRm = {}
                    for d_ in range(2):
                        for h_ in range(2):
                            for p_ in range(2):
                                KRm[(d_, h_, p_)] = tl("KRm%d%d%d" % (d_, h_, p_), [128, 128], BF16, rp)
                                b.op("dve", lambda e, t_=KRm[(d_, h_, p_)]: e.memset(t_.ap[:], 0.0), w=[KRm[(d_, h_, p_)]])
                            for p_ in range(2, 4):
                                ix = (d_ * 2 + h_) * 2 + (p_ - 2)
                                KRm[(d_, h_, p_)] = Tl(bbb[:, ix * 128:(ix + 1) * 128])
                    UA = {}
                    UT = {}
                    UTt = sb("UTt", [64, 4 * 960], BF16, rp)
                    b.op("dve", lambda e: e.memset(UTt[:], 0.0), w=[d0])
                    UTaps = [UTt[:], t1b[0:64, 0:3840]]
                    UTv = [x.rearrange("p (u r) -> p u r", u=4) for x in UTaps]
                    TRD = {}
                    for ts_ in range(2):
                        for q in range(1, 6):
                            for pa in range(2):
                                TRD[(ts_, pa, q)] = (Dep(), Dep())
                        for u in range(4):
                            tr = [None]
                            for q in range(1, 6):
                                o_ = u * 960 + (q - 1) * 192
                                dA, dT = TRD[(ts_, u // 2, q)]
                                X_ = UTaps[ts_]
                                ent = dict(lo=Tl(X_[:, o_ + 128:o_ + 192]), up=Tl(X_[:, o_:o_ + 64]), tt=Tl(X_[:, o_ + 64:o_ + 128]), ut=Tl(X_[:, o_:o_ + 128]))
                                ent["lo"].d = dA; ent["up"].d = dA; ent["tt"].d = dT; ent["ut"].d = dA
                                ent["dA"], ent["dT"] = dA, dT
                                tr.append(ent)
                            UT[(ts_, u)] = tr
                    UAt = []
                    UAD = []
                    for p in range(4):
                        if p < 2:
                            t_ = sb("UAt%d" % p, [64, 4 * 576], BF16, rp)[:]
                            b.op("dve", lambda e, t_=t_: e.memset(t_, 0.0), w=[d0])
                        else:
                            t_ = (sgb if p == 2 else bsb)[0:64, 0:2304]
                        UAt.append(t_)
                        dd = dict(ttf=Dep(), xs=[Dep(), Dep()], y=[Dep(), Dep()])
                        UAD.append(dd)
                        for u in range(4):
                            o_ = u * 576
                            hh_u = u % 2
                            yU = Tl(t_[:, o_ + 448 + hh_u * 64:o_ + 512 + hh_u * 64]); yU.d = dd["y"][u // 2]
                            upad = Tl(t_[:, o_ + 448:o_ + 576]); upad.d = dd["y"][u // 2]
                            sc = Tl(t_[:, o_:o_ + 320])
                            ud = dict(sc=sc, y=[None] * 6 + [yU], upad=upad, ttf=Tl(t_[:, o_ + 320:o_ + 384]), xs=Tl(t_[:, o_ + 384:o_ + 448]))
                            ud["ttf"].d = dd["ttf"]; ud["xs"].d = dd["xs"][u // 2]
                            for nm, lo_, hi_ in (("sc1", 0, 128), ("sc2", 128, 256), ("sc3", 256, 320), ("up0", 0, 64)):
                                ud[nm] = Tl(t_[:, o_ + lo_:o_ + hi_])
                                ud[nm].d = sc.d
                            UA[(u, p)] = ud
                    d_o = [Dep() for _ in range(32)]
                    for j in range(4):
                        c0 = PV_RW + j * 9
                        proj(l, j * 128, 128, act_evac(r_bf))
                        proj(l, 512 + j * 128, 128, act_evac(k32))
                        proj(l, 1024 + j * 128, 128, act_evac(v_bf))
                        b.op("dve", lambda e: e.memset(o32[:], 0.0), w=d_o)
                        b.op("dve", lambda e: e.tensor_scalar(out=t1.ap[:], in0=k32.ap[:], scalar1=pvl(c0 + 4), scalar2=None, op0=ALU.mult),
                             r=[k32], w=[t1])
                        b.op("act", lambda e: e.activation(out=s4.ap[:], in_=t1.ap[:], func=AF.Square), r=[t1], w=[s4])

                        def kk_cons(bi, ps):
                            b.op("act", lambda e: e.activation(out=sg32.ap[:, blk(bi)], in_=ps.ap[:, :], func=AF.Ln, bias=1e-12), r=[ps], w=[sg32])
                            b.op("act", lambda e: e.activation(out=sg32.ap[:, blk(bi)], in_=sg32.ap[:, blk(bi)], func=AF.Exp, scale=-0.5), r=[sg32], w=[sg32])
                            b.op("dve", lambda e: e.tensor_tensor(out=kk_bf.ap[:, blk(bi)], in0=t1.ap[:, blk(bi)], in1=sg32.ap[:, blk(bi)], op=ALU.mult),
                                 r=[t1, sg32], w=[kk_bf])
                        headsum(s4, kk_cons, None)
                        b.op("act", lambda e: e.activation(out=vdir[1].ap[:], in_=v_bf.ap[:, ::-1], func=AF.Copy), r=[v_bf], w=[vdir[1]])
                        for d in range(2):
                            pr_ = slice(d * 64, d * 64 + 64)
                            for bi in range(4):
                                ps = psA.next()
                                b.op("pe", lambda e, ps=ps: e.matmul(ps.ap[:, :], lhsT=wup.ap[pr_, j * 128:(j + 1) * 128], rhs=lora_w.ap[pr_, blk(bi)],
                                                                    start=True, stop=True), r=[wup, lora_w], w=[ps])
                                b.op("act", lambda e, ps=ps: e.activation(out=sg32.ap[:, blk(bi)], in_=ps.ap[:, :], func=AF.Sigmoid, bias=pvl(c0 + d)),
                                     r=[ps], w=[sg32])
                                ps2 = psA.next()
                                b.op("pe", lambda e, ps2=ps2: e.matmul(ps2.ap[:, :], lhsT=aup.ap[pr_, j * 128:(j + 1) * 128], rhs=lora_a.ap[pr_, blk(bi)],
                                                                      start=True, stop=True), r=[aup, lora_a], w=[ps2])
                                b.op("act", lambda e, ps2=ps2: e.activation(out=a_bf.ap[:, blk(bi)], in_=ps2.ap[:, :], func=AF.Sigmoid, bias=pvl(c0 + 2 + d)),
                                     r=[ps2], w=[a_bf])
                            b.op("dve", lambda e: e.tensor_tensor_scan(out=bs32.ap[:], data0=rst64[:], data1=rv(sg32.ap, d), initial=0.0,
                                                                       op0=ALU.mult, op1=ALU.add), r=[sg32], w=[bs32])
                            b.op("dve", lambda e: e.tensor_tensor(out=b_bf.ap[:], in0=kk_bf.ap[:], in1=a_bf.ap[:], op=ALU.mult), r=[kk_bf, a_bf], w=[b_bf])
                            b.op("dve", lambda e: e.tensor_scalar(out=a_bf.ap[:], in0=a_bf.ap[:], scalar1=pvl(c0 + 5), scalar2=der[:, l, j:j + 1],
                                                                  op0=ALU.mult, op1=ALU.add), r=[a_bf, b_bf], w=[a_bf])
                            b.op("dve", lambda e: e.tensor_tensor(out=kd_bf.ap[:], in0=k32.ap[:], in1=a_bf.ap[:], op=ALU.mult), r=[k32, a_bf], w=[kd_bf])
                            krv = KR[d].ap[:].rearrange("p c two i -> p two c i")
                            b.op("act", lambda e: e.activation(out=E.ap[:], in_=bs32.ap[:], func=AF.Exp, scale=CW), r=[bs32], w=[E])
                            b.op("dve", lambda e: e.tensor_tensor(out=krv[:, 1], in0=rv(r_bf.ap, d).rearrange("p (c i) -> p c i", i=64),
                                                                  in1=E.ap[:].rearrange("p (c i) -> p c i", i=64), op=ALU.mult), r=[r_bf, E], w=[KR[d]])
                            b.op("act", lambda e: e.activation(out=WL[d].ap[:], in_=bs32.ap[:].rearrange("p (c i) -> p c i", i=64)[:, :, 63],
                                                               func=AF.Exp, scale=CW), r=[bs32], w=[WL[d]])
                            b.op("act", lambda e: e.activation(out=E.ap[:], in_=bs32.ap[:], func=AF.Exp, scale=-CW), r=[bs32], w=[E])
                            b.op("dve", lambda e: e.tensor_tensor(out=KH[d].ap[:], in0=rv(kd_bf.ap, d), in1=E.ap[:], op=ALU.mult), r=[kd_bf, E], w=[KH[d]])
                            b.op("dve", lambda e: e.tensor_tensor(out=BH[d].ap[:], in0=rv(b_bf.ap, d), in1=E.ap[:], op=ALU.mult), r=[b_bf, E], w=[BH[d]])
                            b.op("dve", lambda e: e.tensor_tensor(out=t1.ap[:], in0=bs32.ap[:], in1=rv(sg32.ap, d), op=ALU.subtract), r=[bs32, sg32], w=[t1])
                            b.op("act", lambda e: e.activation(out=E.ap[:], in_=t1.ap[:], func=AF.Exp, scale=CW), r=[t1], w=[E])
                            b.op("dve", lambda e: e.tensor_tensor(out=krv[:, 0], in0=rv(kk_bf.ap, d).rearrange("p (c i) -> p c i", i=64),
                                                                  in1=E.ap[:].rearrange("p (c i) -> p c i", i=64), op=ALU.mult), r=[kk_bf, E], w=[KR[d]])
                            wlb = WL[d].ap[:].rearrange("p (c o) -> p c o", o=1).to_broadcast([128, 32, 64])
                            c3 = lambda t_: t_.ap[:].rearrange("p (c i) -> p c i", i=64)
                            b.op("dve", lambda e: e.tensor_tensor(out=c3(KB[d]), in0=c3(KH[d]), in1=wlb, op=ALU.mult), r=[KH[d], WL[d]], w=[KB[d]])
                            b.op("dve", lambda e: e.scalar_tensor_tensor(out=c3(BB[d]), in0=c3(BH[d]), scalar=-1.0, in1=wlb,
                                                                         op0=ALU.mult, op1=ALU.mult), r=[BH[d], WL[d]], w=[BB[d]])
                            b.op("dve", lambda e: e.memset(P32[d].ap[:], 0.0), w=[P32[d]])

                        b.op("dve", lambda e: e.scalar_tensor_tensor(out=s4p.ap[:], in0=k32.ap[:], scalar=pvl(c0 + 6), in1=r_bf.ap[:],
                                                                     op0=ALU.mult, op1=ALU.mult), r=[k32, r_bf, kk_bf], w=[s4p])

                        def bon_cons(bi, ps):
                            b.op("dve", lambda e: e.tensor_tensor(out=bon.ap[:, blk(bi)], in0=ps.ap[:, :], in1=v_bf.ap[:, blk(bi)], op=ALU.mult),
                                 r=[ps, v_bf], w=[bon])
                        headsum(s4p, bon_cons, None)
                        if l == 0 and j == 0:
                            ck(5, KR[1].ap[:].rearrange("p c two i -> p (c two i)"))
                            ck(50, BB[0].ap[:, :])
                            ck(51, kk_bf.ap[:, :])
                            ck(52, KH[1].ap[:, :])

                        def stageA(i):
                            par = i % 4
                            ts_ = i % 2
                            cs = slice(i * 64, (i + 1) * 64)
                            levs = []

                            def L0():
                                SK = ""
                                for d in range(2):
                                    if "T" in SK:
                                        break
                                    pt = psA.next()
                                    for q, src in enumerate((vdir[d], KB[d], BB[d])):
                                        b.op("pe", lambda e, q=q, src=src, pt=pt: e.matmul(pt.ap[0:64, q * 128:(q + 1) * 128], lhsT=src.ap[:, cs], rhs=identb[:], start=True, stop=True),
                                             r=[src], w=[pt])
                                    tk = tok[d][par]
                                    for h_ in range(2):
                                        b.op("act", lambda e, pt=pt, tk=tk, h_=h_: e.activation(out=tk.ap[:, :, h_, h_, :], in_=pt.ap[0:64, 0:384].rearrange("s (q h c) -> s q h c", q=3, h=2)[:, :, h_, :],
                                                                                        func=AF.Copy), r=[pt], w=[tk])
                                for u in range(4):
                                    if "M" in SK:
                                        break
                                    if "H" in SK and u % 2 == 1:
                                        continue
                                    d, hh = u // 2, u % 2
                                    pr = slice(hh * 64, hh * 64 + 64)
                                    ua = UA[(u, par)]
                                    kr = KR[d].ap[pr, i].rearrange("p two i -> p (two i)")
                                    krm = KRm[(d, hh, par)]
                                    b.op("pool", lambda e, kr=kr, krm=krm, pr=pr: e.tensor_copy(out=krm.ap[pr, :], in_=kr), r=[KR[d]], w=[krm])
                                    p1 = psB.next()
                                    b.op("pe", lambda e, p1=p1, krm=krm, d=d: e.matmul(p1.ap[0:64, 0:128], lhsT=BH[d].ap[:, cs], rhs=krm.ap[:, :], start=True, stop=True),
                                         r=[BH[d], krm], w=[p1])
                                    b.op("pe", lambda e, p1=p1, krm=krm, d=d: e.matmul(p1.ap[0:64, 128:256], lhsT=KH[d].ap[:, cs], rhs=krm.ap[:, :], start=True, stop=True),
                                         r=[KH[d], krm], w=[p1])
                                    b.op("pe", lambda e, p1=p1, krm=krm, d=d: e.matmul(p1.ap[0:64, 256:320], lhsT=krm.ap[:, 0:64], rhs=BH[d].ap[:, cs], start=True, stop=True),
                                         r=[BH[d], krm], w=[p1])
                                    b.op("dve", lambda e, p1=p1, ua=ua: e.tensor_tensor(out=ua["sc"].ap, in0=p1.ap[0:64, 0:320], in1=cst[0:64, 256:576], op=ALU.mult), r=[p1], w=[ua["sc"]])
                            levs.append(L0)
                            for k in range(1, 6):
                                def Lk(k=k):
                                    for pa in range(2):
                                        pu = psB.next()
                                        dA, dT = TRD[(ts_, pa, k)]
                                        o_ = (k - 1) * 192
                                        for u2 in range(2):
                                            u = pa * 2 + u2
                                            ua = UA[(u, par)]
                                            cb = u2 * 256
                                            if k == 1:
                                                lo_p, up_p, rdeps = ua["sc3"], ua["up0"], [ua["sc"]]
                                                b.op("pe", lambda e: e.matmul(pu.ap[0:64, cb:cb + 64], lhsT=lo_p.ap, rhs=up_p.ap, start=True, stop=True), r=rdeps, w=[pu])
                                            else:
                                                pv_ = UT[(ts_, u)][k - 1]
                                                lo_p, up_p, rdeps = pv_["lo"], pv_["up"], [pv_["dA"], pv_["dT"]]
                                                b.op("pe", lambda e: e.matmul(pu.ap[0:64, cb:cb + 128], lhsT=lo_p.ap, rhs=pv_["ut"].ap, start=True, stop=True), r=rdeps, w=[pu])
                                            b.op("pe", lambda e: e.matmul(pu.ap[0:64, cb + 128:cb + 192], lhsT=up_p.ap, rhs=lo_p.ap, start=True, stop=True), r=rdeps, w=[pu])
                                        pv4 = pu.ap[0:64, :].rearrange("p (u a b) -> p u a b", u=2, a=4)
                                        dst4 = UTv[ts_][:, pa * 2:pa * 2 + 2, o_:o_ + 192].rearrange("p u (a b) -> p u a b", a=3)
                                        b.op("act", lambda e: e.activation(out=dst4[:, :, 0::2, :], in_=pv4[:, :, 0:3:2, :], func=AF.Copy), r=[pu], w=[dA])
                                        if k == 1:
                                            for u2 in range(2):
                                                ua = UA[(pa * 2 + u2, par)]
                                                b.op("dve", lambda e: e.tensor_tensor(out=UT[(ts_, pa * 2 + u2)][1]["tt"].ap, in0=ua["up0"].ap, in1=identb[0:64, 0:64], op=ALU.add), r=[ua["sc"]], w=[dT])
                                        else:
                                            dAp, dTp = TRD[(ts_, pa, k - 1)]
                                            prev_tt = UTv[ts_][:, pa * 2:pa * 2 + 2, o_ - 192 + 64:o_ - 192 + 128]
                                            b.op("dve", lambda e: e.tensor_tensor(out=dst4[:, :, 1, :], in0=pv4[:, :, 1, :], in1=prev_tt, op=ALU.add), r=[pu, dTp, dA], w=[dT])
                                levs.append(Lk)

                            def L6():
                                pz = psB.next()
                                for u in range(4):
                                    pv_ = UT[(ts_, u)][5]
                                    b.op("pe", lambda e: e.matmul(pz.ap[0:64, u * 64:(u + 1) * 64], lhsT=pv_["lo"].ap, rhs=pv_["tt"].ap, start=True, stop=True), r=[pv_["dA"], pv_["dT"]], w=[pz])
                                tt5 = UTv[ts_][:, :, 4 * 192 + 64:4 * 192 + 128]
                                dst = UAt[par].rearrange("p (u r) -> p u r", u=4)[:, :, 320:384]
                                b.op("dve", lambda e: e.tensor_tensor(out=dst, in0=pz.ap[0:64, 0:256].rearrange("p (u c) -> p u c", u=4), in1=tt5, op=ALU.add),
                                     r=[pz, TRD[(ts_, 0, 5)][1], TRD[(ts_, 1, 5)][1]], w=[UAD[par]["ttf"]])
                            levs.append(L6)
                            return levs

                        def stageB(i):
                            par = i % 4
                            ts_ = i % 2
                            cs = slice(i * 64, (i + 1) * 64)
                            seg = i // 4
                            levs = []

                            def L0():
                                if i % 4 == 0:
                                    for d in range(2):
                                        pi = pin.next()
                                        b.dma("sp", pi.ap[:], irw_d[l, d, seg, j], w=[pi])
                                        b.op("dve", lambda e, pi=pi, d=d: e.scalar_tensor_tensor(out=P32[d].ap[:], in0=P32[d].ap[:], scalar=carry[:, 0:1], in1=pi.ap[:],
                                                                                             op0=ALU.mult, op1=ALU.add), r=[P32[d], pi], w=[P32[d]])
                                        b.op("dve", lambda e, d=d: e.tensor_tensor(out=Pbd[d].ap[:].rearrange("p (h v) -> p h v", h=2), in0=P32[d].ap[:].rearrange("p (o v) -> p o v", o=1).to_broadcast([128, 2, 64]),
                                                                           in1=cst[:, 128:256].rearrange("p (h v) -> p h v", h=2), op=ALU.mult), r=[P32[d]], w=[Pbd[d]])
                                for d in range(2):
                                    px = psB.next()
                                    for hh in range(2):
                                        u = d * 2 + hh
                                        ua = UA[(u, par)]
                                        tk = tok[d][par]
                                        krm = KRm[(d, hh, par)]
                                        b.op("pe", lambda e, krm=krm, d=d, hh=hh, px=px: e.matmul(px.ap[0:64, hh * 64:(hh + 1) * 64], lhsT=krm.ap[:, 0:64], rhs=Pbd[d].ap[:, hh * 64:(hh + 1) * 64], start=True, stop=False),
                                             r=[krm, Pbd[d]], w=[px])
                                        b.op("pe", lambda e, ua=ua, tk=tk, hh=hh, px=px: e.matmul(px.ap[0:64, hh * 64:(hh + 1) * 64], lhsT=ua["sc2"].ap[:, 0:64], rhs=tk.ap[:, 0, hh, hh, :],
                                                                                             start=False, stop=True), r=[ua["sc2"], tk], w=[px])
                                    dst = UAt[par].rearrange("p (u r) -> p u r", u=4)[:, d * 2:d * 2 + 2, 384:448]
                                    b.op("act", lambda e, px=px, dst=dst: e.activation(out=dst, in_=px.ap[0:64, 0:128].rearrange("p (u c) -> p u c", u=2), func=AF.Copy), r=[px], w=[UAD[par]["xs"][d]])
                            levs.append(L0)

                            def L1():
                                for d in range(2):
                                    py = psB.next()
                                    for hh in range(2):
                                        ua = UA[(d * 2 + hh, par)]
                                        b.op("pe", lambda e, ua=ua, hh=hh, py=py: e.matmul(py.ap[0:64, hh * 64:(hh + 1) * 64], lhsT=ua["ttf"].ap, rhs=ua["xs"].ap, start=True, stop=True),
                                             r=[ua["ttf"], ua["xs"]], w=[py])
                                    dst = UAt[par].rearrange("p (d r) -> p d r", d=2)[:, d, 448:1152].rearrange("p (a c) -> p a c", c=64)[:, 0::10, :]
                                    b.op("dve", lambda e, py=py, dst=dst: e.tensor_copy(out=dst, in_=py.ap[0:64, 0:128].rearrange("p (h c) -> p h c", h=2)), r=[py], w=[UAD[par]["y"][d]])
                            levs.append(L1)

                            def L7():
                                pps, pos = [], []
                                for d in range(2):
                                    tk = tok[d][par]
                                    pp = psB.next()
                                    pps.append(pp)
                                    for hh in range(2):
                                        ua = UA[(d * 2 + hh, par)]
                                        U = ua["y"][6]
                                        b.op("pe", lambda e, pp=pp, tk=tk, hh=hh: e.matmul(pp.ap[:, 0:64], lhsT=tk.ap[:, 1, hh].rearrange("s h c -> s (h c)"), rhs=tk.ap[:, 0, hh, hh, :],
                                                                                      start=(hh == 0), stop=False), r=[tk], w=[pp])
                                        b.op("pe", lambda e, pp=pp, tk=tk, hh=hh, U=U: e.matmul(pp.ap[:, 0:64], lhsT=tk.ap[:, 2, hh].rearrange("s h c -> s (h c)"), rhs=U.ap, start=False, stop=(hh == 1)),
                                             r=[tk, U], w=[pp])
                                for d in range(2):
                                    tk = tok[d][par]
                                    po = psB.next()
                                    pos.append(po)
                                    b.op("pe", lambda e, po=po, d=d: e.matmul(po.ap[:, 0:64], lhsT=Pbd[d].ap[:, :], rhs=KR[d].ap[:, i, 1, :], start=True, stop=False),
                                         r=[Pbd[d], KR[d]], w=[po])
                                    for hh in range(2):
                                        ua = UA[(d * 2 + hh, par)]
                                        b.op("pe", lambda e, po=po, tk=tk, ua=ua, hh=hh: e.matmul(po.ap[:, 0:64], lhsT=tk.ap[:, 0, hh].rearrange("s h c -> s (h c)"), rhs=ua["sc2"].ap[:, 64:128],
                                                                                             start=False, stop=False), r=[tk, ua["sc2"]], w=[po])
                                        b.op("pe", lambda e, po=po, ua=ua, hh=hh: e.matmul(po.ap[:, 0:64], lhsT=ua["upad"].ap, rhs=ua["sc1"].ap[:, 64:128], start=False, stop=(hh == 1)),
                                             r=[ua["upad"], ua["sc1"]], w=[po])
                                for d in range(2):
                                    pp = pps[d]
                                    b.op("dve", lambda e, pp=pp, d=d: e.scalar_tensor_tensor(out=P32[d].ap[:], in0=P32[d].ap[:], scalar=WL[d].ap[:, i:i + 1], in1=pp.ap[:, 0:64],
                                                                                         op0=ALU.mult, op1=ALU.add), r=[pp, P32[d], WL[d]], w=[P32[d]])
                                    b.op("dve", lambda e, d=d: e.tensor_tensor(out=Pbd[d].ap[:].rearrange("p (h v) -> p h v", h=2), in0=P32[d].ap[:].rearrange("p (o v) -> p o v", o=1).to_broadcast([128, 2, 64]),
                                                                           in1=cst[:, 128:256].rearrange("p (h v) -> p h v", h=2), op=ALU.mult), r=[P32[d]], w=[Pbd[d]])
                                for d in range(2):
                                    po = pos[d]
                                    nat = (31 - i) if d else i
                                    oc = o32[:, nat * 64:(nat + 1) * 64]
                                    ocv = oc[:, ::-1] if d else oc
                                    b.op("dve", lambda e, po=po, ocv=ocv: e.tensor_tensor(out=ocv, in0=po.ap[:, 0:64], in1=ocv, op=ALU.add), r=[po, d_o[nat]], w=[d_o[nat]])
                                    if i % 4 == 3:
                                        po_ = pout.next()
                                        b.op("act", lambda e, d=d, po_=po_: e.activation(out=po_.ap[:], in_=P32[d].ap[:], func=AF.Copy), r=[P32[d]], w=[po_])
                                        b.dma("sp", frw_d[l, d, seg, j], po_.ap[:], r=[po_])
                            levs.append(L7)
                            return levs

                        b.barrier()
                        b.op("dve", lambda e: e.memset(sgb[0:64, 0:3840], 0.0), w=[d0])
                        b.op("dve", lambda e: e.memset(bsb[0:64, 0:3840], 0.0), w=[d0])
                        b.op("dve", lambda e: e.memset(bbb[:, 0:1024], 0.0), w=[d0])
                        b.barrier()
                        A0, A1 = stageA(0), stageA(1)
                        for lev in range(7):
                            A0[lev]()
                            A1[lev]()
                        for i in range(0, 32, 2):
                            An1 = stageA(i + 2) if i + 2 < 32 else []
                            An2 = stageA(i + 3) if i + 3 < 32 else []
                            B1, B2 = stageB(i), stageB(i + 1)
                            for lev in range(7):
                                if lev < 3:
                                    B1[lev]()
                                elif lev < 6:
                                    B2[lev - 3]()
                                if lev < len(An1):
                                    An1[lev]()
                                if lev < len(An2):
                                    An2[lev]()
                        b.barrier()

                        if l == 0 and j == 0:
                            ck(6, o32[:, :])
                        d_all = d_o
                        b.op("act", lambda e: e.activation(out=s4p.ap[:], in_=o32[:], func=AF.Copy), r=d_all, w=[s4p])

                        def mu_cons(bi, ps):
                            b.op("dve", lambda e: e.scalar_tensor_tensor(out=o32[:, blk(bi)], in0=ps.ap[:, :], scalar=-1.0 / 64, in1=o32[:, blk(bi)],
                                                                         op0=ALU.mult, op1=ALU.add), r=[ps] + d_all, w=[d_all[0]])
                        headsum(s4p, mu_cons, None)
                        b.op("act", lambda e: e.activation(out=s4p.ap[:], in_=o32[:], func=AF.Square), r=[d_all[0]], w=[s4p])

                        def var_cons(bi, ps):
                            b.op("act", lambda e: e.activation(out=t1.ap[:, blk(bi)], in_=ps.ap[:, :], func=AF.Ln, scale=1.0 / 64, bias=64e-5), r=[ps], w=[t1])
                            b.op("act", lambda e: e.activation(out=t1.ap[:, blk(bi)], in_=t1.ap[:, blk(bi)], func=AF.Exp, scale=-0.5), r=[t1], w=[t1])
                            b.op("dve", lambda e: e.tensor_tensor(out=o32[:, blk(bi)], in0=o32[:, blk(bi)], in1=t1.ap[:, blk(bi)], op=ALU.mult), r=[t1, d_all[0]], w=[d_all[0]])
                            b.op("dve", lambda e: e.tensor_scalar(out=o32[:, blk(bi)], in0=o32[:, blk(bi)], scalar1=pvl(c0 + 7), scalar2=pvl(c0 + 8), op0=ALU.mult, op1=ALU.add),
                                 r=[d_all[0]], w=[d_all[0]])
                            b.op("dve", lambda e: e.tensor_tensor(out=o32[:, blk(bi)], in0=o32[:, blk(bi)], in1=bon.ap[:, blk(bi)], op=ALU.add), r=[bon, d_all[0]], w=[d_all[0]])
                            pg = psA.next()
                            b.op("pe", lambda e, pg=pg: e.matmul(pg.ap[:, :], lhsT=gup.ap[:, j * 128:(j + 1) * 128], rhs=lora_g.ap[:, blk(bi)], start=True, stop=True),
                                 r=[gup, lora_g], w=[pg])
                            b.op("dve", lambda e, pg=pg: e.tensor_tensor(out=mst.ap[:, blk(bi)], in0=pg.ap[:, :], in1=o32[:, blk(bi)], op=ALU.mult), r=[pg, d_all[0]], w=[mst])
                        headsum(s4p, var_cons, None)
                        b.dma("sp", mixscr[j], mst.ap[:], r=[mst], w=[d_mix[j]])
                        if l == 0 and j == 0:
                            ck(7, mst.ap[:, :])
                b.barrier()

                with contextlib.ExitStack() as lp:
                    mst = tl("mst", [128, T], BF16, lp)
                    xpad = tl("xpad", [128, 8, 259], F32, lp)
                    u32 = tl("u32", [128, T], F32, lp)
                    u_bf = tl("u_bf", [128, T], BF16, lp)
                    gb32 = tl("gb32", [128, T], F32, lp)
                    gt = tl("gt", [128, T], F32, lp)
                    gel = tl("gel", [128, T], F32, lp)
                    rg_ = [tl("rg%d" % d_, [128, T], F32, lp) for d_ in range(2)]
                    ig_ = [tl("ig%d" % d_, [128, T], F32, lp) for d_ in range(2)]
                    a32_ = [tl("a32%d" % d_, [128, T], F32, lp) for d_ in range(2)]
                    m32_ = [tl("m32%d" % d_, [128, T], F32, lp) for d_ in range(2)]
                    hh_ = [tl("hf", [128, T], F32, lp), tl("hb", [128, T], F32, lp)]
                    wab = [[tl("wab%d%d" % (d, q), [128, 128], BF16, lp) for q in range(2)] for d in range(2)]
                    h0 = rot("h0", 4, [128, 1], F32, lp)
                    for j in range(2):
                        c0 = PV_LRU + j * 11
                        for d in range(2):
                            for q, src in enumerate((lwa_d, lwx_d)):
                                w_ = wab[d][q]
                                b.op("dve", lambda e, w_=w_: e.memset(w_.ap[:], 0.0), w=[w_])
                                for hb in range(2):
                                    b.dma("pool", w_.ap[hb * 64:(hb + 1) * 64, hb * 64:(hb + 1) * 64], src[l, d, j * 2 + hb], w=[w_])
                        b.op("dve", lambda e: e.memset(xpad.ap[:], 0.0), w=[xpad])

                        def xb_evac(bi, ps):
                            b.op("act", lambda e: e.activation(out=xpad.ap[:, bi * 2:bi * 2 + 2, 2:258], in_=ps.ap[:, :].rearrange("p (s t) -> p s t", s=2), func=AF.Copy),
                                 r=[ps], w=[xpad])
                        proj(l, 1920 + j * 128, 128, xb_evac)
                        proj(l, 2176 + j * 128, 128, act_evac(gb32))
                        b.op("dve", lambda e: e.tensor_scalar(out=xpad.ap[:, 1:8, 0:2], in0=xpad.ap[:, 0:7, 256:258], scalar1=carry[:, 0:1], scalar2=None, op0=ALU.mult),
                             r=[xpad], w=[xpad])
                        b.op("dve", lambda e: e.tensor_scalar(out=xpad.ap[:, 0:7, 258:259], in0=xpad.ap[:, 1:8, 2:3], scalar1=carry[:, 0:1], scalar2=None, op0=ALU.mult),
                             r=[xpad], w=[xpad])
                        u3 = u32.ap[:].rearrange("p (s t) -> p s t", s=8)
                        b.op("dve", lambda e: e.tensor_scalar(out=u3, in0=xpad.ap[:, :, 0:256], scalar1=pvl(c0), scalar2=pvl(c0 + 4), op0=ALU.mult, op1=ALU.add),
                             r=[xpad], w=[u32])
                        for k in range(1, 4):
                            b.op("dve", lambda e, k=k: e.scalar_tensor_tensor(out=u3, in0=xpad.ap[:, :, k:k + 256], scalar=pvl(c0 + k), in1=u3, op0=ALU.mult, op1=ALU.add),
                                 r=[xpad, u32], w=[u32])
                        b.op("act", lambda e: e.activation(out=u_bf.ap[:], in_=u32.ap[:], func=AF.Copy), r=[u32], w=[u_bf])
                        b.op("act", lambda e: e.activation(out=gt.ap[:], in_=gb32.ap[:], func=AF.Square), r=[gb32], w=[gt])
                        b.op("dve", lambda e: e.tensor_scalar(out=gt.ap[:], in0=gt.ap[:], scalar1=0.044715, scalar2=1.0, op0=ALU.mult, op1=ALU.add), r=[gt], w=[gt])
                        b.op("dve", lambda e: e.tensor_tensor(out=gt.ap[:], in0=gt.ap[:], in1=gb32.ap[:], op=ALU.mult), r=[gt, gb32], w=[gt])
                        b.op("act", lambda e: e.activation(out=gt.ap[:], in_=gt.ap[:], func=AF.Sigmoid, scale=1.5957691216057308), r=[gt], w=[gt])
                        b.op("dve", lambda e: e.tensor_tensor(out=gel.ap[:], in0=gt.ap[:], in1=gb32.ap[:], op=ALU.mult), r=[gt, gb32], w=[gel])
                        for d in range(2):
                            rg, ig, a32, m32 = rg_[d], ig_[d], a32_[d], m32_[d]
                            for bi in range(4):
                                ps = psA.next()
                                b.op("pe", lambda e, ps=ps: e.matmul(ps.ap[:, :], lhsT=wab[d][0].ap[:], rhs=u_bf.ap[:, blk(bi)], start=True, stop=True), r=[wab[d][0], u_bf], w=[ps])
                                b.op("act", lambda e, ps=ps: e.activation(out=rg.ap[:, blk(bi)], in_=ps.ap[:, :], func=AF.Sigmoid, bias=pvl(c0 + 5 + d)), r=[ps], w=[rg])
                                ps2 = psA.next()
                                b.op("pe", lambda e, ps2=ps2: e.matmul(ps2.ap[:, :], lhsT=wab[d][1].ap[:], rhs=u_bf.ap[:, blk(bi)], start=True, stop=True), r=[wab[d][1], u_bf], w=[ps2])
                                b.op("act", lambda e, ps2=ps2: e.activation(out=ig.ap[:, blk(bi)], in_=ps2.ap[:, :], func=AF.Sigmoid, bias=pvl(c0 + 7 + d)), r=[ps2], w=[ig])
                            sc = der[:, l, 4 + j * 2 + d:5 + j * 2 + d]
                            sc2 = der[:, l, 8 + j * 2 + d:9 + j * 2 + d]
                            b.op("act", lambda e: e.activation(out=a32.ap[:], in_=rg.ap[:], func=AF.Exp, scale=sc), r=[rg], w=[a32])
                            b.op("act", lambda e: e.activation(out=m32.ap[:], in_=rg.ap[:], func=AF.Exp, scale=sc2), r=[rg], w=[m32])
                            b.op("act", lambda e: e.activation(out=m32.ap[:], in_=m32.ap[:], func=AF.Sqrt, scale=-1.0, bias=1.0), r=[m32], w=[m32])
                            b.op("dve", lambda e: e.tensor_tensor(out=m32.ap[:], in0=m32.ap[:], in1=ig.ap[:], op=ALU.mult), r=[m32, ig], w=[m32])
                            b.op("dve", lambda e: e.tensor_tensor(out=m32.ap[:], in0=m32.ap[:], in1=u32.ap[:], op=ALU.mult), r=[m32, u32], w=[m32])
                            hd = hh_[d]
                            prev = None
                            for s in range(8):
                                nat = (7 - s) if d else s
                                sl = slice(nat * 256, (nat + 1) * 256)
                                hi = h0.next()
                                icol = ilru[:, l, j, s * 2 + d:s * 2 + d + 1]
                                if prev is None:
                                    b.op("dve", lambda e, hi=hi, icol=icol: e.tensor_copy(out=hi.ap[:], in_=icol), w=[hi])
                                else:
                                    b.op("dve", lambda e, hi=hi, icol=icol, prev=prev: e.scalar_tensor_tensor(out=hi.ap[:], in0=prev, scalar=carry[:, 0:1], in1=icol,
                                                                                                     op0=ALU.mult, op1=ALU.add), r=[hd], w=[hi])
                                b.op("dve", lambda e, hi=hi, sl=sl: e.tensor_tensor_scan(out=rv(hd.ap[:, sl], d), data0=rv(a32.ap[:, sl], d), data1=rv(m32.ap[:, sl], d),
                                                                                    initial=hi.ap[:, 0:1], op0=ALU.mult, op1=ALU.add), r=[a32, m32, hi], w=[hd])
                                last = (nat * 256) if d else (nat * 256 + 255)
                                prev = hd.ap[:, last:last + 1]
                                b.op("act", lambda e, prev=prev, s=s: e.activation(out=flru[:, l, j, s * 2 + d:s * 2 + d + 1], in_=prev, func=AF.Copy), r=[hd], w=[d_flru])
                        b.op("dve", lambda e: e.tensor_tensor(out=hh_[0].ap[:], in0=hh_[0].ap[:], in1=hh_[1].ap[:], op=ALU.add), r=[hh_[0], hh_[1]], w=[hh_[0]])
                        b.op("dve", lambda e: e.tensor_tensor(out=mst.ap[:], in0=hh_[0].ap[:], in1=gel.ap[:], op=ALU.mult), r=[hh_[0], gel], w=[mst])
                        b.dma("sp", mixscr[4 + j], mst.ap[:], r=[mst], w=[d_mix[4 + j]])
                        if l == 0 and j == 0:
                            ck(9, mst.ap[:, :])
                b.barrier()

                with contextlib.ExitStack() as hp:
                    rst32 = sb("rst32", [128, T], BF16, hp)
                    b.op("dve", lambda e: e.memset(rst32[:], 1.0), w=[d0])
                    b.op("dve", lambda e: e.memset(rst32[:].rearrange("p (c i) -> p c i", i=64)[:, :, 0:1], 0.0), w=[d0])
                    mst = tl("mst", [128, T], BF16, hp)
                    qs = tl("qs", [128, T], BF16, hp)
                    fr = tl("fr", [128, T], F32, hp)
                    v_bf = tl("hv_bf", [128, T], BF16, hp)
                    ogs = tl("ogs", [128, T], BF16, hp)
                    f32_ = tl("f32_", [128, T], F32, hp)
                    lf = tl("lf", [128, T], F32, hp)
                    kq = tl("kq", [128, T], BF16, hp)
                    bs32 = tl("hbs32", [128, T], F32, hp)
                    t1 = tl("ht1", [128, T], F32, hp)
                    E = tl("hE", [128, T], BF16, hp)
                    vdir = [v_bf, tl("hvrev", [128, T], BF16, hp)]
                    QT = [tl("QT%d" % d, [128, T], BF16, hp) for d in range(2)]
                    KH = [tl("hKH%d" % d, [128, T], BF16, hp) for d in range(2)]
                    KB = [tl("hKB%d" % d, [128, T], BF16, hp) for d in range(2)]
                    WL = [tl("hWL%d" % d, [128, 32], F32, hp) for d in range(2)]
                    WLm = tl("hWLm", [128, 32], F32, hp)
                    QTa = [tl("QTa%d" % d, [128, T], BF16, hp) for d in range(2)]
                    o32 = sb("ho32", [128, T], F32, hp)
                    s4 = tl("hs4", [128, T], BF16, hp)
                    P32 = [tl("hP32%d" % d, [128, 64], F32, hp) for d in range(2)]
                    Pbf = [tl("hPbf%d" % d, [128, 64], BF16, hp) for d in range(2)]
                    pin = rot("hpin", 4, [128, 64], F32, hp)
                    pout = rot("hpout", 4, [128, 64], F32, hp)
                    tok = rot("htok", 6, [64, 2, 2, 2, 64], BF16, hp)
                    Pbd = [tl("hPbd%d" % d, [128, 128], BF16, hp) for d in range(2)]
                    for t_ in tok.tl + Pbd:
                        b.op("dve", lambda e, t_=t_: e.memset(t_.ap[:], 0.0), w=[t_])
                    scb_ = rot("hsc", 6, [64, 128], BF16, hp)
                    hclamp = rot("hclamp", 3, [64, 128], F32, hp)
                    qbd = rot("qbd", 6, [128, 128], BF16, hp)
                    for t_ in qbd.tl:
                        b.op("dve", lambda e, t_=t_: e.memset(t_.ap[:], 0.0), w=[t_])
                    d_o = [Dep() for _ in range(32)]
                    for j in range(2):
                        c0 = PV_HG + j * 5
                        proj(l, 2432 + j * 128, 128, act_evac(qs, AF.Silu))
                        proj(l, 3200 + j * 128, 128, act_evac(v_bf))
                        proj(l, 3456 + j * 128, 128, act_evac(ogs, AF.Silu))
                        b.op("dve", lambda e: e.memset(o32[:], 0.0), w=d_o)
                        b.op("act", lambda e: e.activation(out=vdir[1].ap[:], in_=v_bf.ap[:, ::-1], func=AF.Copy), r=[v_bf], w=[vdir[1]])
                        for d in range(2):
                            proj(l, 2688 + d * 256 + j * 128, 128, act_evac(fr, AF.Sigmoid))
                            lb = der[:, l, 12 + j * 2 + d:13 + j * 2 + d]
                            oml = der[:, l, 16 + j * 2 + d:17 + j * 2 + d]
                            b.op("dve", lambda e: e.tensor_scalar(out=f32_.ap[:], in0=fr.ap[:], scalar1=oml, scalar2=lb, op0=ALU.mult, op1=ALU.add), r=[fr], w=[f32_])
                            b.op("act", lambda e: e.activation(out=lf.ap[:], in_=f32_.ap[:], func=AF.Ln), r=[f32_], w=[lf])
                            b.op("dve", lambda e: e.tensor_scalar(out=kq.ap[:], in0=f32_.ap[:], scalar1=-1.0, scalar2=1.0, op0=ALU.mult, op1=ALU.add), r=[f32_], w=[kq])
                            b.op("dve", lambda e: e.tensor_tensor_scan(out=bs32.ap[:], data0=rst32[:], data1=rv(lf.ap, d), initial=0.0, op0=ALU.mult, op1=ALU.add),
                                 r=[lf], w=[bs32])
                            bs3 = bs32.ap[:].rearrange("p (c i) -> p c i", i=64)
                            c3 = lambda t_: t_.ap[:].rearrange("p (c i) -> p c i", i=64)
                            b.op("act", lambda e: e.activation(out=E.ap[:], in_=bs32.ap[:], func=AF.Exp), r=[bs32], w=[E])
                            b.op("dve", lambda e: e.tensor_tensor(out=QTa[d].ap[:], in0=rv(qs.ap, d), in1=E.ap[:], op=ALU.mult), r=[qs, E], w=[QTa[d]])
                            b.op("act", lambda e: e.activation(out=WL[d].ap[:], in_=bs3[:, :, 63], func=AF.Exp), r=[bs32], w=[WL[d]])
                            b.op("dve", lambda e: e.tensor_tensor(out=WLm.ap[:], in0=bs3[:, :, 63], in1=bs3[:, :, 31], op=ALU.subtract), r=[bs32], w=[WLm])
                            b.op("act", lambda e: e.activation(out=WLm.ap[:], in_=WLm.ap[:], func=AF.Exp), r=[WLm], w=[WLm])
                            b.op("dve", lambda e: e.tensor_tensor(out=c3(t1), in0=bs3, in1=bs3[:, :, 31:32].to_broadcast([128, 32, 64]), op=ALU.subtract), r=[bs32], w=[t1])
                            b.op("act", lambda e: e.activation(out=E.ap[:], in_=t1.ap[:], func=AF.Exp), r=[t1, QTa[d]], w=[E])
                            b.op("dve", lambda e: e.tensor_tensor(out=QT[d].ap[:], in0=rv(qs.ap, d), in1=E.ap[:], op=ALU.mult), r=[qs, E], w=[QT[d]])
                            b.op("act", lambda e: e.activation(out=E.ap[:], in_=t1.ap[:], func=AF.Exp, scale=-1.0), r=[t1, QT[d]], w=[E])
                            b.op("dve", lambda e: e.tensor_tensor(out=KH[d].ap[:], in0=rv(kq.ap, d), in1=E.ap[:], op=ALU.mult), r=[kq, E], w=[KH[d]])
                            wlb = WLm.ap[:].rearrange("p (c o) -> p c o", o=1).to_broadcast([128, 32, 64])
                            b.op("dve", lambda e: e.tensor_tensor(out=c3(KB[d]), in0=c3(KH[d]), in1=wlb, op=ALU.mult), r=[KH[d], WLm], w=[KB[d]])
                            b.op("dve", lambda e: e.memset(P32[d].ap[:], 0.0), w=[P32[d]])
                        hA = {}

                        def hgA(i, d):
                            cs = slice(i * 64, (i + 1) * 64)
                            qb = qbd.next()
                            for hh in range(2):
                                pr = slice(hh * 64, hh * 64 + 64)
                                b.op("pool", lambda e, hh=hh, pr=pr: e.tensor_copy(out=qb.ap[pr, hh * 64:(hh + 1) * 64], in_=QT[d].ap[pr, cs]), r=[QT[d]], w=[qb])
                            p1 = psB.next()
                            b.op("pe", lambda e: e.matmul(p1.ap[0:64, 0:128], lhsT=KH[d].ap[:, cs], rhs=qb.ap[:, :], start=True, stop=True), r=[KH[d], qb], w=[p1])
                            sc = scb_.next()
                            ctm = hclamp.next()
                            b.op("dve", lambda e: e.tensor_scalar(out=ctm.ap[:], in0=p1.ap[0:64, 0:128], scalar1=1e30, scalar2=-1e30, op0=ALU.min, op1=ALU.max), r=[p1], w=[ctm])
                            b.op("dve", lambda e: e.tensor_tensor(out=sc.ap[:].rearrange("p (h t) -> p h t", h=2), in0=ctm.ap[:].rearrange("p (h t) -> p h t", h=2),
                                                                  in1=cst[0:64, 448:512].rearrange("p (o t) -> p o t", o=1).to_broadcast([64, 2, 64]), op=ALU.mult), r=[ctm], w=[sc])
                            pt = psA.next()
                            for q, src in enumerate((vdir[d], KB[d])):
                                b.op("pe", lambda e, q=q, src=src: e.matmul(pt.ap[0:64, q * 128:(q + 1) * 128], lhsT=src.ap[:, cs], rhs=identb[:], start=True, stop=True), r=[src], w=[pt])
                            tk = tok.next()
                            for h_ in range(2):
                                b.op("act", lambda e, h_=h_: e.activation(out=tk.ap[:, :, h_, h_, :], in_=pt.ap[0:64, 0:256].rearrange("s (q h c) -> s q h c", q=2, h=2)[:, :, h_, :], func=AF.Copy),
                                     r=[pt], w=[tk])
                            hA[(i, d)] = (sc, tk)

                        def hgB(i, d):
                            cs = slice(i * 64, (i + 1) * 64)
                            seg = i // 4
                            sc, tk = hA.pop((i, d))
                            if i % 4 == 0:
                                pi = pin.next()
                                b.dma("sp", pi.ap[:], ihg_d[l, d, seg, j], w=[pi])
                                b.op("dve", lambda e: e.scalar_tensor_tensor(out=P32[d].ap[:], in0=P32[d].ap[:], scalar=carry[:, 0:1], in1=pi.ap[:],
                                                                             op0=ALU.mult, op1=ALU.add), r=[P32[d], pi], w=[P32[d]])
                                b.op("dve", lambda e: e.tensor_tensor(out=Pbd[d].ap[:].rearrange("p (h v) -> p h v", h=2), in0=P32[d].ap[:].rearrange("p (o v) -> p o v", o=1).to_broadcast([128, 2, 64]),
                                                                      in1=cst[:, 128:256].rearrange("p (h v) -> p h v", h=2), op=ALU.mult), r=[P32[d]], w=[Pbd[d]])
                            po = psB.next()
                            pp = Tl(po.ap[:, 64:128])
                            pp.d = po.d
                            pq = psB.next()
                            pp = Tl(pq.ap[:, 64:128])
                            pp.d = pq.d
                            for hh in range(2):
                                b.op("pe", lambda e, hh=hh: e.matmul(pp.ap[:, 0:64], lhsT=tk.ap[:, 1, hh].rearrange("s h c -> s (h c)"), rhs=tk.ap[:, 0, hh, hh, :], start=(hh == 0), stop=(hh == 1)),
                                     r=[tk], w=[pp])
                            b.op("pe", lambda e: e.matmul(po.ap[:, 0:64], lhsT=Pbd[d].ap[:, :], rhs=QTa[d].ap[:, cs], start=True, stop=False), r=[Pbd[d], QTa[d]], w=[po])
                            for hh in range(2):
                                b.op("pe", lambda e, hh=hh: e.matmul(po.ap[:, 0:64], lhsT=tk.ap[:, 0, hh].rearrange("s h c -> s (h c)"), rhs=sc.ap[:, hh * 64:(hh + 1) * 64], start=False, stop=(hh == 1)),
                                     r=[tk, sc], w=[po])
                            nat = (31 - i) if d else i
                            oc = o32[:, nat * 64:(nat + 1) * 64]
                            ocv = oc[:, ::-1] if d else oc
                            b.op("dve", lambda e: e.scalar_tensor_tensor(out=P32[d].ap[:], in0=P32[d].ap[:], scalar=WL[d].ap[:, i:i + 1], in1=pp.ap[:, 0:64],
                                                                         op0=ALU.mult, op1=ALU.add), r=[pp, P32[d], WL[d]], w=[P32[d]])
                            b.op("dve", lambda e: e.tensor_tensor(out=ocv, in0=po.ap[:, 0:64], in1=ocv, op=ALU.add), r=[po, d_o[nat]], w=[d_o[nat]])
                            b.op("dve", lambda e: e.tensor_tensor(out=Pbd[d].ap[:].rearrange("p (h v) -> p h v", h=2), in0=P32[d].ap[:].rearrange("p (o v) -> p o v", o=1).to_broadcast([128, 2, 64]),
                                                                      in1=cst[:, 128:256].rearrange("p (h v) -> p h v", h=2), op=ALU.mult), r=[P32[d]], w=[Pbd[d]])
                            if i % 4 == 3:
                                po_ = pout.next()
                                b.op("act", lambda e: e.activation(out=po_.ap[:], in_=P32[d].ap[:], func=AF.Copy), r=[P32[d]], w=[po_])
                                b.dma("sp", fhg_d[l, d, seg, j], po_.ap[:], r=[po_])

                        hgA(0, 0)
                        hgA(0, 1)
                        for i in range(32):
                            if i < 31:
                                hgA(i + 1, 0)
                                hgA(i + 1, 1)
                            hgB(i, 0)
                            hgB(i, 1)
                        b.op("act", lambda e: e.activation(out=s4.ap[:], in_=o32[:], func=AF.Square), r=d_o, w=[s4])

                        def hv_cons(bi, ps):
                            b.op("act", lambda e: e.activation(out=t1.ap[:, blk(bi)], in_=ps.ap[:, :], func=AF.Ln, scale=1.0 / 64, bias=1e-6), r=[ps], w=[t1])
                            b.op("act", lambda e: e.activation(out=t1.ap[:, blk(bi)], in_=t1.ap[:, blk(bi)], func=AF.Exp, scale=-0.5), r=[t1], w=[t1])
                            b.op("dve", lambda e: e.scalar_tensor_tensor(out=t1.ap[:, blk(bi)], in0=t1.ap[:, blk(bi)], scalar=pvl(c0 + 4), in1=o32[:, blk(bi)],
                                                                         op0=ALU.mult, op1=ALU.mult), r=[t1] + d_o, w=[t1])
                            b.op("dve", lambda e: e.tensor_tensor(out=mst.ap[:, blk(bi)], in0=t1.ap[:, blk(bi)], in1=ogs.ap[:, blk(bi)], op=ALU.mult), r=[t1, ogs], w=[mst])
                        headsum(s4, hv_cons, None)
                        b.dma("sp", mixscr[6 + j], mst.ap[:], r=[mst], w=[d_mix[6 + j]])
                        if l == 0 and j == 0:
                            ck(10, mst.ap[:, :])
            b.barrier()

            xs = contextlib.ExitStack()
            xTt = sb("xT", [128, 8, T], F32, xs)
            ms = contextlib.ExitStack()
            mixT = sb("mixT", [128, 8, T], BF16, ms)
            for ft in range(8):
                b.dma("sp", mixT[:, ft, :], mixscr[ft], r=[d_mix[ft]], w=[d_mix[ft]])
            for bi in range(4):
                b.dma("sp", xTt[:, :, blk(bi)], xscr[:, :, blk(bi)], w=[d_x[bi]])
            wts = [load_w(w_out_d[l], slice(dj * 128, (dj + 1) * 128)) for dj in range(3)]
            for dj in range(8):
                wt = wts[dj % 3]
                for bi in range(4):
                    ps = psA.next()
                    for ft in range(8):
                        b.op("pe", lambda e, ft=ft, ps=ps, wt=wt: e.matmul(ps.ap[:, :], lhsT=wt.ap[:, ft, :], rhs=mixT[:, ft, blk(bi)], start=(ft == 0), stop=(ft == 7)),
                             r=[wt] + d_mix, w=[ps])
                    b.op("dve", lambda e, ps=ps: e.scalar_tensor_tensor(out=xTt[:, dj, blk(bi)], in0=ps.ap[:, :], scalar=modt[:, l, 16 + dj:17 + dj], in1=xTt[:, dj, blk(bi)],
                                                                    op0=ALU.mult, op1=ALU.add), r=[ps, d_x[bi][dj]], w=[d_x[bi][dj]])
                if dj + 3 < 8:
                    wts[dj % 3] = load_w(w_out_d[l], slice((dj + 3) * 128, (dj + 4) * 128))
            b.barrier()
            if l == 0:
                ck(11, xTt[:, 0, :])
            ms.close()
            with contextlib.ExitStack() as ph:
                norm_phase(xTt, ph, None, None, to_hT(l, 1))
            b.barrier()
            with contextlib.ExitStack() as fp:
                wgt = rot("wgt", 2, [128, 8, 512], BF16, fp)
                wut = rot("wut", 2, [128, 8, 512], BF16, fp)
                wdt = rot("wdt", 3, [128, 4, D], BF16, fp)
                gpad = rot("gpad", 3, [128, 8, 66], F32, fp)
                acc = rot("acc", 3, [128, 512], F32, fp)
                sgt = rot("sgt", 3, [128, 512], F32, fp)
                actT = rot("actT", 3, [128, 4, 512], BF16, fp)
                for g_ in gpad.tl:
                    b.op("dve", lambda e, g_=g_: e.memset(g_.ap[:], 0.0), w=[g_])
                groups = [(0, 4), (4, 4), (8, 4), (12, 4), (16, 4), (20, 2)]
                pend = []

                def down_proj(wd_, at, ng, bi):
                    for dj in range(8):
                        ps = psA.next()
                        for ci in range(ng):
                            b.op("pe", lambda e, ci=ci, ps=ps: e.matmul(ps.ap[:, :], lhsT=wd_.ap[:, ci, dj * 128:(dj + 1) * 128], rhs=at.ap[:, ci, :], start=(ci == 0), stop=(ci == ng - 1)),
                                 r=[wd_, at], w=[ps])
                        b.op("dve", lambda e, ps=ps: e.scalar_tensor_tensor(out=xTt[:, dj, blk(bi)], in0=ps.ap[:, :], scalar=modt[:, l, 40 + dj:41 + dj], in1=xTt[:, dj, blk(bi)],
                                                                        op0=ALU.mult, op1=ALU.add), r=[ps, d_x[bi][dj]], w=[d_x[bi][dj]])

                for (c0g, ng) in groups:
                    wg_, wu_, wd_ = wgt.next(), wut.next(), wdt.next()
                    cs = slice(c0g * 128, (c0g + ng) * 128)
                    b.dma("pool", wg_.ap[:, :, 0:ng * 128], wg_d[l].rearrange("(j p) c -> p j c", p=128)[:, :, cs], w=[wg_])
                    b.dma("pool", wu_.ap[:, :, 0:ng * 128], wu_d[l].rearrange("(j p) c -> p j c", p=128)[:, :, cs], w=[wu_])
                    b.dma("pool", wd_.ap[:, 0:ng, :], wd_d[l, c0g * 128:(c0g + ng) * 128, :].rearrange("(c p) d -> p c d", p=128), w=[wd_])
                    for bi in range(4):
                        at = actT.next()
                        for ci in range(ng):
                            c = c0g + ci
                            pg, pu = psA.next(), psA.next()
                            for jj in range(8):
                                b.op("pe", lambda e, jj=jj, pg=pg: e.matmul(pg.ap[:, :], lhsT=wg_.ap[:, jj, ci * 128:(ci + 1) * 128], rhs=hT[:, jj, blk(bi)], start=(jj == 0), stop=(jj == 7)),
                                     r=[wg_, d_hT[bi]], w=[pg])
                            for jj in range(8):
                                b.op("pe", lambda e, jj=jj, pu=pu: e.matmul(pu.ap[:, :], lhsT=wu_.ap[:, jj, ci * 128:(ci + 1) * 128], rhs=hT[:, jj, blk(bi)], start=(jj == 0), stop=(jj == 7)),
                                     r=[wu_, d_hT[bi]], w=[pu])
                            gp = gpad.next()
                            b.op("act", lambda e, gp=gp, pg=pg: e.activation(out=gp.ap[:, :, 1:65], in_=pg.ap[:, :].rearrange("p (r t) -> p r t", r=8), func=AF.Copy), r=[pg], w=[gp])
                            b.op("dve", lambda e, gp=gp: e.tensor_tensor(out=gp.ap[:, 1:8, 0], in0=gp.ap[:, 0:7, 64], in1=fmask[:, :], op=ALU.mult), r=[gp], w=[gp])
                            b.op("dve", lambda e, gp=gp: e.tensor_tensor(out=gp.ap[:, 0:7, 65], in0=gp.ap[:, 1:8, 1], in1=fmask[:, :], op=ALU.mult), r=[gp], w=[gp])
                            ac = acc.next()
                            a3 = ac.ap[:].rearrange("p (r t) -> p r t", r=8)
                            b.op("act", lambda e, gp=gp, a3=a3: e.activation(out=a3, in_=gp.ap[:, :, 0:64], func=AF.Identity, scale=pvl(PV_FW + c)), r=[gp], w=[ac])
                            b.op("dve", lambda e, gp=gp, a3=a3: e.scalar_tensor_tensor(out=a3, in0=gp.ap[:, :, 1:65], scalar=pvl(PV_FW + 22 + c), in1=a3, op0=ALU.mult, op1=ALU.add), r=[gp, ac], w=[ac])
                            b.op("dve", lambda e, gp=gp, a3=a3: e.scalar_tensor_tensor(out=a3, in0=gp.ap[:, :, 2:66], scalar=pvl(PV_FW + 44 + c), in1=a3, op0=ALU.mult, op1=ALU.add), r=[gp, ac], w=[ac])
                            sg = sgt.next()
                            b.op("act", lambda e, sg=sg, ac=ac: e.activation(out=sg.ap[:], in_=ac.ap[:], func=AF.Silu, bias=pvl(PV_FB + c)), r=[ac], w=[sg])
                            b.op("dve", lambda e, sg=sg, pu=pu: e.tensor_tensor(out=at.ap[:, ci, :], in0=pu.ap[:, :], in1=sg.ap[:], op=ALU.mult), r=[pu, sg], w=[at])
                            if ci == 1 and pend:
                                down_proj(*pend.pop(0))
                        pend.append((wd_, at, ng, bi))
                while pend:
                    down_proj(*pend.pop(0))
            b.barrier()
            if l == 0:
                ck(13, xTt[:, 0, :])
                for bi in range(4):
                    b.dma("sp", xscr[:, :, blk(bi)], xTt[:, :, blk(bi)], r=[d_x[bi]])
            else:
                with contextlib.ExitStack() as ph:
                    yT = sb("yT", [128, 8, 512], F32, ph)
                    d_y = [Dep() for _ in range(8)]
                    yo = rot("yo", 3, [128, D], F32, ph)
                    cur = {}

                    def fin(bi, j, tm):
                        b.op("act", lambda e: e.activation(out=yT[:, j, :], in_=tm.ap[:], func=AF.Identity, scale=pv[:, 0, 64 + j:65 + j]), r=[tm], w=[d_y[j]])
                        if j == 7:
                            for tt in range(4):
                                yt = yo.next()
                                for half in range(2):
                                    ps = psA.next()
                                    for q in range(4):
                                        jj = half * 4 + q
                                        b.op("pe", lambda e, jj=jj, q=q, ps=ps: e.transpose(ps.ap[:, q * 128:(q + 1) * 128], yT[:, jj, tt * 128:(tt + 1) * 128], ident), r=[d_y[jj]], w=[ps])
                                    b.op("act", lambda e, ps=ps, yt=yt, half=half: e.activation(out=yt.ap[:, half * 512:(half + 1) * 512], in_=ps.ap[:, :], func=AF.Copy), r=[ps], w=[yt])
                                t0 = bi * 512 + tt * 128
                                b.dma("sp", y_d[t0:t0 + 128, :], yt.ap[:], r=[yt])
                    norm_phase(xTt, ph, None, None, fin)
                    b.barrier()
                xs.close()
        for l in range(2):
            b.dma("sp", flru_d[l].rearrange("j p s -> p j s"), flru[:, l, :, :], r=[d_flru])
        b.barrier()
        DBG["nins"] = b.nins


_NC = None


def _consts():
    c = np.zeros((128, NCONST), np.float32)
    c[:, 0:128] = np.eye(128, dtype=np.float32)
    p = np.arange(128)
    c[:, 128:256] = (p[:, None] // 64 == p[None, :] // 64).astype(np.float32)
    s = np.arange(64)[:, None]
    t = np.arange(64)[None, :]
    c[0:64, 256:320] = -1.0 * (t > s)
    c[0:64, 320:384] = -1.0 * (t >= s)
    c[0:64, 384:448] = (t > s)
    c[0:64, 448:512] = (t >= s)
    c[0:64, 512:576] = -1.0 * (t < s)
    s2 = np.arange(32)[:, None]
    t2 = np.arange(32)[None, :]
    c[0:32, 576:608] = (t2 >= s2)
    c[0:32, 608:640] = (t2 >= s2)
    return c


def prep(inp):
    f = lambda k: np.ascontiguousarray(np.asarray(inp[k], dtype=np.float32))
    xp, xsm = f("x_prompt"), f("x_sample")
    L = 2
    pv = np.zeros((L, 128, NPV), np.float32)
    colT = lambda v: v.reshape(-1, 128).T
    for l in range(L):
        pv[l, :, 0:8] = colT(f("norm_mix_g")[l])
        pv[l, :, 8:16] = colT(f("norm_ffn_g")[l])
        pv[l, :, 16:64] = colT(f("ada_b")[l])
        pv[l, :, 64:72] = colT(f("final_g"))
        for j in range(4):
            c0 = PV_RW + j * 9
            sl = slice(j * 128, (j + 1) * 128)
            pv[l, :, c0 + 0] = f("rwkv_w0")[l, 0, sl]
            pv[l, :, c0 + 1] = f("rwkv_w0")[l, 1, sl]
            pv[l, :, c0 + 2] = f("rwkv_a0")[l, 0, sl]
            pv[l, :, c0 + 3] = f("rwkv_a0")[l, 1, sl]
            pv[l, :, c0 + 4] = f("rwkv_k_k")[l, sl]
            pv[l, :, c0 + 5] = f("rwkv_k_a")[l, sl]
            pv[l, :, c0 + 6] = f("rwkv_r_k")[l].reshape(-1)[sl]
            pv[l, :, c0 + 7] = f("rwkv_ln_w")[l, sl]
            pv[l, :, c0 + 8] = f("rwkv_ln_b")[l, sl]
        for j in range(2):
            c0 = PV_LRU + j * 11
            sl = slice(j * 128, (j + 1) * 128)
            for k in range(4):
                pv[l, :, c0 + k] = f("lru_conv_w")[l, k, sl]
            pv[l, :, c0 + 4] = f("lru_conv_b")[l, sl]
            for d in range(2):
                pv[l, :, c0 + 5 + d] = f("lru_ba")[l, d, sl]
                pv[l, :, c0 + 7 + d] = f("lru_bx")[l, d, sl]
                pv[l, :, c0 + 9 + d] = f("lru_lambda")[l, d, sl]
            c0 = PV_HG + j * 5
            for d in range(2):
                for l2 in range(2):
                    pv[l, :, c0 + d * 2 + l2] = f("hgrn_lb_logits")[d, l2, sl]
            pv[l, :, c0 + 4] = f("hgrn_norm_g")[l, sl]
        for k in range(3):
            pv[l, :, PV_FW + k * 22:PV_FW + (k + 1) * 22] = colT(f("ffn_conv_w")[l, k])
        pv[l, :, PV_FB:PV_FB + 22] = colT(f("ffn_conv_b")[l])
    consts = _consts()
    shared = dict(
        consts=consts, pv=pv, ada_w=f("ada_w"), w_in=f("w_in"), w_out=f("w_out"),
        rwkv_w_up=f("rwkv_w_up").reshape(2, 128, 512), rwkv_a_up=f("rwkv_a_up").reshape(2, 128, 512),
        rwkv_g_up=f("rwkv_g_up"), lru_wa=f("lru_wa"), lru_wx=f("lru_wx"),
        ffn_w_gate=f("ffn_w_gate"), ffn_w_up=f("ffn_w_up"), ffn_w_down=f("ffn_w_down"))
    srw, slru, shg = f("state_rwkv"), f("state_rglru"), f("state_hgrn")
    in_maps = []
    for core in range(8):
        m = dict(shared)
        irw = np.zeros((2, 2, 8, 4, 128, 64), np.float32)
        ilru = np.zeros((2, 2, 128, 16), np.float32)
        ihg = np.zeros((2, 2, 8, 2, 128, 64), np.float32)
        if core < 4:
            bb = core
            m["x"] = xsm[bb]
            m["cond"] = np.ascontiguousarray(f("c")[bb].reshape(8, 128).T)
            m["carry"] = np.ones((128, 1), np.float32)
            m["fmask"] = np.zeros((128, 7), np.float32)
            for l in range(2):
                for d in range(2):
                    st = srw[bb, l, d].transpose(0, 2, 1).reshape(4, 128, 64)
                    irw[l, d, 0] = st
                    ihg[l, d, 0] = shg[bb, l, d].reshape(2, 128, 64)
                    ilru[l, :, :, 0 * 2 + d] = slru[bb, l, d].reshape(2, 128)
        else:
            m["x"] = np.ascontiguousarray(xp[(core - 4) * 8:(core - 3) * 8].reshape(T, D))
            m["cond"] = np.ascontiguousarray(f("c_ctx").reshape(8, 128).T)
            m["carry"] = np.zeros((128, 1), np.float32)
            fm = np.ones((128, 7), np.float32)
            fm[:, 3] = 0.0
            m["fmask"] = fm
        m["irw"], m["ilru"], m["ihg"] = irw, ilru, ihg
        in_maps.append(m)
    return in_maps


def kernel(**inp):
    global _NC
    in_maps = prep(inp)
    xp = inp["x_prompt"]
    if _NC is None:
        _NC = build_nc()
    res = run_bass_kernel_spmd(_NC, in_maps, core_ids=list(range(8)))
    R = res.results
    y_prompt = np.concatenate([R[c]["y"].reshape(8, 256, D) for c in range(4, 8)], axis=0)
    y_sample = np.stack([R[c]["y"] for c in range(4)], axis=0)
    new_rwkv = np.zeros((32, 2, 2, 8, 64, 64), np.float32)
    new_lru = np.zeros((32, 2, 2, 256), np.float32)
    new_hg = np.zeros((32, 2, 2, 4, 64, 64), np.float32)
    for c in range(4, 8):
        frw, flr, fhg = R[c]["frw"], R[c]["flru"], R[c]["fhg"]
        for n in range(8):
            sq = (c - 4) * 8 + n
            for l in range(2):
                for d in range(2):
                    s = (7 - n) if d else n
                    new_rwkv[sq, l, d] = frw[l, d, s].reshape(8, 64, 64).transpose(0, 2, 1)
                    new_hg[sq, l, d] = fhg[l, d, s].reshape(4, 64, 64)
                    new_lru[sq, l, d] = flr[l, :, :, s * 2 + d].reshape(256)
    return (y_prompt, y_sample, new_rwkv, new_lru, new_hg)
```

```python
import contextlib, math
import numpy as np
import concourse.bass as bass
import concourse.mybir as mybir
from concourse.bass_utils import run_bass_kernel_spmd

F32 = mybir.dt.float32
BF16 = mybir.dt.bfloat16
AF = mybir.ActivationFunctionType
ALU = mybir.AluOpType

T = 2048
D = 1024
NPV = 228
NCONST = 640
DFF = 2816
NFC = 22
CW = -math.exp(-0.5)
PV_RW, PV_LRU, PV_HG, PV_FW, PV_FB = 72, 108, 130, 140, 206
DBG = {}


class Dep:
    __slots__ = ("w", "r")

    def __init__(s):
        s.w = None
        s.r = {}


class Tl:
    def __init__(s, ap):
        s.ap = ap
        s.d = Dep()


class Rot:
    def __init__(s, tl):
        s.tl = tl
        s.i = 0

    def next(s):
        x = s.tl[s.i % len(s.tl)]
        s.i += 1
        return x


class B:
    def __init__(s, nc, es):
        s.nc = nc
        s.eng = {}
        for name, obj in (("pe", nc.tensor), ("dve", nc.vector), ("act", nc.scalar),
                          ("pool", nc.gpsimd), ("sp", nc.sync)):
            sem = es.enter_context(nc.semaphore("sem_" + name))
            s.eng[name] = dict(obj=obj, sem=sem, cnt=0, known={})
        s.dsemq = {q: [[es.enter_context(nc.semaphore("dsem%s%d" % (q, i))), 0] for i in range(24)] for q in ("sp", "pool")}
        s.dsem = s.dsemq["sp"] + s.dsemq["pool"]
        s.di = {"sp": 0, "pool": 0}
        s.nins = 0

    def _wait(s, e, evs):
        best = {}
        for (sem, val) in evs:
            k = id(sem)
            if k not in best or best[k][1] < val:
                best[k] = (sem, val)
        for k, (sem, val) in best.items():
            if e["known"].get(k, 0) < val:
                e["obj"].wait_ge(sem, val)
                e["known"][k] = val

    def _collect(s, en, r, w):
        evs = []
        for t in r:
            if t.w is not None:
                evs.append(t.w[:2])
        for t in w:
            if t.w is not None and not (t.w[2] == en and en == "pe"):
                evs.append(t.w[:2])
            for rd in t.r.values():
                if not (rd[2] == en and en == "pe"):
                    evs.append(rd[:2])
        return evs

    def _update(s, ev, r, w):
        for t in r:
            t.r[id(ev[0])] = ev
        for t in w:
            t.w = ev
            t.r = {}

    @staticmethod
    def _flat(lst):
        out = []
        for x in lst:
            if isinstance(x, (list, tuple)):
                out += B._flat(x)
            else:
                out.append(x.d if isinstance(x, Tl) else x)
        return out

    def op(s, en, fn, r=(), w=()):
        r = s._flat(r)
        w = s._flat(w)
        e = s.eng[en]
        s._wait(e, s._collect(en, r, w))
        ins = fn(e["obj"])
        e["cnt"] += 1
        ins.then_inc(e["sem"], 1)
        s.nins += 1
        ev = (e["sem"], e["cnt"], en)
        s._update(ev, r, w)
        return ev

    def dma(s, qn, out, in_, r=(), w=()):
        r = s._flat(r)
        w = s._flat(w)
        e = s.eng[qn]
        slot = s.dsemq[qn][s.di[qn] % 24]
        s.di[qn] += 1
        evs = s._collect(None, r, w)
        if slot[1] > 0:
            evs.append((slot[0], slot[1]))
        s._wait(e, evs)
        ins = e["obj"].dma_start(out=out, in_=in_)
        slot[1] += 16
        ins.then_inc(slot[0], 16)
        s.nins += 1
        ev = (slot[0], slot[1], "dma")
        s._update(ev, r, w)
        return ev

    def barrier(s):
        for en, e in s.eng.items():
            evs = []
            for on, o in s.eng.items():
                if on != en and o["cnt"] > 0:
                    evs.append((o["sem"], o["cnt"]))
            for sl in s.dsem:
                if sl[1] > 0:
                    evs.append((sl[0], sl[1]))
            s._wait(e, evs)


class StopBuild(Exception):
    pass


def build_nc(stop=None):
    nc = bass.Bass("TRN2", target_bir_lowering=False)
    try:
        _build(nc, stop)
    except StopBuild:
        pass
    return nc


def _build(nc, stop):

    def din(name, shape):
        return nc.dram_tensor(name, list(shape), F32, kind="ExternalInput").ap()

    def dout(name, shape):
        return nc.dram_tensor(name, list(shape), F32, kind="ExternalOutput").ap()

    x_d = din("x", [T, D])
    cond_d = din("cond", [128, 8])
    carry_d = din("carry", [128, 1])
    fmask_d = din("fmask", [128, 7])
    consts_d = din("consts", [128, NCONST])
    pv_d = din("pv", [2, 128, NPV])
    irw_d = din("irw", [2, 2, 8, 4, 128, 64])
    ilru_d = din("ilru", [2, 2, 128, 16])
    ihg_d = din("ihg", [2, 2, 8, 2, 128, 64])
    ada_w_d = din("ada_w", [2, D, 6 * D])
    w_in_d = din("w_in", [2, D, 3712])
    w_out_d = din("w_out", [2, D, D])
    w_up_d = din("rwkv_w_up", [2, 128, 512])
    a_up_d = din("rwkv_a_up", [2, 128, 512])
    g_up_d = din("rwkv_g_up", [2, 128, 512])
    lwa_d = din("lru_wa", [2, 2, 4, 64, 64])
    lwx_d = din("lru_wx", [2, 2, 4, 64, 64])
    wg_d = din("ffn_w_gate", [2, D, DFF])
    wu_d = din("ffn_w_up", [2, D, DFF])
    wd_d = din("ffn_w_down", [2, DFF, D])
    y_d = dout("y", [T, D])
    frw_d = dout("frw", [2, 2, 8, 4, 128, 64])
    flru_d = dout("flru", [2, 2, 128, 16])
    fhg_d = dout("fhg", [2, 2, 8, 2, 128, 64])
    xscr = nc.dram_tensor("xscr", [128, 8, T], F32, kind="Internal").ap()
    mixscr = nc.dram_tensor("mixscr", [8, 128, T], BF16, kind="Internal").ap()
    dbg_d = dout("dbg", [128, 8 * T]) if stop is not None else None

    with contextlib.ExitStack() as es:
        b = B(nc, es)

        uid = [0]

        def sb(name, shape, dt=F32, st=es):
            uid[0] += 1
            return st.enter_context(nc.sbuf_tensor("%s_%d" % (name, uid[0]), list(shape), dt))

        def tl(name, shape, dt=F32, st=es):
            return Tl(sb(name, shape, dt, st))

        def rot(name, n, shape, dt=F32, st=es):
            return Rot([tl("%s%d" % (name, i), shape, dt, st) for i in range(n)])

        psA = Rot([Tl(es.enter_context(nc.psum_tensor("psA%d" % i, [128, 512], F32))) for i in range(8)])
        psB = psA

        cst = sb("cst", [128, NCONST])
        identb = sb("identb", [128, 128], BF16)
        bonesb = sb("bonesb", [128, 128], BF16)
        onesb = sb("onesb", [128, 128], BF16)
        pv = sb("pv", [128, 2, NPV])
        carry = sb("carry", [128, 1])
        fmask = sb("fmask", [128, 7])
        condt = sb("condt", [128, 8])
        scb = sb("scb", [128, 8], BF16)
        modt = sb("modt", [128, 2, 48])
        gs = sb("gs", [128, 2, 16])
        der = sb("der", [128, 2, 32])
        hT = sb("hT", [128, 8, T], BF16)
        flru = sb("flru", [128, 2, 2, 16])
        ilru = sb("ilru", [128, 2, 2, 16])
        d_hT = [Dep() for _ in range(4)]
        d_mix = [Dep() for _ in range(8)]
        d_x = [[Dep() for _ in range(8)] for _ in range(4)]
        d_flru = Dep()
        wtile = rot("wt", 3, [128, 8, 128], BF16)
        ident = cst[:, 0:128]
        m1 = cst[0:64, 256:384]
        m2 = cst[0:64, 384:512]
        m3 = cst[0:64, 512:576]
        mh = cst[0:32, 576:608]

        def ck(n, src=None):
            if stop is not None and stop == n:
                b.barrier()
                if src is not None:
                    b.dma("pool", dbg_d[0:src.shape[0], 0:src.shape[1]], src)
                b.barrier()
                DBG["nins"] = b.nins
                raise StopBuild()


        d0 = Dep()
        b.dma("sp", cst[:], consts_d[:, :], w=[d0])
        for l in range(2):
            b.dma("sp", pv[:, l, :], pv_d[l], w=[d0])
        b.dma("sp", carry[:], carry_d[:, :], w=[d0])
        b.dma("sp", fmask[:], fmask_d[:, :], w=[d0])
        b.dma("sp", condt[:], cond_d[:, :], w=[d0])
        for l in range(2):
            b.dma("sp", ilru[:, l, :, :], ilru_d[l].rearrange("j p s -> p j s"), w=[d0])
        b.op("dve", lambda e: e.tensor_copy(out=identb[:], in_=cst[:, 0:128]), r=[d0], w=[d0])
        b.op("dve", lambda e: e.tensor_copy(out=bonesb[:], in_=cst[:, 128:256]), r=[d0], w=[d0])
        b.op("dve", lambda e: e.memset(onesb[:], 1.0), w=[d0])
        b.op("dve", lambda e: e.memset(flru[:], 0.0), w=[d0])
        b.barrier()
        b.op("act", lambda e: e.activation(out=scb[:], in_=condt[:], func=AF.Silu), w=[d0])
        for l in range(2):
            for j in range(4):
                c = PV_RW + j * 9 + 5
                b.op("dve", lambda e, c=c, j=j: e.tensor_scalar(out=der[:, l, j:j + 1], in0=pv[:, l, c:c + 1],
                                                               scalar1=-1.0, scalar2=1.0, op0=ALU.mult, op1=ALU.add), w=[d0])
            for j in range(2):
                for d in range(2):
                    c = PV_LRU + j * 11 + 9 + d
                    o = 4 + j * 2 + d
                    b.op("act", lambda e, c=c, o=o: e.activation(out=der[:, l, o:o + 1], in_=pv[:, l, c:c + 1],
                                                                 func=AF.Exp, scale=-1.0), w=[d0])
                    b.op("act", lambda e, o=o: e.activation(out=der[:, l, o:o + 1], in_=der[:, l, o:o + 1],
                                                            func=AF.Ln, bias=1.0), r=[d0], w=[d0])
                    b.op("dve", lambda e, o=o: e.tensor_scalar(out=der[:, l, o + 4:o + 5], in0=der[:, l, o:o + 1],
                                                              scalar1=-16.0, scalar2=None, op0=ALU.mult), r=[d0], w=[d0])
                    b.op("dve", lambda e, o=o: e.tensor_scalar(out=der[:, l, o:o + 1], in0=der[:, l, o:o + 1],
                                                              scalar1=-8.0, scalar2=None, op0=ALU.mult), r=[d0], w=[d0])
                    o2 = 12 + j * 2 + d
                    if l == 0:
                        b.op("dve", lambda e, o2=o2: e.memset(der[:, l, o2:o2 + 1], 0.0), w=[d0])
                        b.op("dve", lambda e, o2=o2: e.memset(der[:, l, o2 + 4:o2 + 5], 1.0), w=[d0])
                    else:
                        ch = PV_HG + j * 5 + d * 2
                        b.op("dve", lambda e, o2=o2, ch=ch: e.tensor_tensor(out=der[:, l, o2:o2 + 1], in0=pv[:, l, ch + 1:ch + 2],
                                                                            in1=pv[:, l, ch:ch + 1], op=ALU.subtract), w=[d0])
                        b.op("act", lambda e, o2=o2: e.activation(out=der[:, l, o2:o2 + 1], in_=der[:, l, o2:o2 + 1],
                                                                  func=AF.Sigmoid), r=[d0], w=[d0])
                        b.op("dve", lambda e, o2=o2: e.tensor_scalar(out=der[:, l, o2 + 4:o2 + 5], in0=der[:, l, o2:o2 + 1],
                                                                    scalar1=-1.0, scalar2=1.0, op0=ALU.mult, op1=ALU.add), r=[d0], w=[d0])
        b.barrier()
        ck(1, der[:, 0, :])

        with contextlib.ExitStack() as ph:
            apc = rot("apc", 2, [128, 8, 512], BF16, ph)
            for l in range(2):
                ps = psA.next()
                for pc in range(12):
                    wt = apc.next()
                    b.dma("pool", wt.ap[:], ada_w_d[l].rearrange("(j p) c -> p j c", p=128)[:, :, pc * 512:(pc + 1) * 512], w=[wt])
                    for mm in range(4):
                        m = pc * 4 + mm
                        for j in range(8):
                            b.op("pe", lambda e, j=j, m=m, mm=mm, wt=wt, ps=ps: e.matmul(
                                ps.ap[:, m:m + 1], lhsT=wt.ap[:, j, mm * 128:(mm + 1) * 128], rhs=scb[:, j:j + 1],
                                start=(j == 0), stop=(j == 7)), r=[wt], w=[ps])
                b.op("dve", lambda e, ps=ps, l=l: e.tensor_tensor(out=modt[:, l, :], in0=ps.ap[:, 0:48], in1=pv[:, l, 16:64], op=ALU.add),
                     r=[ps], w=[d0])
                b.op("dve", lambda e, l=l: e.scalar_tensor_tensor(out=gs[:, l, 0:8], in0=modt[:, l, 8:16], scalar=1.0, in1=pv[:, l, 0:8],
                                                                  op0=ALU.add, op1=ALU.mult), r=[d0], w=[d0])
                b.op("dve", lambda e, l=l: e.scalar_tensor_tensor(out=gs[:, l, 8:16], in0=modt[:, l, 32:40], scalar=1.0, in1=pv[:, l, 8:16],
                                                                  op0=ALU.add, op1=ALU.mult), r=[d0], w=[d0])
        b.barrier()
        ck(2, modt[:, 0, :])

        def blk(bi):
            return slice(bi * 512, (bi + 1) * 512)

        def norm_phase(xT, ph, gfn, sfn, outfn):
            sqb = rot("nsq", 3, [128, 512], BF16, ph)
            rsb = rot("nrs", 2, [128, 512], F32, ph)
            tmb = rot("ntm", 3, [128, 512], F32, ph)
            for bi in range(4):
                ps = psA.next()
                for j in range(8):
                    sq = sqb.next()
                    b.op("act", lambda e, j=j, sq=sq: e.activation(out=sq.ap[:], in_=xT[:, j, blk(bi)], func=AF.Square),
                         r=[d_x[bi][j]], w=[sq])
                    b.op("pe", lambda e, j=j, sq=sq, ps=ps: e.matmul(ps.ap[:, :], lhsT=onesb[:], rhs=sq.ap[:], start=(j == 0), stop=(j == 7)),
                         r=[sq], w=[ps])
                rs = rsb.next()
                b.op("act", lambda e, rs=rs, ps=ps: e.activation(out=rs.ap[:], in_=ps.ap[:, :], func=AF.Ln, scale=1.0 / D, bias=1e-6),
                     r=[ps], w=[rs])
                b.op("act", lambda e, rs=rs: e.activation(out=rs.ap[:], in_=rs.ap[:], func=AF.Exp, scale=-0.5), r=[rs], w=[rs])
                for j in range(8):
                    tm = tmb.next()
                    b.op("dve", lambda e, j=j, tm=tm, rs=rs: e.tensor_tensor(out=tm.ap[:], in0=xT[:, j, blk(bi)], in1=rs.ap[:], op=ALU.mult),
                         r=[d_x[bi][j], rs], w=[tm])
                    outfn(bi, j, tm)

        def to_hT(l, which):
            def f(bi, j, tm):
                g = gs[:, l, which * 8 + j:which * 8 + j + 1]
                sh = modt[:, l, which * 24 + j:which * 24 + j + 1]
                b.op("act", lambda e: e.activation(out=hT[:, j, blk(bi)], in_=tm.ap[:], func=AF.Identity, scale=g, bias=sh),
                     r=[tm], w=[d_hT[bi]])
            return f

        def load_w(src, cols):
            wt = wtile.next()
            n = cols.stop - cols.start
            b.dma("pool", wt.ap[:, :, 0:n], src.rearrange("(j p) c -> p j c", p=128)[:, :, cols], w=[wt])
            return wt

        def proj(l, col0, ncols, evac):
            wt = load_w(w_in_d[l], slice(col0, col0 + ncols))
            for bi in range(4):
                ps = psA.next()
                for j in range(8):
                    b.op("pe", lambda e, j=j, ps=ps: e.matmul(ps.ap[0:ncols, :], lhsT=wt.ap[:, j, 0:ncols], rhs=hT[:, j, blk(bi)],
                                                              start=(j == 0), stop=(j == 7)), r=[wt, d_hT[bi]], w=[ps])
                evac(bi, ps)

        def act_evac(dst, func=AF.Copy, **kw):
            def f(bi, ps):
                b.op("act", lambda e: e.activation(out=dst.ap[:, blk(bi)], in_=ps.ap[:, :], func=func, **kw), r=[ps], w=[dst])
            return f

        def headsum(src, consume, ph_rot):
            for bi in range(4):
                ps = psA.next()
                b.op("pe", lambda e, ps=ps: e.matmul(ps.ap[:, :], lhsT=bonesb[:], rhs=src.ap[:, blk(bi)], start=True, stop=True),
                     r=[src], w=[ps])
                consume(bi, ps)

        def rv(ap, d):
            return ap[:, ::-1] if d else ap[:, :]

        for l in range(2):
            pvl = lambda c: pv[:, l, c:c + 1]
            if l == 0:
                xs = contextlib.ExitStack()
                xTt = sb("xT", [128, 8, T], F32, xs)
            if l == 0:
                with contextlib.ExitStack() as ph:
                    xin = rot("xin", 3, [128, D], F32, ph)
                    for tt in range(16):
                        xt = xin.next()
                        b.dma("sp", xt.ap[:], x_d[tt * 128:(tt + 1) * 128, :], w=[xt])
                        for half in range(2):
                            ps = psA.next()
                            for q in range(4):
                                j = half * 4 + q
                                b.op("pe", lambda e, j=j, q=q, ps=ps, xt=xt: e.transpose(ps.ap[:, q * 128:(q + 1) * 128], xt.ap[:, j * 128:(j + 1) * 128], ident),
                                     r=[xt], w=[ps])
                            b.op("act" if half else "dve",
                                 (lambda e, ps=ps, half=half, tt=tt: e.activation(out=xTt[:, half * 4:half * 4 + 4, tt * 128:(tt + 1) * 128],
                                                                                 in_=ps.ap[:, :].rearrange("p (q t) -> p q t", q=4), func=AF.Copy)) if half else
                                 (lambda e, ps=ps, half=half, tt=tt: e.tensor_copy(out=xTt[:, half * 4:half * 4 + 4, tt * 128:(tt + 1) * 128],
                                                                                  in_=ps.ap[:, :].rearrange("p (q t) -> p q t", q=4))),
                                 r=[ps], w=[d_x[tt // 4][half * 4:half * 4 + 4]])
            with contextlib.ExitStack() as ph:
                norm_phase(xTt, ph, None, None, to_hT(l, 0))
            if l == 0:
                for bi in range(4):
                    b.dma("sp", xscr[:, :, blk(bi)], xTt[:, :, blk(bi)], r=[d_x[bi]])
            b.barrier()
            if l == 0:
                ck(3, hT[:, 0, :])
            xs.close()

            with contextlib.ExitStack() as mp:
                lora_w = tl("lora_w", [128, T], BF16, mp)
                lora_a = tl("lora_a", [128, T], BF16, mp)
                lora_g = tl("lora_g", [128, T], BF16, mp)
                wup = tl("wup", [128, 512], BF16, mp)
                aup = tl("aup", [128, 512], BF16, mp)
                gup = tl("gup", [128, 512], BF16, mp)
                b.dma("pool", wup.ap[:], w_up_d[l], w=[wup])
                b.dma("pool", aup.ap[:], a_up_d[l], w=[aup])
                b.dma("pool", gup.ap[:], g_up_d[l], w=[gup])
                proj(l, 1536, 128, act_evac(lora_w, AF.Tanh))
                proj(l, 1664, 128, act_evac(lora_a))
                proj(l, 1792, 128, act_evac(lora_g, AF.Sigmoid))
                if l == 0:
                    ck(4, lora_w.ap[:, :])

                with contextlib.ExitStack() as rp:
                    rst64 = sb("rst64", [128, T], BF16, rp)
                    b.op("dve", lambda e: e.memset(rst64[:], 1.0), w=[d0])
                    b.op("dve", lambda e: e.memset(rst64[:].rearrange("p (c i) -> p c i", i=64)[:, :, 0:1], 0.0), w=[d0])
                    r_bf = tl("r_bf", [128, T], BF16, rp)
                    k32 = tl("k32", [128, T], BF16, rp)
                    v_bf = tl("v_bf", [128, T], BF16, rp)
                    t1 = tl("t1", [128, T], F32, rp)
                    kk_bf = tl("kk_bf", [128, T], BF16, rp)
                    sg32 = tl("sg32", [128, T], F32, rp)
                    bs32 = tl("bs32", [128, T], F32, rp)
                    a_bf = tl("a_bf", [128, T], BF16, rp)
                    b_bf = tl("b_bf", [128, T], BF16, rp)
                    kd_bf = tl("kd_bf", [128, T], BF16, rp)
                    s4 = b_bf
                    bon = kk_bf
                    s4p = kd_bf
                    E = a_bf
                    mst = a_bf
                    vdir = [v_bf, tl("vrev", [128, T], BF16, rp)]
                    KR = [tl("KR%d" % d, [128, 32, 2, 64], BF16, rp) for d in range(2)]
                    KH = [tl("KH%d" % d, [128, T], BF16, rp) for d in range(2)]
                    BH = [tl("BH%d" % d, [128, T], BF16, rp) for d in range(2)]
                    KB = [tl("KB%d" % d, [128, T], BF16, rp) for d in range(2)]
                    BB = [tl("BB%d" % d, [128, T], BF16, rp) for d in range(2)]
                    WL = [tl("WL%d" % d, [128, 32], F32, rp) for d in range(2)]
                    o32 = sb("o32", [128, T], F32, rp)
                    P32 = [tl("P32%d" % d, [128, 64], F32, rp) for d in range(2)]
                    Pbf = [tl("Pbf%d" % d, [128, 64], BF16, rp) for d in range(2)]
                    pin = rot("pin", 2, [128, 64], F32, rp)
                    pout = rot("pout", 2, [128, 64], F32, rp)
                    t1b = t1.ap[:].bitcast(BF16)
                    sgb = sg32.ap[:].bitcast(BF16)
                    bsb = bs32.ap[:].bitcast(BF16)
                    bbb = b_bf.ap[:]
                    tok = [[tl("tok%d%d" % (d, p), [64, 3, 2, 2, 64], BF16, rp) for p in range(2)] for d in range(2)]
                    Pbd = [tl("Pbd%d" % d, [128, 128], BF16, rp) for d in range(2)]
                    for t_ in tok[0] + tok[1] + Pbd:
                        b.op("dve", lambda e, t_=t_: e.memset(t_.ap[:], 0.0), w=[t_])
                    for d_ in range(2):
                        tok[d_].append(Tl(sgb[0:64, 2304 + d_ * 768:3072 + d_ * 768].rearrange("p (q a b c) -> p q a b c", q=3, a=2, b=2)))
                        tok[d_].append(Tl(bsb[0:64, 2304 + d_ * 768:3072 + d_ * 768].rearrange("p (q a b c) -> p q a b c", q=3, a=2, b=2)))
                    KRm = {}
                    for d_ in range(2):
                        for h_ in range(2):
                            for p_ in range(2):
                                KRm[(d_, h_, p_)] = tl("KRm%d%d%d" % (d_, h_, p_), [128, 128], BF16, rp)
                                b.op("dve", lambda e, t_=KRm[(d_, h_, p_)]: e.memset(t_.ap[:], 0.0), w=[KRm[(d_, h_, p_)]])
                            for p_ in range(2, 4):
                                ix = (d_ * 2 + h_) * 2 + (p_ - 2)
                                KRm[(d_, h_, p_)] = Tl(bbb[:, ix * 128:(ix + 1) * 128])
                    UA = {}
                    UT = {}
                    UTt = sb("UTt", [64, 4 * 960], BF16, rp)
                    b.op("dve", lambda e: e.memset(UTt[:], 0.0), w=[d0])
                    UTaps = [UTt[:], t1b[0:64, 0:3840]]
                    UTv = [x.rearrange("p (u r) -> p u r", u=4) for x in UTaps]
                    TRD = {}
                    for ts_ in range(2):
                        for q in range(1, 6):
                            for pa in range(2):
                                TRD[(ts_, pa, q)] = (Dep(), Dep())
                        for u in range(4):
                            tr = [None]
                            for q in range(1, 6):
                                o_ = u * 960 + (q - 1) * 192
                                dA, dT = TRD[(ts_, u // 2, q)]
                                X_ = UTaps[ts_]
                                ent = dict(lo=Tl(X_[:, o_ + 128:o_ + 192]), up=Tl(X_[:, o_:o_ + 64]), tt=Tl(X_[:, o_ + 64:o_ + 128]), ut=Tl(X_[:, o_:o_ + 128]))
                                ent["lo"].d = dA; ent["up"].d = dA; ent["tt"].d = dT; ent["ut"].d = dA
                                ent["dA"], ent["dT"] = dA, dT
                                tr.append(ent)
                            UT[(ts_, u)] = tr
                    UAt = []
                    UAD = []
                    for p in range(4):
                        if p < 2:
                            t_ = sb("UAt%d" % p, [64, 4 * 576], BF16, rp)[:]
                            b.op("dve", lambda e, t_=t_: e.memset(t_, 0.0), w=[d0])
                        else:
                            t_ = (sgb if p == 2 else bsb)[0:64, 0:2304]
                        UAt.append(t_)
                        dd = dict(ttf=Dep(), xs=[Dep(), Dep()], y=[Dep(), Dep()])
                        UAD.append(dd)
                        for u in range(4):
                            o_ = u * 576
                            hh_u = u % 2
                            yU = Tl(t_[:, o_ + 448 + hh_u * 64:o_ + 512 + hh_u * 64]); yU.d = dd["y"][u // 2]
                            upad = Tl(t_[:, o_ + 448:o_ + 576]); upad.d = dd["y"][u // 2]
                            sc = Tl(t_[:, o_:o_ + 320])
                            ud = dict(sc=sc, y=[None] * 6 + [yU], upad=upad, ttf=Tl(t_[:, o_ + 320:o_ + 384]), xs=Tl(t_[:, o_ + 384:o_ + 448]))
                            ud["ttf"].d = dd["ttf"]; ud["xs"].d = dd["xs"][u // 2]
                            for nm, lo_, hi_ in (("sc1", 0, 128), ("sc2", 128, 256), ("sc3", 256, 320), ("up0", 0, 64)):
                                ud[nm] = Tl(t_[:, o_ + lo_:o_ + hi_])
                                ud[nm].d = sc.d
                            UA[(u, p)] = ud
                    d_o = [Dep() for _ in range(32)]
                    for j in range(4):
                        c0 = PV_RW + j * 9
                        proj(l, j * 128, 128, act_evac(r_bf))
                        proj(l, 512 + j * 128, 128, act_evac(k32))
                        proj(l, 1024 + j * 128, 128, act_evac(v_bf))
                        b.op("dve", lambda e: e.memset(o32[:], 0.0), w=d_o)
                        b.op("dve", lambda e: e.tensor_scalar(out=t1.ap[:], in0=k32.ap[:], scalar1=pvl(c0 + 4), scalar2=None, op0=ALU.mult),
                             r=[k32], w=[t1])
                        b.op("act", lambda e: e.activation(out=s4.ap[:], in_=t1.ap[:], func=AF.Square), r=[t1], w=[s4])

                        def kk_cons(bi, ps):
                            b.op("act", lambda e: e.activation(out=sg32.ap[:, blk(bi)], in_=ps.ap[:, :], func=AF.Ln, bias=1e-12), r=[ps], w=[sg32])
                            b.op("act", lambda e: e.activation(out=sg32.ap[:, blk(bi)], in_=sg32.ap[:, blk(bi)], func=AF.Exp, scale=-0.5), r=[sg32], w=[sg32])
                            b.op("dve", lambda e: e.tensor_tensor(out=kk_bf.ap[:, blk(bi)], in0=t1.ap[:, blk(bi)], in1=sg32.ap[:, blk(bi)], op=ALU.mult),
                                 r=[t1, sg32], w=[kk_bf])
                        headsum(s4, kk_cons, None)
                        b.op("act", lambda e: e.activation(out=vdir[1].ap[:], in_=v_bf.ap[:, ::-1], func=AF.Copy), r=[v_bf], w=[vdir[1]])
                        for d in range(2):
                            pr_ = slice(d * 64, d * 64 + 64)
                            for bi in range(4):
                                ps = psA.next()
                                b.op("pe", lambda e, ps=ps: e.matmul(ps.ap[:, :], lhsT=wup.ap[pr_, j * 128:(j + 1) * 128], rhs=lora_w.ap[pr_, blk(bi)],
                                                                    start=True, stop=True), r=[wup, lora_w], w=[ps])
                                b.op("act", lambda e, ps=ps: e.activation(out=sg32.ap[:, blk(bi)], in_=ps.ap[:, :], func=AF.Sigmoid, bias=pvl(c0 + d)),
                                     r=[ps], w=[sg32])
                                ps2 = psA.next()
                                b.op("pe", lambda e, ps2=ps2: e.matmul(ps2.ap[:, :], lhsT=aup.ap[pr_, j * 128:(j + 1) * 128], rhs=lora_a.ap[pr_, blk(bi)],
                                                                      start=True, stop=True), r=[aup, lora_a], w=[ps2])
                                b.op("act", lambda e, ps2=ps2: e.activation(out=a_bf.ap[:, blk(bi)], in_=ps2.ap[:, :], func=AF.Sigmoid, bias=pvl(c0 + 2 + d)),
                                     r=[ps2], w=[a_bf])
                            b.op("dve", lambda e: e.tensor_tensor_scan(out=bs32.ap[:], data0=rst64[:], data1=rv(sg32.ap, d), initial=0.0,
                                                                       op0=ALU.mult, op1=ALU.add), r=[sg32], w=[bs32])
                            b.op("dve", lambda e: e.tensor_tensor(out=b_bf.ap[:], in0=kk_bf.ap[:], in1=a_bf.ap[:], op=ALU.mult), r=[kk_bf, a_bf], w=[b_bf])
                            b.op("dve", lambda e: e.tensor_scalar(out=a_bf.ap[:], in0=a_bf.ap[:], scalar1=pvl(c0 + 5), scalar2=der[:, l, j:j + 1],
                                                                  op0=ALU.mult, op1=ALU.add), r=[a_bf, b_bf], w=[a_bf])
                            b.op("dve", lambda e: e.tensor_tensor(out=kd_bf.ap[:], in0=k32.ap[:], in1=a_bf.ap[:], op=ALU.mult), r=[k32, a_bf], w=[kd_bf])
                            krv = KR[d].ap[:].rearrange("p c two i -> p two c i")
                            b.op("act", lambda e: e.activation(out=E.ap[:], in_=bs32.ap[:], func=AF.Exp, scale=CW), r=[bs32], w=[E])
                            b.op("dve", lambda e: e.tensor_tensor(out=krv[:, 1], in0=rv(r_bf.ap, d).rearrange("p (c i) -> p c i", i=64),
                                                                  in1=E.ap[:].rearrange("p (c i) -> p c i", i=64), op=ALU.mult), r=[r_bf, E], w=[KR[d]])
                            b.op("act", lambda e: e.activation(out=WL[d].ap[:], in_=bs32.ap[:].rearrange("p (c i) -> p c i", i=64)[:, :, 63],
                                                               func=AF.Exp, scale=CW), r=[bs32], w=[WL[d]])
                            b.op("act", lambda e: e.activation(out=E.ap[:], in_=bs32.ap[:], func=AF.Exp, scale=-CW), r=[bs32], w=[E])
                            b.op("dve", lambda e: e.tensor_tensor(out=KH[d].ap[:], in0=rv(kd_bf.ap, d), in1=E.ap[:], op=ALU.mult), r=[kd_bf, E], w=[KH[d]])
                            b.op("dve", lambda e: e.tensor_tensor(out=BH[d].ap[:], in0=rv(b_bf.ap, d), in1=E.ap[:], op=ALU.mult), r=[b_bf, E], w=[BH[d]])
                            b.op("dve", lambda e: e.tensor_tensor(out=t1.ap[:], in0=bs32.ap[:], in1=rv(sg32.ap, d), op=ALU.subtract), r=[bs32, sg32], w=[t1])
                            b.op("act", lambda e: e.activation(out=E.ap[:], in_=t1.ap[:], func=AF.Exp, scale=CW), r=[t1], w=[E])
                            b.op("dve", lambda e: e.tensor_tensor(out=krv[:, 0], in0=rv(kk_bf.ap, d).rearrange("p (c i) -> p c i", i=64),
                                                                  in1=E.ap[:].rearrange("p (c i) -> p c i", i=64), op=ALU.mult), r=[kk_bf, E], w=[KR[d]])
                            wlb = WL[d].ap[:].rearrange("p (c o) -> p c o", o=1).to_broadcast([128, 32, 64])
                            c3 = lambda t_: t_.ap[:].rearrange("p (c i) -> p c i", i=64)
                            b.op("dve", lambda e: e.tensor_tensor(out=c3(KB[d]), in0=c3(KH[d]), in1=wlb, op=ALU.mult), r=[KH[d], WL[d]], w=[KB[d]])
                            b.op("dve", lambda e: e.scalar_tensor_tensor(out=c3(BB[d]), in0=c3(BH[d]), scalar=-1.0, in1=wlb,
                                                                         op0=ALU.mult, op1=ALU.mult), r=[BH[d], WL[d]], w=[BB[d]])
                            b.op("dve", lambda e: e.memset(P32[d].ap[:], 0.0), w=[P32[d]])

                        b.op("dve", lambda e: e.scalar_tensor_tensor(out=s4p.ap[:], in0=k32.ap[:], scalar=pvl(c0 + 6), in1=r_bf.ap[:],
                                                                     op0=ALU.mult, op1=ALU.mult), r=[k32, r_bf, kk_bf], w=[s4p])

                        def bon_cons(bi, ps):
                            b.op("dve", lambda e: e.tensor_tensor(out=bon.ap[:, blk(bi)], in0=ps.ap[:, :], in1=v_bf.ap[:, blk(bi)], op=ALU.mult),
                                 r=[ps, v_bf], w=[bon])
                        headsum(s4p, bon_cons, None)
                        if l == 0 and j == 0:
                            ck(5, KR[1].ap[:].rearrange("p c two i -> p (c two i)"))
                            ck(50, BB[0].ap[:, :])
                            ck(51, kk_bf.ap[:, :])
                            ck(52, KH[1].ap[:, :])

                        def stageA(i):
                            par = i % 4
                            ts_ = i % 2
                            cs = slice(i * 64, (i + 1) * 64)
                            levs = []

                            def L0():
                                SK = ""
                                for d in range(2):
                                    if "T" in SK:
                                        break
                                    pt = psA.next()
                                    for q, src in enumerate((vdir[d], KB[d], BB[d])):
                                        b.op("pe", lambda e, q=q, src=src, pt=pt: e.matmul(pt.ap[0:64, q * 128:(q + 1) * 128], lhsT=src.ap[:, cs], rhs=identb[:], start=True, stop=True),
                                             r=[src], w=[pt])
                                    tk = tok[d][par]
                                    for h_ in range(2):
                                        b.op("act", lambda e, pt=pt, tk=tk, h_=h_: e.activation(out=tk.ap[:, :, h_, h_, :], in_=pt.ap[0:64, 0:384].rearrange("s (q h c) -> s q h c", q=3, h=2)[:, :, h_, :],
                                                                                        func=AF.Copy), r=[pt], w=[tk])
                                for u in range(4):
                                    if "M" in SK:
                                        break
                                    if "H" in SK and u % 2 == 1:
                                        continue
                                    d, hh = u // 2, u % 2
                                    pr = slice(hh * 64, hh * 64 + 64)
                                    ua = UA[(u, par)]
                                    kr = KR[d].ap[pr, i].rearrange("p two i -> p (two i)")
                                    krm = KRm[(d, hh, par)]
                                    b.op("pool", lambda e, kr=kr, krm=krm, pr=pr: e.tensor_copy(out=krm.ap[pr, :], in_=kr), r=[KR[d]], w=[krm])
                                    p1 = psB.next()
                                    b.op("pe", lambda e, p1=p1, krm=krm, d=d: e.matmul(p1.ap[0:64, 0:128], lhsT=BH[d].ap[:, cs], rhs=krm.ap[:, :], start=True, stop=True),
                                         r=[BH[d], krm], w=[p1])
                                    b.op("pe", lambda e, p1=p1, krm=krm, d=d: e.matmul(p1.ap[0:64, 128:256], lhsT=KH[d].ap[:, cs], rhs=krm.ap[:, :], start=True, stop=True),
                                         r=[KH[d], krm], w=[p1])
                                    b.op("pe", lambda e, p1=p1, krm=krm, d=d: e.matmul(p1.ap[0:64, 256:320], lhsT=krm.ap[:, 0:64], rhs=BH[d].ap[:, cs], start=True, stop=True),
                                         r=[BH[d], krm], w=[p1])
                                    b.op("dve", lambda e, p1=p1, ua=ua: e.tensor_tensor(out=ua["sc"].ap, in0=p1.ap[0:64, 0:320], in1=cst[0:64, 256:576], op=ALU.mult), r=[p1], w=[ua["sc"]])
                            levs.append(L0)
                            for k in range(1, 6):
                                def Lk(k=k):
                                    for pa in range(2):
                                        pu = psB.next()
                                        dA, dT = TRD[(ts_, pa, k)]
                                        o_ = (k - 1) * 192
                                        for u2 in range(2):
                                            u = pa * 2 + u2
                                            ua = UA[(u, par)]
                                            cb = u2 * 256
                                            if k == 1:
                                                lo_p, up_p, rdeps = ua["sc3"], ua["up0"], [ua["sc"]]
                                                b.op("pe", lambda e: e.matmul(pu.ap[0:64, cb:cb + 64], lhsT=lo_p.ap, rhs=up_p.ap, start=True, stop=True), r=rdeps, w=[pu])
                                            else:
                                                pv_ = UT[(ts_, u)][k - 1]
                                                lo_p, up_p, rdeps = pv_["lo"], pv_["up"], [pv_["dA"], pv_["dT"]]
                                                b.op("pe", lambda e: e.matmul(pu.ap[0:64, cb:cb + 128], lhsT=lo_p.ap, rhs=pv_["ut"].ap, start=True, stop=True), r=rdeps, w=[pu])
                                            b.op("pe", lambda e: e.matmul(pu.ap[0:64, cb + 128:cb + 192], lhsT=up_p.ap, rhs=lo_p.ap, start=True, stop=True), r=rdeps, w=[pu])
                                        pv4 = pu.ap[0:64, :].rearrange("p (u a b) -> p u a b", u=2, a=4)
                                        dst4 = UTv[ts_][:, pa * 2:pa * 2 + 2, o_:o_ + 192].rearrange("p u (a b) -> p u a b", a=3)
                                        b.op("act", lambda e: e.activation(out=dst4[:, :, 0::2, :], in_=pv4[:, :, 0:3:2, :], func=AF.Copy), r=[pu], w=[dA])
                                        if k == 1:
                                            for u2 in range(2):
                                                ua = UA[(pa * 2 + u2, par)]
                                                b.op("dve", lambda e: e.tensor_tensor(out=UT[(ts_, pa * 2 + u2)][1]["tt"].ap, in0=ua["up0"].ap, in1=identb[0:64, 0:64], op=ALU.add), r=[ua["sc"]], w=[dT])
                                        else:
                                            dAp, dTp = TRD[(ts_, pa, k - 1)]
                                            prev_tt = UTv[ts_][:, pa * 2:pa * 2 + 2, o_ - 192 + 64:o_ - 192 + 128]
                                            b.op("dve", lambda e: e.tensor_tensor(out=dst4[:, :, 1, :], in0=pv4[:, :, 1, :], in1=prev_tt, op=ALU.add), r=[pu, dTp, dA], w=[dT])
                                levs.append(Lk)

                            def L6():
                                pz = psB.next()
                                for u in range(4):
                                    pv_ = UT[(ts_, u)][5]
                                    b.op("pe", lambda e: e.matmul(pz.ap[0:64, u * 64:(u + 1) * 64], lhsT=pv_["lo"].ap, rhs=pv_["tt"].ap, start=True, stop=True), r=[pv_["dA"], pv_["dT"]], w=[pz])
                                tt5 = UTv[ts_][:, :, 4 * 192 + 64:4 * 192 + 128]
                                dst = UAt[par].rearrange("p (u r) -> p u r", u=4)[:, :, 320:384]
                                b.op("dve", lambda e: e.tensor_tensor(out=dst, in0=pz.ap[0:64, 0:256].rearrange("p (u c) -> p u c", u=4), in1=tt5, op=ALU.add),
                                     r=[pz, TRD[(ts_, 0, 5)][1], TRD[(ts_, 1, 5)][1]], w=[UAD[par]["ttf"]])
                            levs.append(L6)
                            return levs

                        def stageB(i):
                            par = i % 4
                            ts_ = i % 2
                            cs = slice(i * 64, (i + 1) * 64)
                            seg = i // 4
                            levs = []

                            def L0():
                                if i % 4 == 0:
                                    for d in range(2):
                                        pi = pin.next()
                                        b.dma("sp", pi.ap[:], irw_d[l, d, seg, j], w=[pi])
                                        b.op("dve", lambda e, pi=pi, d=d: e.scalar_tensor_tensor(out=P32[d].ap[:], in0=P32[d].ap[:], scalar=carry[:, 0:1], in1=pi.ap[:],
                                                                                             op0=ALU.mult, op1=ALU.add), r=[P32[d], pi], w=[P32[d]])
                                        b.op("dve", lambda e, d=d: e.tensor_tensor(out=Pbd[d].ap[:].rearrange("p (h v) -> p h v", h=2), in0=P32[d].ap[:].rearrange("p (o v) -> p o v", o=1).to_broadcast([128, 2, 64]),
                                                                           in1=cst[:, 128:256].rearrange("p (h v) -> p h v", h=2), op=ALU.mult), r=[P32[d]], w=[Pbd[d]])
                                for d in range(2):
                                    px = psB.next()
                                    for hh in range(2):
                                        u = d * 2 + hh
                                        ua = UA[(u, par)]
                                        tk = tok[d][par]
                                        krm = KRm[(d, hh, par)]
                                        b.op("pe", lambda e, krm=krm, d=d, hh=hh, px=px: e.matmul(px.ap[0:64, hh * 64:(hh + 1) * 64], lhsT=krm.ap[:, 0:64], rhs=Pbd[d].ap[:, hh * 64:(hh + 1) * 64], start=True, stop=False),
                                             r=[krm, Pbd[d]], w=[px])
                                        b.op("pe", lambda e, ua=ua, tk=tk, hh=hh, px=px: e.matmul(px.ap[0:64, hh * 64:(hh + 1) * 64], lhsT=ua["sc2"].ap[:, 0:64], rhs=tk.ap[:, 0, hh, hh, :],
                                                                                             start=False, stop=True), r=[ua["sc2"], tk], w=[px])
                                    dst = UAt[par].rearrange("p (u r) -> p u r", u=4)[:, d * 2:d * 2 + 2, 384:448]
                                    b.op("act", lambda e, px=px, dst=dst: e.activation(out=dst, in_=px.ap[0:64, 0:128].rearrange("p (u c) -> p u c", u=2), func=AF.Copy), r=[px], w=[UAD[par]["xs"][d]])
                            levs.append(L0)

                            def L1():
                                for d in range(2):
                                    py = psB.next()
                                    for hh in range(2):
                                        ua = UA[(d * 2 + hh, par)]
                                        b.op("pe", lambda e, ua=ua, hh=hh, py=py: e.matmul(py.ap[0:64, hh * 64:(hh + 1) * 64], lhsT=ua["ttf"].ap, rhs=ua["xs"].ap, start=True, stop=True),
                                             r=[ua["ttf"], ua["xs"]], w=[py])
                                    dst = UAt[par].rearrange("p (d r) -> p d r", d=2)[:, d, 448:1152].rearrange("p (a c) -> p a c", c=64)[:, 0::10, :]
                                    b.op("dve", lambda e, py=py, dst=dst: e.tensor_copy(out=dst, in_=py.ap[0:64, 0:128].rearrange("p (h c) -> p h c", h=2)), r=[py], w=[UAD[par]["y"][d]])
                            levs.append(L1)

                            def L7():
                                pps, pos = [], []
                                for d in range(2):
                                    tk = tok[d][par]
                                    pp = psB.next()
                                    pps.append(pp)
                                    for hh in range(2):
                                        ua = UA[(d * 2 + hh, par)]
                                        U = ua["y"][6]
                                        b.op("pe", lambda e, pp=pp, tk=tk, hh=hh: e.matmul(pp.ap[:, 0:64], lhsT=tk.ap[:, 1, hh].rearrange("s h c -> s (h c)"), rhs=tk.ap[:, 0, hh, hh, :],
                                                                                      start=(hh == 0), stop=False), r=[tk], w=[pp])
                                        b.op("pe", lambda e, pp=pp, tk=tk, hh=hh, U=U: e.matmul(pp.ap[:, 0:64], lhsT=tk.ap[:, 2, hh].rearrange("s h c -> s (h c)"), rhs=U.ap, start=False, stop=(hh == 1)),
                                             r=[tk, U], w=[pp])
                                for d in range(2):
                                    tk = tok[d][par]
                                    po = psB.next()
                                    pos.append(po)
                                    b.op("pe", lambda e, po=po, d=d: e.matmul(po.ap[:, 0:64], lhsT=Pbd[d].ap[:, :], rhs=KR[d].ap[:, i, 1, :], start=True, stop=False),
                                         r=[Pbd[d], KR[d]], w=[po])
                                    for hh in range(2):
                                        ua = UA[(d * 2 + hh, par)]
                                        b.op("pe", lambda e, po=po, tk=tk, ua=ua, hh=hh: e.matmul(po.ap[:, 0:64], lhsT=tk.ap[:, 0, hh].rearrange("s h c -> s (h c)"), rhs=ua["sc2"].ap[:, 64:128],
                                                                                             start=False, stop=False), r=[tk, ua["sc2"]], w=[po])
                                        b.op("pe", lambda e, po=po, ua=ua, hh=hh: e.matmul(po.ap[:, 0:64], lhsT=ua["upad"].ap, rhs=ua["sc1"].ap[:, 64:128], start=False, stop=(hh == 1)),
                                             r=[ua["upad"], ua["sc1"]], w=[po])
                                for d in range(2):
                                    pp = pps[d]
                                    b.op("dve", lambda e, pp=pp, d=d: e.scalar_tensor_tensor(out=P32[d].ap[:], in0=P32[d].ap[:], scalar=WL[d].ap[:, i:i + 1], in1=pp.ap[:, 0:64],
                                                                                         op0=ALU.mult, op1=ALU.add), r=[pp, P32[d], WL[d]], w=[P32[d]])
                                    b.op("dve", lambda e, d=d: e.tensor_tensor(out=Pbd[d].ap[:].rearrange("p (h v) -> p h v", h=2), in0=P32[d].ap[:].rearrange("p (o v) -> p o v", o=1).to_broadcast([128, 2, 64]),
                                                                           in1=cst[:, 128:256].rearrange("p (h v) -> p h v", h=2), op=ALU.mult), r=[P32[d]], w=[Pbd[d]])
                                for d in range(2):
                                    po = pos[d]
                                    nat = (31 - i) if d else i
                                    oc = o32[:, nat * 64:(nat + 1) * 64]
                                    ocv = oc[:, ::-1] if d else oc
                                    b.op("dve", lambda e, po=po, ocv=ocv: e.tensor_tensor(out=ocv, in0=po.ap[:, 0:64], in1=ocv, op=ALU.add), r=[po, d_o[nat]], w=[d_o[nat]])
                                    if i % 4 == 3:
                                        po_ = pout.next()
                                        b.op("act", lambda e, d=d, po_=po_: e.activation(out=po_.ap[:], in_=P32[d].ap[:], func=AF.Copy), r=[P32[d]], w=[po_])
                                        b.dma("sp", frw_d[l, d, seg, j], po_.ap[:], r=[po_])
                            levs.append(L7)
                            return levs

                        b.barrier()
                        b.op("dve", lambda e: e.memset(sgb[0:64, 0:3840], 0.0), w=[d0])
                        b.op("dve", lambda e: e.memset(bsb[0:64, 0:3840], 0.0), w=[d0])
                        b.op("dve", lambda e: e.memset(bbb[:, 0:1024], 0.0), w=[d0])
                        b.barrier()
                        A0, A1 = stageA(0), stageA(1)
                        for lev in range(7):
                            A0[lev]()
                            A1[lev]()
                        for i in range(0, 32, 2):
                            An1 = stageA(i + 2) if i + 2 < 32 else []
                            An2 = stageA(i + 3) if i + 3 < 32 else []
                            B1, B2 = stageB(i), stageB(i + 1)
                            for lev in range(7):
                                if lev < 3:
                                    B1[lev]()
                                elif lev < 6:
                                    B2[lev - 3]()
                                if lev < len(An1):
                                    An1[lev]()
                                if lev < len(An2):
                                    An2[lev]()
                        b.barrier()

                        if l == 0 and j == 0:
                            ck(6, o32[:, :])
                        d_all = d_o
                        b.op("act", lambda e: e.activation(out=s4p.ap[:], in_=o32[:], func=AF.Copy), r=d_all, w=[s4p])

                        def mu_cons(bi, ps):
                            b.op("dve", lambda e: e.scalar_tensor_tensor(out=o32[:, blk(bi)], in0=ps.ap[:, :], scalar=-1.0 / 64, in1=o32[:, blk(bi)],
                                                                         op0=ALU.mult, op1=ALU.add), r=[ps] + d_all, w=[d_all[0]])
                        headsum(s4p, mu_cons, None)
                        b.op("act", lambda e: e.activation(out=s4p.ap[:], in_=o32[:], func=AF.Square), r=[d_all[0]], w=[s4p])

                        def var_cons(bi, ps):
                            b.op("act", lambda e: e.activation(out=t1.ap[:, blk(bi)], in_=ps.ap[:, :], func=AF.Ln, scale=1.0 / 64, bias=64e-5), r=[ps], w=[t1])
                            b.op("act", lambda e: e.activation(out=t1.ap[:, blk(bi)], in_=t1.ap[:, blk(bi)], func=AF.Exp, scale=-0.5), r=[t1], w=[t1])
                            b.op("dve", lambda e: e.tensor_tensor(out=o32[:, blk(bi)], in0=o32[:, blk(bi)], in1=t1.ap[:, blk(bi)], op=ALU.mult), r=[t1, d_all[0]], w=[d_all[0]])
                            b.op("dve", lambda e: e.tensor_scalar(out=o32[:, blk(bi)], in0=o32[:, blk(bi)], scalar1=pvl(c0 + 7), scalar2=pvl(c0 + 8), op0=ALU.mult, op1=ALU.add),
                                 r=[d_all[0]], w=[d_all[0]])
                            b.op("dve", lambda e: e.tensor_tensor(out=o32[:, blk(bi)], in0=o32[:, blk(bi)], in1=bon.ap[:, blk(bi)], op=ALU.add), r=[bon, d_all[0]], w=[d_all[0]])
                            pg = psA.next()
                            b.op("pe", lambda e, pg=pg: e.matmul(pg.ap[:, :], lhsT=gup.ap[:, j * 128:(j + 1) * 128], rhs=lora_g.ap[:, blk(bi)], start=True, stop=True),
                                 r=[gup, lora_g], w=[pg])
                            b.op("dve", lambda e, pg=pg: e.tensor_tensor(out=mst.ap[:, blk(bi)], in0=pg.ap[:, :], in1=o32[:, blk(bi)], op=ALU.mult), r=[pg, d_all[0]], w=[mst])
                        headsum(s4p, var_cons, None)
                        b.dma("sp", mixscr[j], mst.ap[:], r=[mst], w=[d_mix[j]])
                        if l == 0 and j == 0:
                            ck(7, mst.ap[:, :])
                b.barrier()

                with contextlib.ExitStack() as lp:
                    mst = tl("mst", [128, T], BF16, lp)
                    xpad = tl("xpad", [128, 8, 259], F32, lp)
                    u32 = tl("u32", [128, T], F32, lp)
                    u_bf = tl("u_bf", [128, T], BF16, lp)
                    gb32 = tl("gb32", [128, T], F32, lp)
                    gt = tl("gt", [128, T], F32, lp)
                    gel = tl("gel", [128, T], F32, lp)
                    rg_ = [tl("rg%d" % d_, [128, T], F32, lp) for d_ in range(2)]
                    ig_ = [tl("ig%d" % d_, [128, T], F32, lp) for d_ in range(2)]
                    a32_ = [tl("a32%d" % d_, [128, T], F32, lp) for d_ in range(2)]
                    m32_ = [tl("m32%d" % d_, [128, T], F32, lp) for d_ in range(2)]
                    hh_ = [tl("hf", [128, T], F32, lp), tl("hb", [128, T], F32, lp)]
                    wab = [[tl("wab%d%d" % (d, q), [128, 128], BF16, lp) for q in range(2)] for d in range(2)]
                    h0 = rot("h0", 4, [128, 1], F32, lp)
                    for j in range(2):
                        c0 = PV_LRU + j * 11
                        for d in range(2):
                            for q, src in enumerate((lwa_d, lwx_d)):
                                w_ = wab[d][q]
                                b.op("dve", lambda e, w_=w_: e.memset(w_.ap[:], 0.0), w=[w_])
                                for hb in range(2):
                                    b.dma("pool", w_.ap[hb * 64:(hb + 1) * 64, hb * 64:(hb + 1) * 64], src[l, d, j * 2 + hb], w=[w_])
                        b.op("dve", lambda e: e.memset(xpad.ap[:], 0.0), w=[xpad])

                        def xb_evac(bi, ps):
                            b.op("act", lambda e: e.activation(out=xpad.ap[:, bi * 2:bi * 2 + 2, 2:258], in_=ps.ap[:, :].rearrange("p (s t) -> p s t", s=2), func=AF.Copy),
                                 r=[ps], w=[xpad])
                        proj(l, 1920 + j * 128, 128, xb_evac)
                        proj(l, 2176 + j * 128, 128, act_evac(gb32))
                        b.op("dve", lambda e: e.tensor_scalar(out=xpad.ap[:, 1:8, 0:2], in0=xpad.ap[:, 0:7, 256:258], scalar1=carry[:, 0:1], scalar2=None, op0=ALU.mult),
                             r=[xpad], w=[xpad])
                        b.op("dve", lambda e: e.tensor_scalar(out=xpad.ap[:, 0:7, 258:259], in0=xpad.ap[:, 1:8, 2:3], scalar1=carry[:, 0:1], scalar2=None, op0=ALU.mult),
                             r=[xpad], w=[xpad])
                        u3 = u32.ap[:].rearrange("p (s t) -> p s t", s=8)
                        b.op("dve", lambda e: e.tensor_scalar(out=u3, in0=xpad.ap[:, :, 0:256], scalar1=pvl(c0), scalar2=pvl(c0 + 4), op0=ALU.mult, op1=ALU.add),
                             r=[xpad], w=[u32])
                        for k in range(1, 4):
                            b.op("dve", lambda e, k=k: e.scalar_tensor_tensor(out=u3, in0=xpad.ap[:, :, k:k + 256], scalar=pvl(c0 + k), in1=u3, op0=ALU.mult, op1=ALU.add),
                                 r=[xpad, u32], w=[u32])
                        b.op("act", lambda e: e.activation(out=u_bf.ap[:], in_=u32.ap[:], func=AF.Copy), r=[u32], w=[u_bf])
                        b.op("act", lambda e: e.activation(out=gt.ap[:], in_=gb32.ap[:], func=AF.Square), r=[gb32], w=[gt])
                        b.op("dve", lambda e: e.tensor_scalar(out=gt.ap[:], in0=gt.ap[:], scalar1=0.044715, scalar2=1.0, op0=ALU.mult, op1=ALU.add), r=[gt], w=[gt])
                        b.op("dve", lambda e: e.tensor_tensor(out=gt.ap[:], in0=gt.ap[:], in1=gb32.ap[:], op=ALU.mult), r=[gt, gb32], w=[gt])
                        b.op("act", lambda e: e.activation(out=gt.ap[:], in_=gt.ap[:], func=AF.Sigmoid, scale=1.5957691216057308), r=[gt], w=[gt])
                        b.op("dve", lambda e: e.tensor_tensor(out=gel.ap[:], in0=gt.ap[:], in1=gb32.ap[:], op=ALU.mult), r=[gt, gb32], w=[gel])
                        for d in range(2):
                            rg, ig, a32, m32 = rg_[d], ig_[d], a32_[d], m32_[d]
                            for bi in range(4):
                                ps = psA.next()
                                b.op("pe", lambda e, ps=ps: e.matmul(ps.ap[:, :], lhsT=wab[d][0].ap[:], rhs=u_bf.ap[:, blk(bi)], start=True, stop=True), r=[wab[d][0], u_bf], w=[ps])
                                b.op("act", lambda e, ps=ps: e.activation(out=rg.ap[:, blk(bi)], in_=ps.ap[:, :], func=AF.Sigmoid, bias=pvl(c0 + 5 + d)), r=[ps], w=[rg])
                                ps2 = psA.next()
                                b.op("pe", lambda e, ps2=ps2: e.matmul(ps2.ap[:, :], lhsT=wab[d][1].ap[:], rhs=u_bf.ap[:, blk(bi)], start=True, stop=True), r=[wab[d][1], u_bf], w=[ps2])
                                b.op("act", lambda e, ps2=ps2: e.activation(out=ig.ap[:, blk(bi)], in_=ps2.ap[:, :], func=AF.Sigmoid, bias=pvl(c0 + 7 + d)), r=[ps2], w=[ig])
                            sc = der[:, l, 4 + j * 2 + d:5 + j * 2 + d]
                            sc2 = der[:, l, 8 + j * 2 + d:9 + j * 2 + d]
                            b.op("act", lambda e: e.activation(out=a32.ap[:], in_=rg.ap[:], func=AF.Exp, scale=sc), r=[rg], w=[a32])
                            b.op("act", lambda e: e.activation(out=m32.ap[:], in_=rg.ap[:], func=AF.Exp, scale=sc2), r=[rg], w=[m32])
                            b.op("act", lambda e: e.activation(out=m32.ap[:], in_=m32.ap[:], func=AF.Sqrt, scale=-1.0, bias=1.0), r=[m32], w=[m32])
                            b.op("dve", lambda e: e.tensor_tensor(out=m32.ap[:], in0=m32.ap[:], in1=ig.ap[:], op=ALU.mult), r=[m32, ig], w=[m32])
                            b.op("dve", lambda e: e.tensor_tensor(out=m32.ap[:], in0=m32.ap[:], in1=u32.ap[:], op=ALU.mult), r=[m32, u32], w=[m32])
                            hd = hh_[d]
                            prev = None
                            for s in range(8):
                                nat = (7 - s) if d else s
                                sl = slice(nat * 256, (nat + 1) * 256)
                                hi = h0.next()
                                icol = ilru[:, l, j, s * 2 + d:s * 2 + d + 1]
                                if prev is None:
                                    b.op("dve", lambda e, hi=hi, icol=icol: e.tensor_copy(out=hi.ap[:], in_=icol), w=[hi])
                                else:
                                    b.op("dve", lambda e, hi=hi, icol=icol, prev=prev: e.scalar_tensor_tensor(out=hi.ap[:], in0=prev, scalar=carry[:, 0:1], in1=icol,
                                                                                                     op0=ALU.mult, op1=ALU.add), r=[hd], w=[hi])
                                b.op("dve", lambda e, hi=hi, sl=sl: e.tensor_tensor_scan(out=rv(hd.ap[:, sl], d), data0=rv(a32.ap[:, sl], d), data1=rv(m32.ap[:, sl], d),
                                                                                    initial=hi.ap[:, 0:1], op0=ALU.mult, op1=ALU.add), r=[a32, m32, hi], w=[hd])
                                last = (nat * 256) if d else (nat * 256 + 255)
                                prev = hd.ap[:, last:last + 1]
                                b.op("act", lambda e, prev=prev, s=s: e.activation(out=flru[:, l, j, s * 2 + d:s * 2 + d + 1], in_=prev, func=AF.Copy), r=[hd], w=[d_flru])
                        b.op("dve", lambda e: e.tensor_tensor(out=hh_[0].ap[:], in0=hh_[0].ap[:], in1=hh_[1].ap[:], op=ALU.add), r=[hh_[0], hh_[1]], w=[hh_[0]])
                        b.op("dve", lambda e: e.tensor_tensor(out=mst.ap[:], in0=hh_[0].ap[:], in1=gel.ap[:], op=ALU.mult), r=[hh_[0], gel], w=[mst])
                        b.dma("sp", mixscr[4 + j], mst.ap[:], r=[mst], w=[d_mix[4 + j]])
                        if l == 0 and j == 0:
                            ck(9, mst.ap[:, :])
                b.barrier()

                with contextlib.ExitStack() as hp:
                    rst32 = sb("rst32", [128, T], BF16, hp)
                    b.op("dve", lambda e: e.memset(rst32[:], 1.0), w=[d0])
                    b.op("dve", lambda e: e.memset(rst32[:].rearrange("p (c i) -> p c i", i=64)[:, :, 0:1], 0.0), w=[d0])
                    mst = tl("mst", [128, T], BF16, hp)
                    qs = tl("qs", [128, T], BF16, hp)
                    fr = tl("fr", [128, T], F32, hp)
                    v_bf = tl("hv_bf", [128, T], BF16, hp)
                    ogs = tl("ogs", [128, T], BF16, hp)
                    f32_ = tl("f32_", [128, T], F32, hp)
                    lf = tl("lf", [128, T], F32, hp)
                    kq = tl("kq", [128, T], BF16, hp)
                    bs32 = tl("hbs32", [128, T], F32, hp)
                    t1 = tl("ht1", [128, T], F32, hp)
                    E = tl("hE", [128, T], BF16, hp)
                    vdir = [v_bf, tl("hvrev", [128, T], BF16, hp)]
                    QT = [tl("QT%d" % d, [128, T], BF16, hp) for d in range(2)]
                    KH = [tl("hKH%d" % d, [128, T], BF16, hp) for d in range(2)]
                    KB = [tl("hKB%d" % d, [128, T], BF16, hp) for d in range(2)]
                    WL = [tl("hWL%d" % d, [128, 32], F32, hp) for d in range(2)]
                    WLm = tl("hWLm", [128, 32], F32, hp)
                    QTa = [tl("QTa%d" % d, [128, T], BF16, hp) for d in range(2)]
                    o32 = sb("ho32", [128, T], F32, hp)
                    s4 = tl("hs4", [128, T], BF16, hp)
                    P32 = [tl("hP32%d" % d, [128, 64], F32, hp) for d in range(2)]
                    Pbf = [tl("hPbf%d" % d, [128, 64], BF16, hp) for d in range(2)]
                    pin = rot("hpin", 4, [128, 64], F32, hp)
                    pout = rot("hpout", 4, [128, 64], F32, hp)
                    tok = rot("htok", 6, [64, 2, 2, 2, 64], BF16, hp)
                    Pbd = [tl("hPbd%d" % d, [128, 128], BF16, hp) for d in range(2)]
                    for t_ in tok.tl + Pbd:
                        b.op("dve", lambda e, t_=t_: e.memset(t_.ap[:], 0.0), w=[t_])
                    scb_ = rot("hsc", 6, [64, 128], BF16, hp)
                    hclamp = rot("hclamp", 3, [64, 128], F32, hp)
                    qbd = rot("qbd", 6, [128, 128], BF16, hp)
                    for t_ in qbd.tl:
                        b.op("dve", lambda e, t_=t_: e.memset(t_.ap[:], 0.0), w=[t_])
                    d_o = [Dep() for _ in range(32)]
                    for j in range(2):
                        c0 = PV_HG + j * 5
                        proj(l, 2432 + j * 128, 128, act_evac(qs, AF.Silu))
                        proj(l, 3200 + j * 128, 128, act_evac(v_bf))
                        proj(l, 3456 + j * 128, 128, act_evac(ogs, AF.Silu))
                        b.op("dve", lambda e: e.memset(o32[:], 0.0), w=d_o)
                        b.op("act", lambda e: e.activation(out=vdir[1].ap[:], in_=v_bf.ap[:, ::-1], func=AF.Copy), r=[v_bf], w=[vdir[1]])
                        for d in range(2):
                            proj(l, 2688 + d * 256 + j * 128, 128, act_evac(fr, AF.Sigmoid))
                            lb = der[:, l, 12 + j * 2 + d:13 + j * 2 + d]
                            oml = der[:, l, 16 + j * 2 + d:17 + j * 2 + d]
                            b.op("dve", lambda e: e.tensor_scalar(out=f32_.ap[:], in0=fr.ap[:], scalar1=oml, scalar2=lb, op0=ALU.mult, op1=ALU.add), r=[fr], w=[f32_])
                            b.op("act", lambda e: e.activation(out=lf.ap[:], in_=f32_.ap[:], func=AF.Ln), r=[f32_], w=[lf])
                            b.op("dve", lambda e: e.tensor_scalar(out=kq.ap[:], in0=f32_.ap[:], scalar1=-1.0, scalar2=1.0, op0=ALU.mult, op1=ALU.add), r=[f32_], w=[kq])
                            b.op("dve", lambda e: e.tensor_tensor_scan(out=bs32.ap[:], data0=rst32[:], data1=rv(lf.ap, d), initial=0.0, op0=ALU.mult, op1=ALU.add),
                                 r=[lf], w=[bs32])
                            bs3 = bs32.ap[:].rearrange("p (c i) -> p c i", i=64)
                            c3 = lambda t_: t_.ap[:].rearrange("p (c i) -> p c i", i=64)
                            b.op("act", lambda e: e.activation(out=E.ap[:], in_=bs32.ap[:], func=AF.Exp), r=[bs32], w=[E])
                            b.op("dve", lambda e: e.tensor_tensor(out=QTa[d].ap[:], in0=rv(qs.ap, d), in1=E.ap[:], op=ALU.mult), r=[qs, E], w=[QTa[d]])
                            b.op("act", lambda e: e.activation(out=WL[d].ap[:], in_=bs3[:, :, 63], func=AF.Exp), r=[bs32], w=[WL[d]])
                            b.op("dve", lambda e: e.tensor_tensor(out=WLm.ap[:], in0=bs3[:, :, 63], in1=bs3[:, :, 31], op=ALU.subtract), r=[bs32], w=[WLm])
                            b.op("act", lambda e: e.activation(out=WLm.ap[:], in_=WLm.ap[:], func=AF.Exp), r=[WLm], w=[WLm])
                            b.op("dve", lambda e: e.tensor_tensor(out=c3(t1), in0=bs3, in1=bs3[:, :, 31:32].to_broadcast([128, 32, 64]), op=ALU.subtract), r=[bs32], w=[t1])
                            b.op("act", lambda e: e.activation(out=E.ap[:], in_=t1.ap[:], func=AF.Exp), r=[t1, QTa[d]], w=[E])
                            b.op("dve", lambda e: e.tensor_tensor(out=QT[d].ap[:], in0=rv(qs.ap, d), in1=E.ap[:], op=ALU.mult), r=[qs, E], w=[QT[d]])
                            b.op("act", lambda e: e.activation(out=E.ap[:], in_=t1.ap[:], func=AF.Exp, scale=-1.0), r=[t1, QT[d]], w=[E])
                            b.op("dve", lambda e: e.tensor_tensor(out=KH[d].ap[:], in0=rv(kq.ap, d), in1=E.ap[:], op=ALU.mult), r=[kq, E], w=[KH[d]])
                            wlb = WLm.ap[:].rearrange("p (c o) -> p c o", o=1).to_broadcast([128, 32, 64])
                            b.op("dve", lambda e: e.tensor_tensor(out=c3(KB[d]), in0=c3(KH[d]), in1=wlb, op=ALU.mult), r=[KH[d], WLm], w=[KB[d]])
                            b.op("dve", lambda e: e.memset(P32[d].ap[:], 0.0), w=[P32[d]])
                        hA = {}

                        def hgA(i, d):
                            cs = slice(i * 64, (i + 1) * 64)
                            qb = qbd.next()
                            for hh in range(2):
                                pr = slice(hh * 64, hh * 64 + 64)
                                b.op("pool", lambda e, hh=hh, pr=pr: e.tensor_copy(out=qb.ap[pr, hh * 64:(hh + 1) * 64], in_=QT[d].ap[pr, cs]), r=[QT[d]], w=[qb])
                            p1 = psB.next()
                            b.op("pe", lambda e: e.matmul(p1.ap[0:64, 0:128], lhsT=KH[d].ap[:, cs], rhs=qb.ap[:, :], start=True, stop=True), r=[KH[d], qb], w=[p1])
                            sc = scb_.next()
                            ctm = hclamp.next()
                            b.op("dve", lambda e: e.tensor_scalar(out=ctm.ap[:], in0=p1.ap[0:64, 0:128], scalar1=1e30, scalar2=-1e30, op0=ALU.min, op1=ALU.max), r=[p1], w=[ctm])
                            b.op("dve", lambda e: e.tensor_tensor(out=sc.ap[:].rearrange("p (h t) -> p h t", h=2), in0=ctm.ap[:].rearrange("p (h t) -> p h t", h=2),
                                                                  in1=cst[0:64, 448:512].rearrange("p (o t) -> p o t", o=1).to_broadcast([64, 2, 64]), op=ALU.mult), r=[ctm], w=[sc])
                            pt = psA.next()
                            for q, src in enumerate((vdir[d], KB[d])):
                                b.op("pe", lambda e, q=q, src=src: e.matmul(pt.ap[0:64, q * 128:(q + 1) * 128], lhsT=src.ap[:, cs], rhs=identb[:], start=True, stop=True), r=[src], w=[pt])
                            tk = tok.next()
                            for h_ in range(2):
                                b.op("act", lambda e, h_=h_: e.activation(out=tk.ap[:, :, h_, h_, :], in_=pt.ap[0:64, 0:256].rearrange("s (q h c) -> s q h c", q=2, h=2)[:, :, h_, :], func=AF.Copy),
                                     r=[pt], w=[tk])
                            hA[(i, d)] = (sc, tk)

                        def hgB(i, d):
                            cs = slice(i * 64, (i + 1) * 64)
                            seg = i // 4
                            sc, tk = hA.pop((i, d))
                            if i % 4 == 0:
                                pi = pin.next()
                                b.dma("sp", pi.ap[:], ihg_d[l, d, seg, j], w=[pi])
                                b.op("dve", lambda e: e.scalar_tensor_tensor(out=P32[d].ap[:], in0=P32[d].ap[:], scalar=carry[:, 0:1], in1=pi.ap[:],
                                                                             op0=ALU.mult, op1=ALU.add), r=[P32[d], pi], w=[P32[d]])
                                b.op("dve", lambda e: e.tensor_tensor(out=Pbd[d].ap[:].rearrange("p (h v) -> p h v", h=2), in0=P32[d].ap[:].rearrange("p (o v) -> p o v", o=1).to_broadcast([128, 2, 64]),
                                                                      in1=cst[:, 128:256].rearrange("p (h v) -> p h v", h=2), op=ALU.mult), r=[P32[d]], w=[Pbd[d]])
                            po = psB.next()
                            pp = Tl(po.ap[:, 64:128])
                            pp.d = po.d
                            pq = psB.next()
                            pp = Tl(pq.ap[:, 64:128])
                            pp.d = pq.d
                            for hh in range(2):
                                b.op("pe", lambda e, hh=hh: e.matmul(pp.ap[:, 0:64], lhsT=tk.ap[:, 1, hh].rearrange("s h c -> s (h c)"), rhs=tk.ap[:, 0, hh, hh, :], start=(hh == 0), stop=(hh == 1)),
                                     r=[tk], w=[pp])
                            b.op("pe", lambda e: e.matmul(po.ap[:, 0:64], lhsT=Pbd[d].ap[:, :], rhs=QTa[d].ap[:, cs], start=True, stop=False), r=[Pbd[d], QTa[d]], w=[po])
                            for hh in range(2):
                                b.op("pe", lambda e, hh=hh: e.matmul(po.ap[:, 0:64], lhsT=tk.ap[:, 0, hh].rearrange("s h c -> s (h c)"), rhs=sc.ap[:, hh * 64:(hh + 1) * 64], start=False, stop=(hh == 1)),
                                     r=[tk, sc], w=[po])
                            nat = (31 - i) if d else i
                            oc = o32[:, nat * 64:(nat + 1) * 64]
                            ocv = oc[:, ::-1] if d else oc
                            b.op("dve", lambda e: e.scalar_tensor_tensor(out=P32[d].ap[:], in0=P32[d].ap[:], scalar=WL[d].ap[:, i:i + 1], in1=pp.ap[:, 0:64],
                                                                         op0=ALU.mult, op1=ALU.add), r=[pp, P32[d], WL[d]], w=[P32[d]])
                            b.op("dve", lambda e: e.tensor_tensor(out=ocv, in0=po.ap[:, 0:64], in1=ocv, op=ALU.add), r=[po, d_o[nat]], w=[d_o[nat]])
                            b.op("dve", lambda e: e.tensor_tensor(out=Pbd[d].ap[:].rearrange("p (h v) -> p h v", h=2), in0=P32[d].ap[:].rearrange("p (o v) -> p o v", o=1).to_broadcast([128, 2, 64]),
                                                                      in1=cst[:, 128:256].rearrange("p (h v) -> p h v", h=2), op=ALU.mult), r=[P32[d]], w=[Pbd[d]])
                            if i % 4 == 3:
                                po_ = pout.next()
                                b.op("act", lambda e: e.activation(out=po_.ap[:], in_=P32[d].ap[:], func=AF.Copy), r=[P32[d]], w=[po_])
                                b.dma("sp", fhg_d[l, d, seg, j], po_.ap[:], r=[po_])

                        hgA(0, 0)
                        hgA(0, 1)
                        for i in range(32):
                            if i < 31:
                                hgA(i + 1, 0)
                                hgA(i + 1, 1)
                            hgB(i, 0)
                            hgB(i, 1)
                        b.op("act", lambda e: e.activation(out=s4.ap[:], in_=o32[:], func=AF.Square), r=d_o, w=[s4])

                        def hv_cons(bi, ps):
                            b.op("act", lambda e: e.activation(out=t1.ap[:, blk(bi)], in_=ps.ap[:, :], func=AF.Ln, scale=1.0 / 64, bias=1e-6), r=[ps], w=[t1])
                            b.op("act", lambda e: e.activation(out=t1.ap[:, blk(bi)], in_=t1.ap[:, blk(bi)], func=AF.Exp, scale=-0.5), r=[t1], w=[t1])
                            b.op("dve", lambda e: e.scalar_tensor_tensor(out=t1.ap[:, blk(bi)], in0=t1.ap[:, blk(bi)], scalar=pvl(c0 + 4), in1=o32[:, blk(bi)],
                                                                         op0=ALU.mult, op1=ALU.mult), r=[t1] + d_o, w=[t1])
                            b.op("dve", lambda e: e.tensor_tensor(out=mst.ap[:, blk(bi)], in0=t1.ap[:, blk(bi)], in1=ogs.ap[:, blk(bi)], op=ALU.mult), r=[t1, ogs], w=[mst])
                        headsum(s4, hv_cons, None)
                        b.dma("sp", mixscr[6 + j], mst.ap[:], r=[mst], w=[d_mix[6 + j]])
                        if l == 0 and j == 0:
                            ck(10, mst.ap[:, :])
            b.barrier()

            xs = contextlib.ExitStack()
            xTt = sb("xT", [128, 8, T], F32, xs)
            ms = contextlib.ExitStack()
            mixT = sb("mixT", [128, 8, T], BF16, ms)
            for ft in range(8):
                b.dma("sp", mixT[:, ft, :], mixscr[ft], r=[d_mix[ft]], w=[d_mix[ft]])
            for bi in range(4):
                b.dma("sp", xTt[:, :, blk(bi)], xscr[:, :, blk(bi)], w=[d_x[bi]])
            wts = [load_w(w_out_d[l], slice(dj * 128, (dj + 1) * 128)) for dj in range(3)]
            for dj in range(8):
                wt = wts[dj % 3]
                for bi in range(4):
                    ps = psA.next()
                    for ft in range(8):
                        b.op("pe", lambda e, ft=ft, ps=ps, wt=wt: e.matmul(ps.ap[:, :], lhsT=wt.ap[:, ft, :], rhs=mixT[:, ft, blk(bi)], start=(ft == 0), stop=(ft == 7)),
                             r=[wt] + d_mix, w=[ps])
                    b.op("dve", lambda e, ps=ps: e.scalar_tensor_tensor(out=xTt[:, dj, blk(bi)], in0=ps.ap[:, :], scalar=modt[:, l, 16 + dj:17 + dj], in1=xTt[:, dj, blk(bi)],
                                                                    op0=ALU.mult, op1=ALU.add), r=[ps, d_x[bi][dj]], w=[d_x[bi][dj]])
                if dj + 3 < 8:
                    wts[dj % 3] = load_w(w_out_d[l], slice((dj + 3) * 128, (dj + 4) * 128))
            b.barrier()
            if l == 0:
                ck(11, xTt[:, 0, :])
            ms.close()
            fpw = contextlib.ExitStack()
            wgt = rot("wgt", 2, [128, 8, 512], BF16, fpw)
            wut = rot("wut", 2, [128, 8, 512], BF16, fpw)
            wdt = rot("wdt", 3, [128, 4, D], BF16, fpw)

            def ffn_wload(c0g, ng):
                wg_, wu_, wd_ = wgt.next(), wut.next(), wdt.next()
                cs = slice(c0g * 128, (c0g + ng) * 128)
                b.dma("pool", wg_.ap[:, :, 0:ng * 128], wg_d[l].rearrange("(j p) c -> p j c", p=128)[:, :, cs], w=[wg_])
                b.dma("pool", wu_.ap[:, :, 0:ng * 128], wu_d[l].rearrange("(j p) c -> p j c", p=128)[:, :, cs], w=[wu_])
                b.dma("pool", wd_.ap[:, 0:ng, :], wd_d[l, c0g * 128:(c0g + ng) * 128, :].rearrange("(c p) d -> p c d", p=128), w=[wd_])
                return wg_, wu_, wd_
            pre_w = ffn_wload(0, 4)
            with contextlib.ExitStack() as ph:
                norm_phase(xTt, ph, None, None, to_hT(l, 1))
            b.barrier()
            with contextlib.ExitStack() as fp:
                gpad = rot("gpad", 3, [128, 8, 66], F32, fp)
                acc = rot("acc", 3, [128, 512], F32, fp)
                sgt = rot("sgt", 3, [128, 512], F32, fp)
                actT = rot("actT", 3, [128, 4, 512], BF16, fp)
                for g_ in gpad.tl:
                    b.op("dve", lambda e, g_=g_: e.memset(g_.ap[:], 0.0), w=[g_])
                groups = [(0, 4), (4, 4), (8, 4), (12, 4), (16, 4), (20, 2)]
                pend = []

                def down_proj(wd_, at, ng, bi):
                    for dj in range(8):
                        ps = psA.next()
                        for ci in range(ng):
                            b.op("pe", lambda e, ci=ci, ps=ps: e.matmul(ps.ap[:, :], lhsT=wd_.ap[:, ci, dj * 128:(dj + 1) * 128], rhs=at.ap[:, ci, :], start=(ci == 0), stop=(ci == ng - 1)),
                                 r=[wd_, at], w=[ps])
                        b.op("dve", lambda e, ps=ps: e.scalar_tensor_tensor(out=xTt[:, dj, blk(bi)], in0=ps.ap[:, :], scalar=modt[:, l, 40 + dj:41 + dj], in1=xTt[:, dj, blk(bi)],
                                                                        op0=ALU.mult, op1=ALU.add), r=[ps, d_x[bi][dj]], w=[d_x[bi][dj]])

                for (c0g, ng) in groups:
                    wg_, wu_, wd_ = pre_w if c0g == 0 else ffn_wload(c0g, ng)
                    for bi in range(4):
                        at = actT.next()
                        for ci in range(ng):
                            c = c0g + ci
                            pg, pu = psA.next(), psA.next()
                            for jj in range(8):
                                b.op("pe", lambda e, jj=jj, pg=pg: e.matmul(pg.ap[:, :], lhsT=wg_.ap[:, jj, ci * 128:(ci + 1) * 128], rhs=hT[:, jj, blk(bi)], start=(jj == 0), stop=(jj == 7)),
                                     r=[wg_, d_hT[bi]], w=[pg])
                            for jj in range(8):
                                b.op("pe", lambda e, jj=jj, pu=pu: e.matmul(pu.ap[:, :], lhsT=wu_.ap[:, jj, ci * 128:(ci + 1) * 128], rhs=hT[:, jj, blk(bi)], start=(jj == 0), stop=(jj == 7)),
                                     r=[wu_, d_hT[bi]], w=[pu])
                            gp = gpad.next()
                            b.op("act", lambda e, gp=gp, pg=pg: e.activation(out=gp.ap[:, :, 1:65], in_=pg.ap[:, :].rearrange("p (r t) -> p r t", r=8), func=AF.Copy), r=[pg], w=[gp])
                            b.op("dve", lambda e, gp=gp: e.tensor_tensor(out=gp.ap[:, 1:8, 0], in0=gp.ap[:, 0:7, 64], in1=fmask[:, :], op=ALU.mult), r=[gp], w=[gp])
                            b.op("dve", lambda e, gp=gp: e.tensor_tensor(out=gp.ap[:, 0:7, 65], in0=gp.ap[:, 1:8, 1], in1=fmask[:, :], op=ALU.mult), r=[gp], w=[gp])
                            ac = acc.next()
                            a3 = ac.ap[:].rearrange("p (r t) -> p r t", r=8)
                            b.op("act", lambda e, gp=gp, a3=a3: e.activation(out=a3, in_=gp.ap[:, :, 0:64], func=AF.Identity, scale=pvl(PV_FW + c)), r=[gp], w=[ac])
                            b.op("dve", lambda e, gp=gp, a3=a3: e.scalar_tensor_tensor(out=a3, in0=gp.ap[:, :, 1:65], scalar=pvl(PV_FW + 22 + c), in1=a3, op0=ALU.mult, op1=ALU.add), r=[gp, ac], w=[ac])
                            b.op("dve", lambda e, gp=gp, a3=a3: e.scalar_tensor_tensor(out=a3, in0=gp.ap[:, :, 2:66], scalar=pvl(PV_FW + 44 + c), in1=a3, op0=ALU.mult, op1=ALU.add), r=[gp, ac], w=[ac])
                            sg = sgt.next()
                            b.op("act", lambda e, sg=sg, ac=ac: e.activation(out=sg.ap[:], in_=ac.ap[:], func=AF.Silu, bias=pvl(PV_FB + c)), r=[ac], w=[sg])
                            b.op("dve", lambda e, sg=sg, pu=pu: e.tensor_tensor(out=at.ap[:, ci, :], in0=pu.ap[:, :], in1=sg.ap[:], op=ALU.mult), r=[pu, sg], w=[at])
                            if ci == 1 and pend:
                                down_proj(*pend.pop(0))
                        pend.append((wd_, at, ng, bi))
                while pend:
                    down_proj(*pend.pop(0))
            b.barrier()
            fpw.close()
            if l == 0:
                ck(13, xTt[:, 0, :])
                for bi in range(4):
                    b.dma("sp", xscr[:, :, blk(bi)], xTt[:, :, blk(bi)], r=[d_x[bi]])
            else:
                with contextlib.ExitStack() as ph:
                    yT = sb("yT", [128, 8, 512], F32, ph)
                    d_y = [Dep() for _ in range(8)]
                    yo = rot("yo", 3, [128, D], F32, ph)
                    cur = {}

                    def fin(bi, j, tm):
                        b.op("act", lambda e: e.activation(out=yT[:, j, :], in_=tm.ap[:], func=AF.Identity, scale=pv[:, 0, 64 + j:65 + j]), r=[tm], w=[d_y[j]])
                        if j == 7:
                            for tt in range(4):
                                yt = yo.next()
                                for half in range(2):
                                    ps = psA.next()
                                    for q in range(4):
                                        jj = half * 4 + q
                                        b.op("pe", lambda e, jj=jj, q=q, ps=ps: e.transpose(ps.ap[:, q * 128:(q + 1) * 128], yT[:, jj, tt * 128:(tt + 1) * 128], ident), r=[d_y[jj]], w=[ps])
                                    b.op("act", lambda e, ps=ps, yt=yt, half=half: e.activation(out=yt.ap[:, half * 512:(half + 1) * 512], in_=ps.ap[:, :], func=AF.Copy), r=[ps], w=[yt])
                                t0 = bi * 512 + tt * 128
                                b.dma("sp", y_d[t0:t0 + 128, :], yt.ap[:], r=[yt])
                    norm_phase(xTt, ph, None, None, fin)
                    b.barrier()
                xs.close()
        for l in range(2):
            b.dma("sp", flru_d[l].rearrange("j p s -> p j s"), flru[:, l, :, :], r=[d_flru])
        b.barrier()
        DBG["nins"] = b.nins


_NC = None


def _consts():
    c = np.zeros((128, NCONST), np.float32)
    c[:, 0:128] = np.eye(128, dtype=np.float32)
    p = np.arange(128)
    c[:, 128:256] = (p[:, None] // 64 == p[None, :] // 64).astype(np.float32)
    s = np.arange(64)[:, None]
    t = np.arange(64)[None, :]
    c[0:64, 256:320] = -1.0 * (t > s)
    c[0:64, 320:384] = -1.0 * (t >= s)
    c[0:64, 384:448] = (t > s)
    c[0:64, 448:512] = (t >= s)
    c[0:64, 512:576] = -1.0 * (t < s)
    s2 = np.arange(32)[:, None]
    t2 = np.arange(32)[None, :]
    c[0:32, 576:608] = (t2 >= s2)
    c[0:32, 608:640] = (t2 >= s2)
    return c


def prep(inp):
    f = lambda k: np.ascontiguousarray(np.asarray(inp[k], dtype=np.float32))
    xp, xsm = f("x_prompt"), f("x_sample")
    L = 2
    pv = np.zeros((L, 128, NPV), np.float32)
    colT = lambda v: v.reshape(-1, 128).T
    for l in range(L):
        pv[l, :, 0:8] = colT(f("norm_mix_g")[l])
        pv[l, :, 8:16] = colT(f("norm_ffn_g")[l])
        pv[l, :, 16:64] = colT(f("ada_b")[l])
        pv[l, :, 64:72] = colT(f("final_g"))
        for j in range(4):
            c0 = PV_RW + j * 9
            sl = slice(j * 128, (j + 1) * 128)
            pv[l, :, c0 + 0] = f("rwkv_w0")[l, 0, sl]
            pv[l, :, c0 + 1] = f("rwkv_w0")[l, 1, sl]
            pv[l, :, c0 + 2] = f("rwkv_a0")[l, 0, sl]
            pv[l, :, c0 + 3] = f("rwkv_a0")[l, 1, sl]
            pv[l, :, c0 + 4] = f("rwkv_k_k")[l, sl]
            pv[l, :, c0 + 5] = f("rwkv_k_a")[l, sl]
            pv[l, :, c0 + 6] = f("rwkv_r_k")[l].reshape(-1)[sl]
            pv[l, :, c0 + 7] = f("rwkv_ln_w")[l, sl]
            pv[l, :, c0 + 8] = f("rwkv_ln_b")[l, sl]
        for j in range(2):
            c0 = PV_LRU + j * 11
            sl = slice(j * 128, (j + 1) * 128)
            for k in range(4):
                pv[l, :, c0 + k] = f("lru_conv_w")[l, k, sl]
            pv[l, :, c0 + 4] = f("lru_conv_b")[l, sl]
            for d in range(2):
                pv[l, :, c0 + 5 + d] = f("lru_ba")[l, d, sl]
                pv[l, :, c0 + 7 + d] = f("lru_bx")[l, d, sl]
                pv[l, :, c0 + 9 + d] = f("lru_lambda")[l, d, sl]
            c0 = PV_HG + j * 5
            for d in range(2):
                for l2 in range(2):
                    pv[l, :, c0 + d * 2 + l2] = f("hgrn_lb_logits")[d, l2, sl]
            pv[l, :, c0 + 4] = f("hgrn_norm_g")[l, sl]
        for k in range(3):
            pv[l, :, PV_FW + k * 22:PV_FW + (k + 1) * 22] = colT(f("ffn_conv_w")[l, k])
        pv[l, :, PV_FB:PV_FB + 22] = colT(f("ffn_conv_b")[l])
    consts = _consts()
    shared = dict(
        consts=consts, pv=pv, ada_w=f("ada_w"), w_in=f("w_in"), w_out=f("w_out"),
        rwkv_w_up=f("rwkv_w_up").reshape(2, 128, 512), rwkv_a_up=f("rwkv_a_up").reshape(2, 128, 512),
        rwkv_g_up=f("rwkv_g_up"), lru_wa=f("lru_wa"), lru_wx=f("lru_wx"),
        ffn_w_gate=f("ffn_w_gate"), ffn_w_up=f("ffn_w_up"), ffn_w_down=f("ffn_w_down"))
    srw, slru, shg = f("state_rwkv"), f("state_rglru"), f("state_hgrn")
    in_maps = []
    for core in range(8):
        m = dict(shared)
        irw = np.zeros((2, 2, 8, 4, 128, 64), np.float32)
        ilru = np.zeros((2, 2, 128, 16), np.float32)
        ihg = np.zeros((2, 2, 8, 2, 128, 64), np.float32)
        if core < 4:
            bb = core
            m["x"] = xsm[bb]
            m["cond"] = np.ascontiguousarray(f("c")[bb].reshape(8, 128).T)
            m["carry"] = np.ones((128, 1), np.float32)
            m["fmask"] = np.zeros((128, 7), np.float32)
            for l in range(2):
                for d in range(2):
                    st = srw[bb, l, d].transpose(0, 2, 1).reshape(4, 128, 64)
                    irw[l, d, 0] = st
                    ihg[l, d, 0] = shg[bb, l, d].reshape(2, 128, 64)
                    ilru[l, :, :, 0 * 2 + d] = slru[bb, l, d].reshape(2, 128)
        else:
            m["x"] = np.ascontiguousarray(xp[(core - 4) * 8:(core - 3) * 8].reshape(T, D))
            m["cond"] = np.ascontiguousarray(f("c_ctx").reshape(8, 128).T)
            m["carry"] = np.zeros((128, 1), np.float32)
            fm = np.ones((128, 7), np.float32)
            fm[:, 3] = 0.0
            m["fmask"] = fm
        m["irw"], m["ilru"], m["ihg"] = irw, ilru, ihg
        in_maps.append(m)
    return in_maps


def kernel(**inp):
    global _NC
    in_maps = prep(inp)
    xp = inp["x_prompt"]
    if _NC is None:
        _NC = build_nc()
    res = run_bass_kernel_spmd(_NC, in_maps, core_ids=list(range(8)))
    R = res.results
    y_prompt = np.concatenate([R[c]["y"].reshape(8, 256, D) for c in range(4, 8)], axis=0)
    y_sample = np.stack([R[c]["y"] for c in range(4)], axis=0)
    new_rwkv = np.zeros((32, 2, 2, 8, 64, 64), np.float32)
    new_lru = np.zeros((32, 2, 2, 256), np.float32)
    new_hg = np.zeros((32, 2, 2, 4, 64, 64), np.float32)
    for c in range(4, 8):
        frw, flr, fhg = R[c]["frw"], R[c]["flru"], R[c]["fhg"]
        for n in range(8):
            sq = (c - 4) * 8 + n
            for l in range(2):
                for d in range(2):
                    s = (7 - n) if d else n
                    new_rwkv[sq, l, d] = frw[l, d, s].reshape(8, 64, 64).transpose(0, 2, 1)
                    new_hg[sq, l, d] = fhg[l, d, s].reshape(4, 64, 64)
                    new_lru[sq, l, d] = flr[l, :, :, s * 2 + d].reshape(256)
    return (y_prompt, y_sample, new_rwkv, new_lru, new_hg)
```

```python
import contextlib, math
import numpy as np
import concourse.bass as bass
import concourse.mybir as mybir
from concourse.bass_utils import run_bass_kernel_spmd

F32 = mybir.dt.float32
BF16 = mybir.dt.bfloat16
AF = mybir.ActivationFunctionType
ALU = mybir.AluOpType

T = 2048
D = 1024
NPV = 228
NCONST = 640
DFF = 2816
NFC = 22
CW = -math.exp(-0.5)
PV_RW, PV_LRU, PV_HG, PV_FW, PV_FB = 72, 108, 130, 140, 206
DBG = {}


class Dep:
    __slots__ = ("w", "r")

    def __init__(s):
        s.w = None
        s.r = {}


class Tl:
    def __init__(s, ap):
        s.ap = ap
        s.d = Dep()


class Rot:
    def __init__(s, tl):
        s.tl = tl
        s.i = 0

    def next(s):
        x = s.tl[s.i % len(s.tl)]
        s.i += 1
        return x


class B:
    def __init__(s, nc, es):
        s.nc = nc
        s.eng = {}
        for name, obj in (("pe", nc.tensor), ("dve", nc.vector), ("act", nc.scalar),
                          ("pool", nc.gpsimd), ("sp", nc.sync)):
            sem = es.enter_context(nc.semaphore("sem_" + name))
            s.eng[name] = dict(obj=obj, sem=sem, cnt=0, known={})
        s.dsemq = {q: [[es.enter_context(nc.semaphore("dsem%s%d" % (q, i))), 0] for i in range(24)] for q in ("sp", "pool")}
        s.dsem = s.dsemq["sp"] + s.dsemq["pool"]
        s.di = {"sp": 0, "pool": 0}
        s.nins = 0

    def _wait(s, e, evs):
        best = {}
        for (sem, val) in evs:
            k = id(sem)
            if k not in best or best[k][1] < val:
                best[k] = (sem, val)
        for k, (sem, val) in best.items():
            if e["known"].get(k, 0) < val:
                e["obj"].wait_ge(sem, val)
                e["known"][k] = val

    def _collect(s, en, r, w):
        evs = []
        for t in r:
            if t.w is not None:
                evs.append(t.w[:2])
        for t in w:
            if t.w is not None and not (t.w[2] == en and en == "pe"):
                evs.append(t.w[:2])
            for rd in t.r.values():
                if not (rd[2] == en and en == "pe"):
                    evs.append(rd[:2])
        return evs

    def _update(s, ev, r, w):
        for t in r:
            t.r[id(ev[0])] = ev
        for t in w:
            t.w = ev
            t.r = {}

    @staticmethod
    def _flat(lst):
        out = []
        for x in lst:
            if isinstance(x, (list, tuple)):
                out += B._flat(x)
            else:
                out.append(x.d if isinstance(x, Tl) else x)
        return out

    def op(s, en, fn, r=(), w=()):
        r = s._flat(r)
        w = s._flat(w)
        e = s.eng[en]
        s._wait(e, s._collect(en, r, w))
        ins = fn(e["obj"])
        e["cnt"] += 1
        ins.then_inc(e["sem"], 1)
        s.nins += 1
        ev = (e["sem"], e["cnt"], en)
        s._update(ev, r, w)
        return ev

    def dma(s, qn, out, in_, r=(), w=()):
        r = s._flat(r)
        w = s._flat(w)
        e = s.eng[qn]
        slot = s.dsemq[qn][s.di[qn] % 24]
        s.di[qn] += 1
        evs = s._collect(None, r, w)
        if slot[1] > 0:
            evs.append((slot[0], slot[1]))
        s._wait(e, evs)
        ins = e["obj"].dma_start(out=out, in_=in_)
        slot[1] += 16
        ins.then_inc(slot[0], 16)
        s.nins += 1
        ev = (slot[0], slot[1], "dma")
        s._update(ev, r, w)
        return ev

    def barrier(s):
        for en, e in s.eng.items():
            evs = []
            for on, o in s.eng.items():
                if on != en and o["cnt"] > 0:
                    evs.append((o["sem"], o["cnt"]))
            for sl in s.dsem:
                if sl[1] > 0:
                    evs.append((sl[0], sl[1]))
            s._wait(e, evs)


class StopBuild(Exception):
    pass


def build_nc(stop=None):
    nc = bass.Bass("TRN2", target_bir_lowering=False)
    try:
        _build(nc, stop)
    except StopBuild:
        pass
    return nc


def _build(nc, stop):

    def din(name, shape):
        return nc.dram_tensor(name, list(shape), F32, kind="ExternalInput").ap()

    def dout(name, shape):
        return nc.dram_tensor(name, list(shape), F32, kind="ExternalOutput").ap()

    x_d = din("x", [T, D])
    cond_d = din("cond", [128, 8])
    carry_d = din("carry", [128, 1])
    fmask_d = din("fmask", [128, 7])
    consts_d = din("consts", [128, NCONST])
    pv_d = din("pv", [2, 128, NPV])
    irw_d = din("irw", [2, 2, 8, 4, 128, 64])
    ilru_d = din("ilru", [2, 2, 128, 16])
    ihg_d = din("ihg", [2, 2, 8, 2, 128, 64])
    ada_w_d = din("ada_w", [2, D, 6 * D])
    w_in_d = din("w_in", [2, D, 3712])
    w_out_d = din("w_out", [2, D, D])
    w_up_d = din("rwkv_w_up", [2, 128, 512])
    a_up_d = din("rwkv_a_up", [2, 128, 512])
    g_up_d = din("rwkv_g_up", [2, 128, 512])
    lwa_d = din("lru_wa", [2, 2, 4, 64, 64])
    lwx_d = din("lru_wx", [2, 2, 4, 64, 64])
    wg_d = din("ffn_w_gate", [2, D, DFF])
    wu_d = din("ffn_w_up", [2, D, DFF])
    wd_d = din("ffn_w_down", [2, DFF, D])
    y_d = dout("y", [T, D])
    frw_d = dout("frw", [2, 2, 8, 4, 128, 64])
    flru_d = dout("flru", [2, 2, 128, 16])
    fhg_d = dout("fhg", [2, 2, 8, 2, 128, 64])
    xscr = nc.dram_tensor("xscr", [128, 8, T], F32, kind="Internal").ap()
    mixscr = nc.dram_tensor("mixscr", [8, 128, T], BF16, kind="Internal").ap()
    dbg_d = dout("dbg", [128, 8 * T]) if stop is not None else None

    with contextlib.ExitStack() as es:
        b = B(nc, es)

        uid = [0]

        def sb(name, shape, dt=F32, st=es):
            uid[0] += 1
            return st.enter_context(nc.sbuf_tensor("%s_%d" % (name, uid[0]), list(shape), dt))

        def tl(name, shape, dt=F32, st=es):
            return Tl(sb(name, shape, dt, st))

        def rot(name, n, shape, dt=F32, st=es):
            return Rot([tl("%s%d" % (name, i), shape, dt, st) for i in range(n)])

        psA = Rot([Tl(es.enter_context(nc.psum_tensor("psA%d" % i, [128, 512], F32))) for i in range(8)])
        psB = psA

        cst = sb("cst", [128, NCONST])
        identb = sb("identb", [128, 128], BF16)
        bonesb = sb("bonesb", [128, 128], BF16)
        onesb = sb("onesb", [128, 128], BF16)
        pv = sb("pv", [128, 2, NPV])
        carry = sb("carry", [128, 1])
        fmask = sb("fmask", [128, 7])
        condt = sb("condt", [128, 8])
        scb = sb("scb", [128, 8], BF16)
        modt = sb("modt", [128, 2, 48])
        gs = sb("gs", [128, 2, 16])
        der = sb("der", [128, 2, 32])
        hT = sb("hT", [128, 8, T], BF16)
        flru = sb("flru", [128, 2, 2, 16])
        ilru = sb("ilru", [128, 2, 2, 16])
        d_hT = [Dep() for _ in range(4)]
        d_mix = [Dep() for _ in range(8)]
        d_x = [[Dep() for _ in range(8)] for _ in range(4)]
        d_flru = Dep()
        wtile = rot("wt", 3, [128, 8, 128], BF16)
        ident = cst[:, 0:128]
        m1 = cst[0:64, 256:384]
        m2 = cst[0:64, 384:512]
        m3 = cst[0:64, 512:576]
        mh = cst[0:32, 576:608]

        def ck(n, src=None):
            if stop is not None and stop == n:
                b.barrier()
                if src is not None:
                    b.dma("pool", dbg_d[0:src.shape[0], 0:src.shape[1]], src)
                b.barrier()
                DBG["nins"] = b.nins
                raise StopBuild()


        d0 = Dep()
        b.dma("sp", cst[:], consts_d[:, :], w=[d0])
        for l in range(2):
            b.dma("sp", pv[:, l, :], pv_d[l], w=[d0])
        b.dma("sp", carry[:], carry_d[:, :], w=[d0])
        b.dma("sp", fmask[:], fmask_d[:, :], w=[d0])
        b.dma("sp", condt[:], cond_d[:, :], w=[d0])
        for l in range(2):
            b.dma("sp", ilru[:, l, :, :], ilru_d[l].rearrange("j p s -> p j s"), w=[d0])
        b.op("dve", lambda e: e.tensor_copy(out=identb[:], in_=cst[:, 0:128]), r=[d0], w=[d0])
        b.op("dve", lambda e: e.tensor_copy(out=bonesb[:], in_=cst[:, 128:256]), r=[d0], w=[d0])
        b.op("dve", lambda e: e.memset(onesb[:], 1.0), w=[d0])
        b.op("dve", lambda e: e.memset(flru[:], 0.0), w=[d0])
        b.barrier()
        b.op("act", lambda e: e.activation(out=scb[:], in_=condt[:], func=AF.Silu), w=[d0])
        for l in range(2):
            for j in range(4):
                c = PV_RW + j * 9 + 5
                b.op("dve", lambda e, c=c, j=j: e.tensor_scalar(out=der[:, l, j:j + 1], in0=pv[:, l, c:c + 1],
                                                               scalar1=-1.0, scalar2=1.0, op0=ALU.mult, op1=ALU.add), w=[d0])
            for j in range(2):
                for d in range(2):
                    c = PV_LRU + j * 11 + 9 + d
                    o = 4 + j * 2 + d
                    b.op("act", lambda e, c=c, o=o: e.activation(out=der[:, l, o:o + 1], in_=pv[:, l, c:c + 1],
                                                                 func=AF.Exp, scale=-1.0), w=[d0])
                    b.op("act", lambda e, o=o: e.activation(out=der[:, l, o:o + 1], in_=der[:, l, o:o + 1],
                                                            func=AF.Ln, bias=1.0), r=[d0], w=[d0])
                    b.op("dve", lambda e, o=o: e.tensor_scalar(out=der[:, l, o + 4:o + 5], in0=der[:, l, o:o + 1],
                                                              scalar1=-16.0, scalar2=None, op0=ALU.mult), r=[d0], w=[d0])
                    b.op("dve", lambda e, o=o: e.tensor_scalar(out=der[:, l, o:o + 1], in0=der[:, l, o:o + 1],
                                                              scalar1=-8.0, scalar2=None, op0=ALU.mult), r=[d0], w=[d0])
                    o2 = 12 + j * 2 + d
                    if l == 0:
                        b.op("dve", lambda e, o2=o2: e.memset(der[:, l, o2:o2 + 1], 0.0), w=[d0])
                        b.op("dve", lambda e, o2=o2: e.memset(der[:, l, o2 + 4:o2 + 5], 1.0), w=[d0])
                    else:
                        ch = PV_HG + j * 5 + d * 2
                        b.op("dve", lambda e, o2=o2, ch=ch: e.tensor_tensor(out=der[:, l, o2:o2 + 1], in0=pv[:, l, ch + 1:ch + 2],
                                                                            in1=pv[:, l, ch:ch + 1], op=ALU.subtract), w=[d0])
                        b.op("act", lambda e, o2=o2: e.activation(out=der[:, l, o2:o2 + 1], in_=der[:, l, o2:o2 + 1],
                                                                  func=AF.Sigmoid), r=[d0], w=[d0])
                        b.op("dve", lambda e, o2=o2: e.tensor_scalar(out=der[:, l, o2 + 4:o2 + 5], in0=der[:, l, o2:o2 + 1],
                                                                    scalar1=-1.0, scalar2=1.0, op0=ALU.mult, op1=ALU.add), r=[d0], w=[d0])
        b.barrier()
        ck(1, der[:, 0, :])

        with contextlib.ExitStack() as ph:
            apc = rot("apc", 2, [128, 8, 512], BF16, ph)
            for l in range(2):
                ps = psA.next()
                for pc in range(12):
                    wt = apc.next()
                    b.dma("pool", wt.ap[:], ada_w_d[l].rearrange("(j p) c -> p j c", p=128)[:, :, pc * 512:(pc + 1) * 512], w=[wt])
                    for mm in range(4):
                        m = pc * 4 + mm
                        for j in range(8):
                            b.op("pe", lambda e, j=j, m=m, mm=mm, wt=wt, ps=ps: e.matmul(
                                ps.ap[:, m:m + 1], lhsT=wt.ap[:, j, mm * 128:(mm + 1) * 128], rhs=scb[:, j:j + 1],
                                start=(j == 0), stop=(j == 7)), r=[wt], w=[ps])
                b.op("dve", lambda e, ps=ps, l=l: e.tensor_tensor(out=modt[:, l, :], in0=ps.ap[:, 0:48], in1=pv[:, l, 16:64], op=ALU.add),
                     r=[ps], w=[d0])
                b.op("dve", lambda e, l=l: e.scalar_tensor_tensor(out=gs[:, l, 0:8], in0=modt[:, l, 8:16], scalar=1.0, in1=pv[:, l, 0:8],
                                                                  op0=ALU.add, op1=ALU.mult), r=[d0], w=[d0])
                b.op("dve", lambda e, l=l: e.scalar_tensor_tensor(out=gs[:, l, 8:16], in0=modt[:, l, 32:40], scalar=1.0, in1=pv[:, l, 8:16],
                                                                  op0=ALU.add, op1=ALU.mult), r=[d0], w=[d0])
        b.barrier()
        ck(2, modt[:, 0, :])

        def blk(bi):
            return slice(bi * 512, (bi + 1) * 512)

        def norm_phase(xT, ph, gfn, sfn, outfn):
            sqb = rot("nsq", 3, [128, 512], BF16, ph)
            rsb = rot("nrs", 2, [128, 512], F32, ph)
            tmb = rot("ntm", 3, [128, 512], F32, ph)
            for bi in range(4):
                ps = psA.next()
                for j in range(8):
                    sq = sqb.next()
                    b.op("act", lambda e, j=j, sq=sq: e.activation(out=sq.ap[:], in_=xT[:, j, blk(bi)], func=AF.Square),
                         r=[d_x[bi][j]], w=[sq])
                    b.op("pe", lambda e, j=j, sq=sq, ps=ps: e.matmul(ps.ap[:, :], lhsT=onesb[:], rhs=sq.ap[:], start=(j == 0), stop=(j == 7)),
                         r=[sq], w=[ps])
                rs = rsb.next()
                b.op("act", lambda e, rs=rs, ps=ps: e.activation(out=rs.ap[:], in_=ps.ap[:, :], func=AF.Ln, scale=1.0 / D, bias=1e-6),
                     r=[ps], w=[rs])
                b.op("act", lambda e, rs=rs: e.activation(out=rs.ap[:], in_=rs.ap[:], func=AF.Exp, scale=-0.5), r=[rs], w=[rs])
                for j in range(8):
                    tm = tmb.next()
                    b.op("dve", lambda e, j=j, tm=tm, rs=rs: e.tensor_tensor(out=tm.ap[:], in0=xT[:, j, blk(bi)], in1=rs.ap[:], op=ALU.mult),
                         r=[d_x[bi][j], rs], w=[tm])
                    outfn(bi, j, tm)

        def to_hT(l, which):
            def f(bi, j, tm):
                g = gs[:, l, which * 8 + j:which * 8 + j + 1]
                sh = modt[:, l, which * 24 + j:which * 24 + j + 1]
                b.op("act", lambda e: e.activation(out=hT[:, j, blk(bi)], in_=tm.ap[:], func=AF.Identity, scale=g, bias=sh),
                     r=[tm], w=[d_hT[bi]])
            return f

        def load_w(src, cols):
            wt = wtile.next()
            n = cols.stop - cols.start
            b.dma("pool", wt.ap[:, :, 0:n], src.rearrange("(j p) c -> p j c", p=128)[:, :, cols], w=[wt])
            return wt

        def proj(l, col0, ncols, evac, wt=None):
            if wt is None:
                wt = load_w(w_in_d[l], slice(col0, col0 + ncols))
            for bi in range(4):
                ps = psA.next()
                for j in range(8):
                    b.op("pe", lambda e, j=j, ps=ps: e.matmul(ps.ap[0:ncols, :], lhsT=wt.ap[:, j, 0:ncols], rhs=hT[:, j, blk(bi)],
                                                              start=(j == 0), stop=(j == 7)), r=[wt, d_hT[bi]], w=[ps])
                evac(bi, ps)

        def act_evac(dst, func=AF.Copy, **kw):
            def f(bi, ps):
                b.op("act", lambda e: e.activation(out=dst.ap[:, blk(bi)], in_=ps.ap[:, :], func=func, **kw), r=[ps], w=[dst])
            return f

        def headsum(src, consume, ph_rot):
            for bi in range(4):
                ps = psA.next()
                b.op("pe", lambda e, ps=ps: e.matmul(ps.ap[:, :], lhsT=bonesb[:], rhs=src.ap[:, blk(bi)], start=True, stop=True),
                     r=[src], w=[ps])
                consume(bi, ps)

        def rv(ap, d):
            return ap[:, ::-1] if d else ap[:, :]

        for l in range(2):
            pvl = lambda c: pv[:, l, c:c + 1]
            if l == 0:
                xs = contextlib.ExitStack()
                xTt = sb("xT", [128, 8, T], F32, xs)
            if l == 0:
                with contextlib.ExitStack() as ph:
                    xin = rot("xin", 3, [128, D], F32, ph)
                    for tt in range(16):
                        xt = xin.next()
                        b.dma("sp", xt.ap[:], x_d[tt * 128:(tt + 1) * 128, :], w=[xt])
                        for half in range(2):
                            ps = psA.next()
                            for q in range(4):
                                j = half * 4 + q
                                b.op("pe", lambda e, j=j, q=q, ps=ps, xt=xt: e.transpose(ps.ap[:, q * 128:(q + 1) * 128], xt.ap[:, j * 128:(j + 1) * 128], ident),
                                     r=[xt], w=[ps])
                            b.op("act" if half else "dve",
                                 (lambda e, ps=ps, half=half, tt=tt: e.activation(out=xTt[:, half * 4:half * 4 + 4, tt * 128:(tt + 1) * 128],
                                                                                 in_=ps.ap[:, :].rearrange("p (q t) -> p q t", q=4), func=AF.Copy)) if half else
                                 (lambda e, ps=ps, half=half, tt=tt: e.tensor_copy(out=xTt[:, half * 4:half * 4 + 4, tt * 128:(tt + 1) * 128],
                                                                                  in_=ps.ap[:, :].rearrange("p (q t) -> p q t", q=4))),
                                 r=[ps], w=[d_x[tt // 4][half * 4:half * 4 + 4]])
            with contextlib.ExitStack() as ph:
                norm_phase(xTt, ph, None, None, to_hT(l, 0))
            if l == 0:
                for bi in range(4):
                    b.dma("sp", xscr[:, :, blk(bi)], xTt[:, :, blk(bi)], r=[d_x[bi]])
            b.barrier()
            if l == 0:
                ck(3, hT[:, 0, :])
            xs.close()

            with contextlib.ExitStack() as mp:
                lora_w = tl("lora_w", [128, T], BF16, mp)
                lora_a = tl("lora_a", [128, T], BF16, mp)
                lora_g = tl("lora_g", [128, T], BF16, mp)
                wup = tl("wup", [128, 512], BF16, mp)
                aup = tl("aup", [128, 512], BF16, mp)
                gup = tl("gup", [128, 512], BF16, mp)
                b.dma("pool", wup.ap[:], w_up_d[l], w=[wup])
                b.dma("pool", aup.ap[:], a_up_d[l], w=[aup])
                b.dma("pool", gup.ap[:], g_up_d[l], w=[gup])
                proj(l, 1536, 128, act_evac(lora_w, AF.Tanh))
                proj(l, 1664, 128, act_evac(lora_a))
                proj(l, 1792, 128, act_evac(lora_g, AF.Sigmoid))
                if l == 0:
                    ck(4, lora_w.ap[:, :])

                with contextlib.ExitStack() as rp:
                    rst64 = sb("rst64", [128, T], BF16, rp)
                    b.op("dve", lambda e: e.memset(rst64[:], 1.0), w=[d0])
                    b.op("dve", lambda e: e.memset(rst64[:].rearrange("p (c i) -> p c i", i=64)[:, :, 0:1], 0.0), w=[d0])
                    r_bf = tl("r_bf", [128, T], BF16, rp)
                    k32 = tl("k32", [128, T], BF16, rp)
                    v_bf = tl("v_bf", [128, T], BF16, rp)
                    t1 = tl("t1", [128, T], F32, rp)
                    kk_bf = tl("kk_bf", [128, T], BF16, rp)
                    sg32 = tl("sg32", [128, T], F32, rp)
                    bs32 = tl("bs32", [128, T], F32, rp)
                    a_bf = tl("a_bf", [128, T], BF16, rp)
                    b_bf = tl("b_bf", [128, T], BF16, rp)
                    kd_bf = tl("kd_bf", [128, T], BF16, rp)
                    s4 = b_bf
                    bon = kk_bf
                    s4p = kd_bf
                    E = a_bf
                    mst = a_bf
                    vdir = [v_bf, tl("vrev", [128, T], BF16, rp)]
                    KR = [tl("KR%d" % d, [128, 32, 2, 64], BF16, rp) for d in range(2)]
                    KH = [tl("KH%d" % d, [128, T], BF16, rp) for d in range(2)]
                    BH = [tl("BH%d" % d, [128, T], BF16, rp) for d in range(2)]
                    KB = [tl("KB%d" % d, [128, T], BF16, rp) for d in range(2)]
                    BB = [tl("BB%d" % d, [128, T], BF16, rp) for d in range(2)]
                    WL = [tl("WL%d" % d, [128, 32], F32, rp) for d in range(2)]
                    o32 = sb("o32", [128, T], F32, rp)
                    P32 = [tl("P32%d" % d, [128, 64], F32, rp) for d in range(2)]
                    Pbf = [tl("Pbf%d" % d, [128, 64], BF16, rp) for d in range(2)]
                    pin = rot("pin", 2, [128, 64], F32, rp)
                    pout = rot("pout", 2, [128, 64], F32, rp)
                    t1b = t1.ap[:].bitcast(BF16)
                    sgb = sg32.ap[:].bitcast(BF16)
                    bsb = bs32.ap[:].bitcast(BF16)
                    bbb = b_bf.ap[:]
                    tok = [[tl("tok%d%d" % (d, p), [64, 3, 2, 2, 64], BF16, rp) for p in range(2)] for d in range(2)]
                    Pbd = [tl("Pbd%d" % d, [128, 128], BF16, rp) for d in range(2)]
                    for t_ in tok[0] + tok[1] + Pbd:
                        b.op("dve", lambda e, t_=t_: e.memset(t_.ap[:], 0.0), w=[t_])
                    for d_ in range(2):
                        tok[d_].append(Tl(sgb[0:64, 2304 + d_ * 768:3072 + d_ * 768].rearrange("p (q a b c) -> p q a b c", q=3, a=2, b=2)))
                        tok[d_].append(Tl(bsb[0:64, 2304 + d_ * 768:3072 + d_ * 768].rearrange("p (q a b c) -> p q a b c", q=3, a=2, b=2)))
                    KRm = {}
                    for d_ in range(2):
                        for h_ in range(2):
                            for p_ in range(2):
                                KRm[(d_, h_, p_)] = tl("KRm%d%d%d" % (d_, h_, p_), [128, 128], BF16, rp)
                                b.op("dve", lambda e, t_=KRm[(d_, h_, p_)]: e.memset(t_.ap[:], 0.0), w=[KRm[(d_, h_, p_)]])
                            for p_ in range(2, 4):
                                ix = (d_ * 2 + h_) * 2 + (p_ - 2)
                                KRm[(d_, h_, p_)] = Tl(bbb[:, ix * 128:(ix + 1) * 128])
                    UA = {}
                    UT = {}
                    UTt = sb("UTt", [64, 4 * 960], BF16, rp)
                    b.op("dve", lambda e: e.memset(UTt[:], 0.0), w=[d0])
                    UTaps = [UTt[:], t1b[0:64, 0:3840]]
                    UTv = [x.rearrange("p (u r) -> p u r", u=4) for x in UTaps]
                    TRD = {}
                    for ts_ in range(2):
                        for q in range(1, 6):
                            for pa in range(2):
                                TRD[(ts_, pa, q)] = (Dep(), Dep())
                        for u in range(4):
                            tr = [None]
                            for q in range(1, 6):
                                o_ = u * 960 + (q - 1) * 192
                                dA, dT = TRD[(ts_, u // 2, q)]
                                X_ = UTaps[ts_]
                                ent = dict(lo=Tl(X_[:, o_ + 128:o_ + 192]), up=Tl(X_[:, o_:o_ + 64]), tt=Tl(X_[:, o_ + 64:o_ + 128]), ut=Tl(X_[:, o_:o_ + 128]))
                                ent["lo"].d = dA; ent["up"].d = dA; ent["tt"].d = dT; ent["ut"].d = dA
                                ent["dA"], ent["dT"] = dA, dT
                                tr.append(ent)
                            UT[(ts_, u)] = tr
                    UAt = []
                    UAD = []
                    for p in range(4):
                        if p < 2:
                            t_ = sb("UAt%d" % p, [64, 4 * 576], BF16, rp)[:]
                            b.op("dve", lambda e, t_=t_: e.memset(t_, 0.0), w=[d0])
                        else:
                            t_ = (sgb if p == 2 else bsb)[0:64, 0:2304]
                        UAt.append(t_)
                        dd = dict(ttf=Dep(), xs=[Dep(), Dep()], y=[Dep(), Dep()])
                        UAD.append(dd)
                        for u in range(4):
                            o_ = u * 576
                            hh_u = u % 2
                            yU = Tl(t_[:, o_ + 448 + hh_u * 64:o_ + 512 + hh_u * 64]); yU.d = dd["y"][u // 2]
                            upad = Tl(t_[:, o_ + 448:o_ + 576]); upad.d = dd["y"][u // 2]
                            sc = Tl(t_[:, o_:o_ + 320])
                            ud = dict(sc=sc, y=[None] * 6 + [yU], upad=upad, ttf=Tl(t_[:, o_ + 320:o_ + 384]), xs=Tl(t_[:, o_ + 384:o_ + 448]))
                            ud["ttf"].d = dd["ttf"]; ud["xs"].d = dd["xs"][u // 2]
                            for nm, lo_, hi_ in (("sc1", 0, 128), ("sc2", 128, 256), ("sc3", 256, 320), ("up0", 0, 64)):
                                ud[nm] = Tl(t_[:, o_ + lo_:o_ + hi_])
                                ud[nm].d = sc.d
                            UA[(u, p)] = ud
                    d_o = [Dep() for _ in range(32)]
                    prew = [load_w(w_in_d[l], slice(q_ * 512, q_ * 512 + 128)) for q_ in range(3)]
                    for j in range(4):
                        c0 = PV_RW + j * 9
                        proj(l, j * 128, 128, act_evac(r_bf), prew[0])
                        proj(l, 512 + j * 128, 128, act_evac(k32), prew[1])
                        proj(l, 1024 + j * 128, 128, act_evac(v_bf), prew[2])
                        if j < 3:
                            prew = [load_w(w_in_d[l], slice(q_ * 512 + (j + 1) * 128, q_ * 512 + (j + 2) * 128)) for q_ in range(3)]
                        b.op("dve", lambda e: e.memset(o32[:], 0.0), w=d_o)
                        b.op("dve", lambda e: e.tensor_scalar(out=t1.ap[:], in0=k32.ap[:], scalar1=pvl(c0 + 4), scalar2=None, op0=ALU.mult),
                             r=[k32], w=[t1])
                        b.op("act", lambda e: e.activation(out=s4.ap[:], in_=t1.ap[:], func=AF.Square), r=[t1], w=[s4])

                        def kk_cons(bi, ps):
                            b.op("act", lambda e: e.activation(out=sg32.ap[:, blk(bi)], in_=ps.ap[:, :], func=AF.Ln, bias=1e-12), r=[ps], w=[sg32])
                            b.op("act", lambda e: e.activation(out=sg32.ap[:, blk(bi)], in_=sg32.ap[:, blk(bi)], func=AF.Exp, scale=-0.5), r=[sg32], w=[sg32])
                            b.op("dve", lambda e: e.tensor_tensor(out=kk_bf.ap[:, blk(bi)], in0=t1.ap[:, blk(bi)], in1=sg32.ap[:, blk(bi)], op=ALU.mult),
                                 r=[t1, sg32], w=[kk_bf])
                        headsum(s4, kk_cons, None)
                        b.op("act", lambda e: e.activation(out=vdir[1].ap[:], in_=v_bf.ap[:, ::-1], func=AF.Copy), r=[v_bf], w=[vdir[1]])
                        for d in range(2):
                            pr_ = slice(d * 64, d * 64 + 64)
                            for bi in range(4):
                                ps = psA.next()
                                b.op("pe", lambda e, ps=ps: e.matmul(ps.ap[:, :], lhsT=wup.ap[pr_, j * 128:(j + 1) * 128], rhs=lora_w.ap[pr_, blk(bi)],
                                                                    start=True, stop=True), r=[wup, lora_w], w=[ps])
                                b.op("act", lambda e, ps=ps: e.activation(out=sg32.ap[:, blk(bi)], in_=ps.ap[:, :], func=AF.Sigmoid, bias=pvl(c0 + d)),
                                     r=[ps], w=[sg32])
                                ps2 = psA.next()
                                b.op("pe", lambda e, ps2=ps2: e.matmul(ps2.ap[:, :], lhsT=aup.ap[pr_, j * 128:(j + 1) * 128], rhs=lora_a.ap[pr_, blk(bi)],
                                                                      start=True, stop=True), r=[aup, lora_a], w=[ps2])
                                b.op("act", lambda e, ps2=ps2: e.activation(out=a_bf.ap[:, blk(bi)], in_=ps2.ap[:, :], func=AF.Sigmoid, bias=pvl(c0 + 2 + d)),
                                     r=[ps2], w=[a_bf])
                            b.op("dve", lambda e: e.tensor_tensor_scan(out=bs32.ap[:], data0=rst64[:], data1=rv(sg32.ap, d), initial=0.0,
                                                                       op0=ALU.mult, op1=ALU.add), r=[sg32], w=[bs32])
                            b.op("dve", lambda e: e.tensor_tensor(out=b_bf.ap[:], in0=kk_bf.ap[:], in1=a_bf.ap[:], op=ALU.mult), r=[kk_bf, a_bf], w=[b_bf])
                            b.op("dve", lambda e: e.tensor_scalar(out=a_bf.ap[:], in0=a_bf.ap[:], scalar1=pvl(c0 + 5), scalar2=der[:, l, j:j + 1],
                                                                  op0=ALU.mult, op1=ALU.add), r=[a_bf, b_bf], w=[a_bf])
                            b.op("dve", lambda e: e.tensor_tensor(out=kd_bf.ap[:], in0=k32.ap[:], in1=a_bf.ap[:], op=ALU.mult), r=[k32, a_bf], w=[kd_bf])
                            krv = KR[d].ap[:].rearrange("p c two i -> p two c i")
                            b.op("act", lambda e: e.activation(out=E.ap[:], in_=bs32.ap[:], func=AF.Exp, scale=CW), r=[bs32], w=[E])
                            b.op("dve", lambda e: e.tensor_tensor(out=krv[:, 1], in0=rv(r_bf.ap, d).rearrange("p (c i) -> p c i", i=64),
                                                                  in1=E.ap[:].rearrange("p (c i) -> p c i", i=64), op=ALU.mult), r=[r_bf, E], w=[KR[d]])
                            b.op("act", lambda e: e.activation(out=WL[d].ap[:], in_=bs32.ap[:].rearrange("p (c i) -> p c i", i=64)[:, :, 63],
                                                               func=AF.Exp, scale=CW), r=[bs32], w=[WL[d]])
                            b.op("act", lambda e: e.activation(out=E.ap[:], in_=bs32.ap[:], func=AF.Exp, scale=-CW), r=[bs32], w=[E])
                            b.op("dve", lambda e: e.tensor_tensor(out=KH[d].ap[:], in0=rv(kd_bf.ap, d), in1=E.ap[:], op=ALU.mult), r=[kd_bf, E], w=[KH[d]])
                            b.op("dve", lambda e: e.tensor_tensor(out=BH[d].ap[:], in0=rv(b_bf.ap, d), in1=E.ap[:], op=ALU.mult), r=[b_bf, E], w=[BH[d]])
                            b.op("dve", lambda e: e.tensor_tensor(out=t1.ap[:], in0=bs32.ap[:], in1=rv(sg32.ap, d), op=ALU.subtract), r=[bs32, sg32], w=[t1])
                            b.op("act", lambda e: e.activation(out=E.ap[:], in_=t1.ap[:], func=AF.Exp, scale=CW), r=[t1], w=[E])
                            b.op("dve", lambda e: e.tensor_tensor(out=krv[:, 0], in0=rv(kk_bf.ap, d).rearrange("p (c i) -> p c i", i=64),
                                                                  in1=E.ap[:].rearrange("p (c i) -> p c i", i=64), op=ALU.mult), r=[kk_bf, E], w=[KR[d]])
                            wlb = WL[d].ap[:].rearrange("p (c o) -> p c o", o=1).to_broadcast([128, 32, 64])
                            c3 = lambda t_: t_.ap[:].rearrange("p (c i) -> p c i", i=64)
                            b.op("dve", lambda e: e.tensor_tensor(out=c3(KB[d]), in0=c3(KH[d]), in1=wlb, op=ALU.mult), r=[KH[d], WL[d]], w=[KB[d]])
                            b.op("dve", lambda e: e.scalar_tensor_tensor(out=c3(BB[d]), in0=c3(BH[d]), scalar=-1.0, in1=wlb,
                                                                         op0=ALU.mult, op1=ALU.mult), r=[BH[d], WL[d]], w=[BB[d]])
                            b.op("dve", lambda e: e.memset(P32[d].ap[:], 0.0), w=[P32[d]])

                        b.op("dve", lambda e: e.scalar_tensor_tensor(out=s4p.ap[:], in0=k32.ap[:], scalar=pvl(c0 + 6), in1=r_bf.ap[:],
                                                                     op0=ALU.mult, op1=ALU.mult), r=[k32, r_bf, kk_bf], w=[s4p])

                        def bon_cons(bi, ps):
                            b.op("dve", lambda e: e.tensor_tensor(out=bon.ap[:, blk(bi)], in0=ps.ap[:, :], in1=v_bf.ap[:, blk(bi)], op=ALU.mult),
                                 r=[ps, v_bf], w=[bon])
                        headsum(s4p, bon_cons, None)
                        if l == 0 and j == 0:
                            ck(5, KR[1].ap[:].rearrange("p c two i -> p (c two i)"))
                            ck(50, BB[0].ap[:, :])
                            ck(51, kk_bf.ap[:, :])
                            ck(52, KH[1].ap[:, :])

                        def stageA(i):
                            par = i % 4
                            ts_ = i % 2
                            cs = slice(i * 64, (i + 1) * 64)
                            levs = []

                            def L0():
                                SK = ""
                                for d in range(2):
                                    if "T" in SK:
                                        break
                                    pt = psA.next()
                                    for q, src in enumerate((vdir[d], KB[d], BB[d])):
                                        b.op("pe", lambda e, q=q, src=src, pt=pt: e.matmul(pt.ap[0:64, q * 128:(q + 1) * 128], lhsT=src.ap[:, cs], rhs=identb[:], start=True, stop=True),
                                             r=[src], w=[pt])
                                    tk = tok[d][par]
                                    for h_ in range(2):
                                        b.op("act", lambda e, pt=pt, tk=tk, h_=h_: e.activation(out=tk.ap[:, :, h_, h_, :], in_=pt.ap[0:64, 0:384].rearrange("s (q h c) -> s q h c", q=3, h=2)[:, :, h_, :],
                                                                                        func=AF.Copy), r=[pt], w=[tk])
                                for u in range(4):
                                    if "M" in SK:
                                        break
                                    if "H" in SK and u % 2 == 1:
                                        continue
                                    d, hh = u // 2, u % 2
                                    pr = slice(hh * 64, hh * 64 + 64)
                                    ua = UA[(u, par)]
                                    kr = KR[d].ap[pr, i].rearrange("p two i -> p (two i)")
                                    krm = KRm[(d, hh, par)]
                                    b.op("pool", lambda e, kr=kr, krm=krm, pr=pr: e.tensor_copy(out=krm.ap[pr, :], in_=kr), r=[KR[d]], w=[krm])
                                    p1 = psB.next()
                                    b.op("pe", lambda e, p1=p1, krm=krm, d=d: e.matmul(p1.ap[0:64, 0:128], lhsT=BH[d].ap[:, cs], rhs=krm.ap[:, :], start=True, stop=True),
                                         r=[BH[d], krm], w=[p1])
                                    b.op("pe", lambda e, p1=p1, krm=krm, d=d: e.matmul(p1.ap[0:64, 128:256], lhsT=KH[d].ap[:, cs], rhs=krm.ap[:, :], start=True, stop=True),
                                         r=[KH[d], krm], w=[p1])
                                    b.op("pe", lambda e, p1=p1, krm=krm, d=d: e.matmul(p1.ap[0:64, 256:320], lhsT=krm.ap[:, 0:64], rhs=BH[d].ap[:, cs], start=True, stop=True),
                                         r=[BH[d], krm], w=[p1])
                                    b.op("dve", lambda e, p1=p1, ua=ua: e.tensor_tensor(out=ua["sc"].ap, in0=p1.ap[0:64, 0:320], in1=cst[0:64, 256:576], op=ALU.mult), r=[p1], w=[ua["sc"]])
                            levs.append(L0)
                            for k in range(1, 6):
                                def Lk(k=k):
                                    for pa in range(2):
                                        pu = psB.next()
                                        dA, dT = TRD[(ts_, pa, k)]
                                        o_ = (k - 1) * 192
                                        for u2 in range(2):
                                            u = pa * 2 + u2
                                            ua = UA[(u, par)]
                                            cb = u2 * 256
                                            if k == 1:
                                                lo_p, up_p, rdeps = ua["sc3"], ua["up0"], [ua["sc"]]
                                                b.op("pe", lambda e: e.matmul(pu.ap[0:64, cb:cb + 64], lhsT=lo_p.ap, rhs=up_p.ap, start=True, stop=True), r=rdeps, w=[pu])
                                            else:
                                                pv_ = UT[(ts_, u)][k - 1]
                                                lo_p, up_p, rdeps = pv_["lo"], pv_["up"], [pv_["dA"], pv_["dT"]]
                                                b.op("pe", lambda e: e.matmul(pu.ap[0:64, cb:cb + 128], lhsT=lo_p.ap, rhs=pv_["ut"].ap, start=True, stop=True), r=rdeps, w=[pu])
                                            b.op("pe", lambda e: e.matmul(pu.ap[0:64, cb + 128:cb + 192], lhsT=up_p.ap, rhs=lo_p.ap, start=True, stop=True), r=rdeps, w=[pu])
                                        pv4 = pu.ap[0:64, :].rearrange("p (u a b) -> p u a b", u=2, a=4)
                                        dst4 = UTv[ts_][:, pa * 2:pa * 2 + 2, o_:o_ + 192].rearrange("p u (a b) -> p u a b", a=3)
                                        b.op("act", lambda e: e.activation(out=dst4[:, :, 0::2, :], in_=pv4[:, :, 0:3:2, :], func=AF.Copy), r=[pu], w=[dA])
                                        if k == 1:
                                            for u2 in range(2):
                                                ua = UA[(pa * 2 + u2, par)]
                                                b.op("dve", lambda e: e.tensor_tensor(out=UT[(ts_, pa * 2 + u2)][1]["tt"].ap, in0=ua["up0"].ap, in1=identb[0:64, 0:64], op=ALU.add), r=[ua["sc"]], w=[dT])
                                        else:
                                            dAp, dTp = TRD[(ts_, pa, k - 1)]
                                            prev_tt = UTv[ts_][:, pa * 2:pa * 2 + 2, o_ - 192 + 64:o_ - 192 + 128]
                                            b.op("dve", lambda e: e.tensor_tensor(out=dst4[:, :, 1, :], in0=pv4[:, :, 1, :], in1=prev_tt, op=ALU.add), r=[pu, dTp, dA], w=[dT])
                                levs.append(Lk)

                            def L6():
                                pz = psB.next()
                                for u in range(4):
                                    pv_ = UT[(ts_, u)][5]
                                    b.op("pe", lambda e: e.matmul(pz.ap[0:64, u * 64:(u + 1) * 64], lhsT=pv_["lo"].ap, rhs=pv_["tt"].ap, start=True, stop=True), r=[pv_["dA"], pv_["dT"]], w=[pz])
                                tt5 = UTv[ts_][:, :, 4 * 192 + 64:4 * 192 + 128]
                                dst = UAt[par].rearrange("p (u r) -> p u r", u=4)[:, :, 320:384]
                                b.op("dve", lambda e: e.tensor_tensor(out=dst, in0=pz.ap[0:64, 0:256].rearrange("p (u c) -> p u c", u=4), in1=tt5, op=ALU.add),
                                     r=[pz, TRD[(ts_, 0, 5)][1], TRD[(ts_, 1, 5)][1]], w=[UAD[par]["ttf"]])
                            levs.append(L6)
                            return levs

                        def stageB(i):
                            par = i % 4
                            ts_ = i % 2
                            cs = slice(i * 64, (i + 1) * 64)
                            seg = i // 4
                            levs = []

                            def L0():
                                if i % 4 == 0:
                                    for d in range(2):
                                        pi = pin.next()
                                        b.dma("sp", pi.ap[:], irw_d[l, d, seg, j], w=[pi])
                                        b.op("dve", lambda e, pi=pi, d=d: e.scalar_tensor_tensor(out=P32[d].ap[:], in0=P32[d].ap[:], scalar=carry[:, 0:1], in1=pi.ap[:],
                                                                                             op0=ALU.mult, op1=ALU.add), r=[P32[d], pi], w=[P32[d]])
                                        b.op("dve", lambda e, d=d: e.tensor_tensor(out=Pbd[d].ap[:].rearrange("p (h v) -> p h v", h=2), in0=P32[d].ap[:].rearrange("p (o v) -> p o v", o=1).to_broadcast([128, 2, 64]),
                                                                           in1=cst[:, 128:256].rearrange("p (h v) -> p h v", h=2), op=ALU.mult), r=[P32[d]], w=[Pbd[d]])
                                for d in range(2):
                                    px = psB.next()
                                    for hh in range(2):
                                        u = d * 2 + hh
                                        ua = UA[(u, par)]
                                        tk = tok[d][par]
                                        krm = KRm[(d, hh, par)]
                                        b.op("pe", lambda e, krm=krm, d=d, hh=hh, px=px: e.matmul(px.ap[0:64, hh * 64:(hh + 1) * 64], lhsT=krm.ap[:, 0:64], rhs=Pbd[d].ap[:, hh * 64:(hh + 1) * 64], start=True, stop=False),
                                             r=[krm, Pbd[d]], w=[px])
                                        b.op("pe", lambda e, ua=ua, tk=tk, hh=hh, px=px: e.matmul(px.ap[0:64, hh * 64:(hh + 1) * 64], lhsT=ua["sc2"].ap[:, 0:64], rhs=tk.ap[:, 0, hh, hh, :],
                                                                                             start=False, stop=True), r=[ua["sc2"], tk], w=[px])
                                    dst = UAt[par].rearrange("p (u r) -> p u r", u=4)[:, d * 2:d * 2 + 2, 384:448]
                                    b.op("act", lambda e, px=px, dst=dst: e.activation(out=dst, in_=px.ap[0:64, 0:128].rearrange("p (u c) -> p u c", u=2), func=AF.Copy), r=[px], w=[UAD[par]["xs"][d]])
                            levs.append(L0)

                            def L1():
                                for d in range(2):
                                    py = psB.next()
                                    for hh in range(2):
                                        ua = UA[(d * 2 + hh, par)]
                                        b.op("pe", lambda e, ua=ua, hh=hh, py=py: e.matmul(py.ap[0:64, hh * 64:(hh + 1) * 64], lhsT=ua["ttf"].ap, rhs=ua["xs"].ap, start=True, stop=True),
                                             r=[ua["ttf"], ua["xs"]], w=[py])
                                    dst = UAt[par].rearrange("p (d r) -> p d r", d=2)[:, d, 448:1152].rearrange("p (a c) -> p a c", c=64)[:, 0::10, :]
                                    b.op("dve", lambda e, py=py, dst=dst: e.tensor_copy(out=dst, in_=py.ap[0:64, 0:128].rearrange("p (h c) -> p h c", h=2)), r=[py], w=[UAD[par]["y"][d]])
                            levs.append(L1)

                            def L7():
                                pps, pos = [], []
                                for d in range(2):
                                    tk = tok[d][par]
                                    pp = psB.next()
                                    pps.append(pp)
                                    for hh in range(2):
                                        ua = UA[(d * 2 + hh, par)]
                                        U = ua["y"][6]
                                        b.op("pe", lambda e, pp=pp, tk=tk, hh=hh: e.matmul(pp.ap[:, 0:64], lhsT=tk.ap[:, 1, hh].rearrange("s h c -> s (h c)"), rhs=tk.ap[:, 0, hh, hh, :],
                                                                                      start=(hh == 0), stop=False), r=[tk], w=[pp])
                                        b.op("pe", lambda e, pp=pp, tk=tk, hh=hh, U=U: e.matmul(pp.ap[:, 0:64], lhsT=tk.ap[:, 2, hh].rearrange("s h c -> s (h c)"), rhs=U.ap, start=False, stop=(hh == 1)),
                                             r=[tk, U], w=[pp])
                                for d in range(2):
                                    tk = tok[d][par]
                                    po = psB.next()
                                    pos.append(po)
                                    b.op("pe", lambda e, po=po, d=d: e.matmul(po.ap[:, 0:64], lhsT=Pbd[d].ap[:, :], rhs=KR[d].ap[:, i, 1, :], start=True, stop=False),
                                         r=[Pbd[d], KR[d]], w=[po])
                                    for hh in range(2):
                                        ua = UA[(d * 2 + hh, par)]
                                        b.op("pe", lambda e, po=po, tk=tk, ua=ua, hh=hh: e.matmul(po.ap[:, 0:64], lhsT=tk.ap[:, 0, hh].rearrange("s h c -> s (h c)"), rhs=ua["sc2"].ap[:, 64:128],
                                                                                             start=False, stop=False), r=[tk, ua["sc2"]], w=[po])
                                        b.op("pe", lambda e, po=po, ua=ua, hh=hh: e.matmul(po.ap[:, 0:64], lhsT=ua["upad"].ap, rhs=ua["sc1"].ap[:, 64:128], start=False, stop=(hh == 1)),
                                             r=[ua["upad"], ua["sc1"]], w=[po])
                                for d in range(2):
                                    pp = pps[d]
                                    b.op("dve", lambda e, pp=pp, d=d: e.scalar_tensor_tensor(out=P32[d].ap[:], in0=P32[d].ap[:], scalar=WL[d].ap[:, i:i + 1], in1=pp.ap[:, 0:64],
                                                                                         op0=ALU.mult, op1=ALU.add), r=[pp, P32[d], WL[d]], w=[P32[d]])
                                    b.op("dve", lambda e, d=d: e.tensor_tensor(out=Pbd[d].ap[:].rearrange("p (h v) -> p h v", h=2), in0=P32[d].ap[:].rearrange("p (o v) -> p o v", o=1).to_broadcast([128, 2, 64]),
                                                                           in1=cst[:, 128:256].rearrange("p (h v) -> p h v", h=2), op=ALU.mult), r=[P32[d]], w=[Pbd[d]])
                                for d in range(2):
                                    po = pos[d]
                                    nat = (31 - i) if d else i
                                    oc = o32[:, nat * 64:(nat + 1) * 64]
                                    ocv = oc[:, ::-1] if d else oc
                                    b.op("dve", lambda e, po=po, ocv=ocv: e.tensor_tensor(out=ocv, in0=po.ap[:, 0:64], in1=ocv, op=ALU.add), r=[po, d_o[nat]], w=[d_o[nat]])
                                    if i % 4 == 3:
                                        po_ = pout.next()
                                        b.op("act", lambda e, d=d, po_=po_: e.activation(out=po_.ap[:], in_=P32[d].ap[:], func=AF.Copy), r=[P32[d]], w=[po_])
                                        b.dma("sp", frw_d[l, d, seg, j], po_.ap[:], r=[po_])
                            levs.append(L7)
                            return levs

                        b.barrier()
                        b.op("dve", lambda e: e.memset(sgb[0:64, 0:3840], 0.0), w=[d0])
                        b.op("dve", lambda e: e.memset(bsb[0:64, 0:3840], 0.0), w=[d0])
                        b.op("dve", lambda e: e.memset(bbb[:, 0:1024], 0.0), w=[d0])
                        b.barrier()
                        A0, A1 = stageA(0), stageA(1)
                        for lev in range(7):
                            A0[lev]()
                            A1[lev]()
                        for i in range(0, 32, 2):
                            An1 = stageA(i + 2) if i + 2 < 32 else []
                            An2 = stageA(i + 3) if i + 3 < 32 else []
                            B1, B2 = stageB(i), stageB(i + 1)
                            for lev in range(7):
                                if lev < 3:
                                    B1[lev]()
                                elif lev < 6:
                                    B2[lev - 3]()
                                if lev < len(An1):
                                    An1[lev]()
                                if lev < len(An2):
                                    An2[lev]()
                        b.barrier()

                        if l == 0 and j == 0:
                            ck(6, o32[:, :])
                        d_all = d_o
                        b.op("act", lambda e: e.activation(out=s4p.ap[:], in_=o32[:], func=AF.Copy), r=d_all, w=[s4p])

                        def mu_cons(bi, ps):
                            b.op("dve", lambda e: e.scalar_tensor_tensor(out=o32[:, blk(bi)], in0=ps.ap[:, :], scalar=-1.0 / 64, in1=o32[:, blk(bi)],
                                                                         op0=ALU.mult, op1=ALU.add), r=[ps] + d_all, w=[d_all[0]])
                        headsum(s4p, mu_cons, None)
                        b.op("act", lambda e: e.activation(out=s4p.ap[:], in_=o32[:], func=AF.Square), r=[d_all[0]], w=[s4p])

                        def var_cons(bi, ps):
                            b.op("act", lambda e: e.activation(out=t1.ap[:, blk(bi)], in_=ps.ap[:, :], func=AF.Ln, scale=1.0 / 64, bias=64e-5), r=[ps], w=[t1])
                            b.op("act", lambda e: e.activation(out=t1.ap[:, blk(bi)], in_=t1.ap[:, blk(bi)], func=AF.Exp, scale=-0.5), r=[t1], w=[t1])
                            b.op("dve", lambda e: e.tensor_tensor(out=o32[:, blk(bi)], in0=o32[:, blk(bi)], in1=t1.ap[:, blk(bi)], op=ALU.mult), r=[t1, d_all[0]], w=[d_all[0]])
                            b.op("dve", lambda e: e.tensor_scalar(out=o32[:, blk(bi)], in0=o32[:, blk(bi)], scalar1=pvl(c0 + 7), scalar2=pvl(c0 + 8), op0=ALU.mult, op1=ALU.add),
                                 r=[d_all[0]], w=[d_all[0]])
                            b.op("dve", lambda e: e.tensor_tensor(out=o32[:, blk(bi)], in0=o32[:, blk(bi)], in1=bon.ap[:, blk(bi)], op=ALU.add), r=[bon, d_all[0]], w=[d_all[0]])
                            pg = psA.next()
                            b.op("pe", lambda e, pg=pg: e.matmul(pg.ap[:, :], lhsT=gup.ap[:, j * 128:(j + 1) * 128], rhs=lora_g.ap[:, blk(bi)], start=True, stop=True),
                                 r=[gup, lora_g], w=[pg])
                            b.op("dve", lambda e, pg=pg: e.tensor_tensor(out=mst.ap[:, blk(bi)], in0=pg.ap[:, :], in1=o32[:, blk(bi)], op=ALU.mult), r=[pg, d_all[0]], w=[mst])
                        headsum(s4p, var_cons, None)
                        b.dma("sp", mixscr[j], mst.ap[:], r=[mst], w=[d_mix[j]])
                        if l == 0 and j == 0:
                            ck(7, mst.ap[:, :])
                b.barrier()

                with contextlib.ExitStack() as lp:
                    mst = tl("mst", [128, T], BF16, lp)
                    xpad = tl("xpad", [128, 8, 259], F32, lp)
                    u32 = tl("u32", [128, T], F32, lp)
                    u_bf = tl("u_bf", [128, T], BF16, lp)
                    gb32 = tl("gb32", [128, T], F32, lp)
                    gt = tl("gt", [128, T], F32, lp)
                    gel = tl("gel", [128, T], F32, lp)
                    rg_ = [tl("rg%d" % d_, [128, T], F32, lp) for d_ in range(2)]
                    ig_ = [tl("ig%d" % d_, [128, T], F32, lp) for d_ in range(2)]
                    a32_ = [tl("a32%d" % d_, [128, T], F32, lp) for d_ in range(2)]
                    m32_ = [tl("m32%d" % d_, [128, T], F32, lp) for d_ in range(2)]
                    hh_ = [tl("hf", [128, T], F32, lp), tl("hb", [128, T], F32, lp)]
                    wab = [[tl("wab%d%d" % (d, q), [128, 128], BF16, lp) for q in range(2)] for d in range(2)]
                    h0 = rot("h0", 4, [128, 1], F32, lp)
                    for j in range(2):
                        c0 = PV_LRU + j * 11
                        for d in range(2):
                            for q, src in enumerate((lwa_d, lwx_d)):
                                w_ = wab[d][q]
                                b.op("dve", lambda e, w_=w_: e.memset(w_.ap[:], 0.0), w=[w_])
                                for hb in range(2):
                                    b.dma("pool", w_.ap[hb * 64:(hb + 1) * 64, hb * 64:(hb + 1) * 64], src[l, d, j * 2 + hb], w=[w_])
                        b.op("dve", lambda e: e.memset(xpad.ap[:], 0.0), w=[xpad])

                        def xb_evac(bi, ps):
                            b.op("act", lambda e: e.activation(out=xpad.ap[:, bi * 2:bi * 2 + 2, 2:258], in_=ps.ap[:, :].rearrange("p (s t) -> p s t", s=2), func=AF.Copy),
                                 r=[ps], w=[xpad])
                        proj(l, 1920 + j * 128, 128, xb_evac)
                        proj(l, 2176 + j * 128, 128, act_evac(gb32))
                        b.op("dve", lambda e: e.tensor_scalar(out=xpad.ap[:, 1:8, 0:2], in0=xpad.ap[:, 0:7, 256:258], scalar1=carry[:, 0:1], scalar2=None, op0=ALU.mult),
                             r=[xpad], w=[xpad])
                        b.op("dve", lambda e: e.tensor_scalar(out=xpad.ap[:, 0:7, 258:259], in0=xpad.ap[:, 1:8, 2:3], scalar1=carry[:, 0:1], scalar2=None, op0=ALU.mult),
                             r=[xpad], w=[xpad])
                        u3 = u32.ap[:].rearrange("p (s t) -> p s t", s=8)
                        b.op("dve", lambda e: e.tensor_scalar(out=u3, in0=xpad.ap[:, :, 0:256], scalar1=pvl(c0), scalar2=pvl(c0 + 4), op0=ALU.mult, op1=ALU.add),
                             r=[xpad], w=[u32])
                        for k in range(1, 4):
                            b.op("dve", lambda e, k=k: e.scalar_tensor_tensor(out=u3, in0=xpad.ap[:, :, k:k + 256], scalar=pvl(c0 + k), in1=u3, op0=ALU.mult, op1=ALU.add),
                                 r=[xpad, u32], w=[u32])
                        b.op("act", lambda e: e.activation(out=u_bf.ap[:], in_=u32.ap[:], func=AF.Copy), r=[u32], w=[u_bf])
                        b.op("act", lambda e: e.activation(out=gt.ap[:], in_=gb32.ap[:], func=AF.Square), r=[gb32], w=[gt])
                        b.op("dve", lambda e: e.tensor_scalar(out=gt.ap[:], in0=gt.ap[:], scalar1=0.044715, scalar2=1.0, op0=ALU.mult, op1=ALU.add), r=[gt], w=[gt])
                        b.op("dve", lambda e: e.tensor_tensor(out=gt.ap[:], in0=gt.ap[:], in1=gb32.ap[:], op=ALU.mult), r=[gt, gb32], w=[gt])
                        b.op("act", lambda e: e.activation(out=gt.ap[:], in_=gt.ap[:], func=AF.Sigmoid, scale=1.5957691216057308), r=[gt], w=[gt])
                        b.op("dve", lambda e: e.tensor_tensor(out=gel.ap[:], in0=gt.ap[:], in1=gb32.ap[:], op=ALU.mult), r=[gt, gb32], w=[gel])
                        for d in range(2):
                            rg, ig, a32, m32 = rg_[d], ig_[d], a32_[d], m32_[d]
                            for bi in range(4):
                                ps = psA.next()
                                b.op("pe", lambda e, ps=ps: e.matmul(ps.ap[:, :], lhsT=wab[d][0].ap[:], rhs=u_bf.ap[:, blk(bi)], start=True, stop=True), r=[wab[d][0], u_bf], w=[ps])
                                b.op("act", lambda e, ps=ps: e.activation(out=rg.ap[:, blk(bi)], in_=ps.ap[:, :], func=AF.Sigmoid, bias=pvl(c0 + 5 + d)), r=[ps], w=[rg])
                                ps2 = psA.next()
                                b.op("pe", lambda e, ps2=ps2: e.matmul(ps2.ap[:, :], lhsT=wab[d][1].ap[:], rhs=u_bf.ap[:, blk(bi)], start=True, stop=True), r=[wab[d][1], u_bf], w=[ps2])
                                b.op("act", lambda e, ps2=ps2: e.activation(out=ig.ap[:, blk(bi)], in_=ps2.ap[:, :], func=AF.Sigmoid, bias=pvl(c0 + 7 + d)), r=[ps2], w=[ig])
                            sc = der[:, l, 4 + j * 2 + d:5 + j * 2 + d]
                            sc2 = der[:, l, 8 + j * 2 + d:9 + j * 2 + d]
                            b.op("act", lambda e: e.activation(out=a32.ap[:], in_=rg.ap[:], func=AF.Exp, scale=sc), r=[rg], w=[a32])
                            b.op("act", lambda e: e.activation(out=m32.ap[:], in_=rg.ap[:], func=AF.Exp, scale=sc2), r=[rg], w=[m32])
                            b.op("act", lambda e: e.activation(out=m32.ap[:], in_=m32.ap[:], func=AF.Sqrt, scale=-1.0, bias=1.0), r=[m32], w=[m32])
                            b.op("dve", lambda e: e.tensor_tensor(out=m32.ap[:], in0=m32.ap[:], in1=ig.ap[:], op=ALU.mult), r=[m32, ig], w=[m32])
                            b.op("dve", lambda e: e.tensor_tensor(out=m32.ap[:], in0=m32.ap[:], in1=u32.ap[:], op=ALU.mult), r=[m32, u32], w=[m32])
                            hd = hh_[d]
                            prev = None
                            for s in range(8):
                                nat = (7 - s) if d else s
                                sl = slice(nat * 256, (nat + 1) * 256)
                                hi = h0.next()
                                icol = ilru[:, l, j, s * 2 + d:s * 2 + d + 1]
                                if prev is None:
                                    b.op("dve", lambda e, hi=hi, icol=icol: e.tensor_copy(out=hi.ap[:], in_=icol), w=[hi])
                                else:
                                    b.op("dve", lambda e, hi=hi, icol=icol, prev=prev: e.scalar_tensor_tensor(out=hi.ap[:], in0=prev, scalar=carry[:, 0:1], in1=icol,
                                                                                                     op0=ALU.mult, op1=ALU.add), r=[hd], w=[hi])
                                b.op("dve", lambda e, hi=hi, sl=sl: e.tensor_tensor_scan(out=rv(hd.ap[:, sl], d), data0=rv(a32.ap[:, sl], d), data1=rv(m32.ap[:, sl], d),
                                                                                    initial=hi.ap[:, 0:1], op0=ALU.mult, op1=ALU.add), r=[a32, m32, hi], w=[hd])
                                last = (nat * 256) if d else (nat * 256 + 255)
                                prev = hd.ap[:, last:last + 1]
                                b.op("act", lambda e, prev=prev, s=s: e.activation(out=flru[:, l, j, s * 2 + d:s * 2 + d + 1], in_=prev, func=AF.Copy), r=[hd], w=[d_flru])
                        b.op("dve", lambda e: e.tensor_tensor(out=hh_[0].ap[:], in0=hh_[0].ap[:], in1=hh_[1].ap[:], op=ALU.add), r=[hh_[0], hh_[1]], w=[hh_[0]])
                        b.op("dve", lambda e: e.tensor_tensor(out=mst.ap[:], in0=hh_[0].ap[:], in1=gel.ap[:], op=ALU.mult), r=[hh_[0], gel], w=[mst])
                        b.dma("sp", mixscr[4 + j], mst.ap[:], r=[mst], w=[d_mix[4 + j]])
                        if l == 0 and j == 0:
                            ck(9, mst.ap[:, :])
                b.barrier()

                with contextlib.ExitStack() as hp:
                    rst32 = sb("rst32", [128, T], BF16, hp)
                    b.op("dve", lambda e: e.memset(rst32[:], 1.0), w=[d0])
                    b.op("dve", lambda e: e.memset(rst32[:].rearrange("p (c i) -> p c i", i=64)[:, :, 0:1], 0.0), w=[d0])
                    mst = tl("mst", [128, T], BF16, hp)
                    qs = tl("qs", [128, T], BF16, hp)
                    fr = tl("fr", [128, T], F32, hp)
                    v_bf = tl("hv_bf", [128, T], BF16, hp)
                    ogs = tl("ogs", [128, T], BF16, hp)
                    f32_ = tl("f32_", [128, T], F32, hp)
                    lf = tl("lf", [128, T], F32, hp)
                    kq = tl("kq", [128, T], BF16, hp)
                    bs32 = tl("hbs32", [128, T], F32, hp)
                    t1 = tl("ht1", [128, T], F32, hp)
                    E = tl("hE", [128, T], BF16, hp)
                    vdir = [v_bf, tl("hvrev", [128, T], BF16, hp)]
                    QT = [tl("QT%d" % d, [128, T], BF16, hp) for d in range(2)]
                    KH = [tl("hKH%d" % d, [128, T], BF16, hp) for d in range(2)]
                    KB = [tl("hKB%d" % d, [128, T], BF16, hp) for d in range(2)]
                    WL = [tl("hWL%d" % d, [128, 32], F32, hp) for d in range(2)]
                    WLm = tl("hWLm", [128, 32], F32, hp)
                    QTa = [tl("QTa%d" % d, [128, T], BF16, hp) for d in range(2)]
                    o32 = sb("ho32", [128, T], F32, hp)
                    s4 = tl("hs4", [128, T], BF16, hp)
                    P32 = [tl("hP32%d" % d, [128, 64], F32, hp) for d in range(2)]
                    Pbf = [tl("hPbf%d" % d, [128, 64], BF16, hp) for d in range(2)]
                    pin = rot("hpin", 4, [128, 64], F32, hp)
                    pout = rot("hpout", 4, [128, 64], F32, hp)
                    tok = rot("htok", 6, [64, 2, 2, 2, 64], BF16, hp)
                    Pbd = [tl("hPbd%d" % d, [128, 128], BF16, hp) for d in range(2)]
                    for t_ in tok.tl + Pbd:
                        b.op("dve", lambda e, t_=t_: e.memset(t_.ap[:], 0.0), w=[t_])
                    scb_ = rot("hsc", 6, [64, 128], BF16, hp)
                    hclamp = rot("hclamp", 3, [64, 128], F32, hp)
                    qbd = rot("qbd", 6, [128, 128], BF16, hp)
                    for t_ in qbd.tl:
                        b.op("dve", lambda e, t_=t_: e.memset(t_.ap[:], 0.0), w=[t_])
                    d_o = [Dep() for _ in range(32)]
                    for j in range(2):
                        c0 = PV_HG + j * 5
                        proj(l, 2432 + j * 128, 128, act_evac(qs, AF.Silu))
                        proj(l, 3200 + j * 128, 128, act_evac(v_bf))
                        proj(l, 3456 + j * 128, 128, act_evac(ogs, AF.Silu))
                        b.op("dve", lambda e: e.memset(o32[:], 0.0), w=d_o)
                        b.op("act", lambda e: e.activation(out=vdir[1].ap[:], in_=v_bf.ap[:, ::-1], func=AF.Copy), r=[v_bf], w=[vdir[1]])
                        for d in range(2):
                            proj(l, 2688 + d * 256 + j * 128, 128, act_evac(fr, AF.Sigmoid))
                            lb = der[:, l, 12 + j * 2 + d:13 + j * 2 + d]
                            oml = der[:, l, 16 + j * 2 + d:17 + j * 2 + d]
                            b.op("dve", lambda e: e.tensor_scalar(out=f32_.ap[:], in0=fr.ap[:], scalar1=oml, scalar2=lb, op0=ALU.mult, op1=ALU.add), r=[fr], w=[f32_])
                            b.op("act", lambda e: e.activation(out=lf.ap[:], in_=f32_.ap[:], func=AF.Ln), r=[f32_], w=[lf])
                            b.op("dve", lambda e: e.tensor_scalar(out=kq.ap[:], in0=f32_.ap[:], scalar1=-1.0, scalar2=1.0, op0=ALU.mult, op1=ALU.add), r=[f32_], w=[kq])
                            b.op("dve", lambda e: e.tensor_tensor_scan(out=bs32.ap[:], data0=rst32[:], data1=rv(lf.ap, d), initial=0.0, op0=ALU.mult, op1=ALU.add),
                                 r=[lf], w=[bs32])
                            bs3 = bs32.ap[:].rearrange("p (c i) -> p c i", i=64)
                            c3 = lambda t_: t_.ap[:].rearrange("p (c i) -> p c i", i=64)
                            b.op("act", lambda e: e.activation(out=E.ap[:], in_=bs32.ap[:], func=AF.Exp), r=[bs32], w=[E])
                            b.op("dve", lambda e: e.tensor_tensor(out=QTa[d].ap[:], in0=rv(qs.ap, d), in1=E.ap[:], op=ALU.mult), r=[qs, E], w=[QTa[d]])
                            b.op("act", lambda e: e.activation(out=WL[d].ap[:], in_=bs3[:, :, 63], func=AF.Exp), r=[bs32], w=[WL[d]])
                            b.op("dve", lambda e: e.tensor_tensor(out=WLm.ap[:], in0=bs3[:, :, 63], in1=bs3[:, :, 31], op=ALU.subtract), r=[bs32], w=[WLm])
                            b.op("act", lambda e: e.activation(out=WLm.ap[:], in_=WLm.ap[:], func=AF.Exp), r=[WLm], w=[WLm])
                            b.op("dve", lambda e: e.tensor_tensor(out=c3(t1), in0=bs3, in1=bs3[:, :, 31:32].to_broadcast([128, 32, 64]), op=ALU.subtract), r=[bs32], w=[t1])
                            b.op("act", lambda e: e.activation(out=E.ap[:], in_=t1.ap[:], func=AF.Exp), r=[t1, QTa[d]], w=[E])
                            b.op("dve", lambda e: e.tensor_tensor(out=QT[d].ap[:], in0=rv(qs.ap, d), in1=E.ap[:], op=ALU.mult), r=[qs, E], w=[QT[d]])
                            b.op("act", lambda e: e.activation(out=E.ap[:], in_=t1.ap[:], func=AF.Exp, scale=-1.0), r=[t1, QT[d]], w=[E])
                            b.op("dve", lambda e: e.tensor_tensor(out=KH[d].ap[:], in0=rv(kq.ap, d), in1=E.ap[:], op=ALU.mult), r=[kq, E], w=[KH[d]])
                            wlb = WLm.ap[:].rearrange("p (c o) -> p c o", o=1).to_broadcast([128, 32, 64])
                            b.op("dve", lambda e: e.tensor_tensor(out=c3(KB[d]), in0=c3(KH[d]), in1=wlb, op=ALU.mult), r=[KH[d], WLm], w=[KB[d]])
                            b.op("dve", lambda e: e.memset(P32[d].ap[:], 0.0), w=[P32[d]])
                        hA = {}

                        def hgA(i, d):
                            cs = slice(i * 64, (i + 1) * 64)
                            qb = qbd.next()
                            for hh in range(2):
                                pr = slice(hh * 64, hh * 64 + 64)
                                b.op("pool", lambda e, hh=hh, pr=pr: e.tensor_copy(out=qb.ap[pr, hh * 64:(hh + 1) * 64], in_=QT[d].ap[pr, cs]), r=[QT[d]], w=[qb])
                            p1 = psB.next()
                            b.op("pe", lambda e: e.matmul(p1.ap[0:64, 0:128], lhsT=KH[d].ap[:, cs], rhs=qb.ap[:, :], start=True, stop=True), r=[KH[d], qb], w=[p1])
                            sc = scb_.next()
                            ctm = hclamp.next()
                            b.op("dve", lambda e: e.tensor_scalar(out=ctm.ap[:], in0=p1.ap[0:64, 0:128], scalar1=1e30, scalar2=-1e30, op0=ALU.min, op1=ALU.max), r=[p1], w=[ctm])
                            b.op("dve", lambda e: e.tensor_tensor(out=sc.ap[:].rearrange("p (h t) -> p h t", h=2), in0=ctm.ap[:].rearrange("p (h t) -> p h t", h=2),
                                                                  in1=cst[0:64, 448:512].rearrange("p (o t) -> p o t", o=1).to_broadcast([64, 2, 64]), op=ALU.mult), r=[ctm], w=[sc])
                            pt = psA.next()
                            for q, src in enumerate((vdir[d], KB[d])):
                                b.op("pe", lambda e, q=q, src=src: e.matmul(pt.ap[0:64, q * 128:(q + 1) * 128], lhsT=src.ap[:, cs], rhs=identb[:], start=True, stop=True), r=[src], w=[pt])
                            tk = tok.next()
                            for h_ in range(2):
                                b.op("act", lambda e, h_=h_: e.activation(out=tk.ap[:, :, h_, h_, :], in_=pt.ap[0:64, 0:256].rearrange("s (q h c) -> s q h c", q=2, h=2)[:, :, h_, :], func=AF.Copy),
                                     r=[pt], w=[tk])
                            hA[(i, d)] = (sc, tk)

                        def hgB(i, d):
                            cs = slice(i * 64, (i + 1) * 64)
                            seg = i // 4
                            sc, tk = hA.pop((i, d))
                            if i % 4 == 0:
                                pi = pin.next()
                                b.dma("sp", pi.ap[:], ihg_d[l, d, seg, j], w=[pi])
                                b.op("dve", lambda e: e.scalar_tensor_tensor(out=P32[d].ap[:], in0=P32[d].ap[:], scalar=carry[:, 0:1], in1=pi.ap[:],
                                                                             op0=ALU.mult, op1=ALU.add), r=[P32[d], pi], w=[P32[d]])
                                b.op("dve", lambda e: e.tensor_tensor(out=Pbd[d].ap[:].rearrange("p (h v) -> p h v", h=2), in0=P32[d].ap[:].rearrange("p (o v) -> p o v", o=1).to_broadcast([128, 2, 64]),
                                                                      in1=cst[:, 128:256].rearrange("p (h v) -> p h v", h=2), op=ALU.mult), r=[P32[d]], w=[Pbd[d]])
                            po = psB.next()
                            pp = Tl(po.ap[:, 64:128])
                            pp.d = po.d
                            pq = psB.next()
                            pp = Tl(pq.ap[:, 64:128])
                            pp.d = pq.d
                            for hh in range(2):
                                b.op("pe", lambda e, hh=hh: e.matmul(pp.ap[:, 0:64], lhsT=tk.ap[:, 1, hh].rearrange("s h c -> s (h c)"), rhs=tk.ap[:, 0, hh, hh, :], start=(hh == 0), stop=(hh == 1)),
                                     r=[tk], w=[pp])
                            b.op("pe", lambda e: e.matmul(po.ap[:, 0:64], lhsT=Pbd[d].ap[:, :], rhs=QTa[d].ap[:, cs], start=True, stop=False), r=[Pbd[d], QTa[d]], w=[po])
                            for hh in range(2):
                                b.op("pe", lambda e, hh=hh: e.matmul(po.ap[:, 0:64], lhsT=tk.ap[:, 0, hh].rearrange("s h c -> s (h c)"), rhs=sc.ap[:, hh * 64:(hh + 1) * 64], start=False, stop=(hh == 1)),
                                     r=[tk, sc], w=[po])
                            nat = (31 - i) if d else i
                            oc = o32[:, nat * 64:(nat + 1) * 64]
                            ocv = oc[:, ::-1] if d else oc
                            b.op("dve", lambda e: e.scalar_tensor_tensor(out=P32[d].ap[:], in0=P32[d].ap[:], scalar=WL[d].ap[:, i:i + 1], in1=pp.ap[:, 0:64],
                                                                         op0=ALU.mult, op1=ALU.add), r=[pp, P32[d], WL[d]], w=[P32[d]])
                            b.op("dve", lambda e: e.tensor_tensor(out=ocv, in0=po.ap[:, 0:64], in1=ocv, op=ALU.add), r=[po, d_o[nat]], w=[d_o[nat]])
                            b.op("dve", lambda e: e.tensor_tensor(out=Pbd[d].ap[:].rearrange("p (h v) -> p h v", h=2), in0=P32[d].ap[:].rearrange("p (o v) -> p o v", o=1).to_broadcast([128, 2, 64]),
                                                                      in1=cst[:, 128:256].rearrange("p (h v) -> p h v", h=2), op=ALU.mult), r=[P32[d]], w=[Pbd[d]])
                            if i % 4 == 3:
                                po_ = pout.next()
                                b.op("act", lambda e: e.activation(out=po_.ap[:], in_=P32[d].ap[:], func=AF.Copy), r=[P32[d]], w=[po_])
                                b.dma("sp", fhg_d[l, d, seg, j], po_.ap[:], r=[po_])

                        hgA(0, 0)
                        hgA(0, 1)
                        for i in range(32):
                            if i < 31:
                                hgA(i + 1, 0)
                                hgA(i + 1, 1)
                            hgB(i, 0)
                            hgB(i, 1)
                        b.op("act", lambda e: e.activation(out=s4.ap[:], in_=o32[:], func=AF.Square), r=d_o, w=[s4])

                        def hv_cons(bi, ps):
                            b.op("act", lambda e: e.activation(out=t1.ap[:, blk(bi)], in_=ps.ap[:, :], func=AF.Ln, scale=1.0 / 64, bias=1e-6), r=[ps], w=[t1])
                            b.op("act", lambda e: e.activation(out=t1.ap[:, blk(bi)], in_=t1.ap[:, blk(bi)], func=AF.Exp, scale=-0.5), r=[t1], w=[t1])
                            b.op("dve", lambda e: e.scalar_tensor_tensor(out=t1.ap[:, blk(bi)], in0=t1.ap[:, blk(bi)], scalar=pvl(c0 + 4), in1=o32[:, blk(bi)],
                                                                         op0=ALU.mult, op1=ALU.mult), r=[t1] + d_o, w=[t1])
                            b.op("dve", lambda e: e.tensor_tensor(out=mst.ap[:, blk(bi)], in0=t1.ap[:, blk(bi)], in1=ogs.ap[:, blk(bi)], op=ALU.mult), r=[t1, ogs], w=[mst])
                        headsum(s4, hv_cons, None)
                        b.dma("sp", mixscr[6 + j], mst.ap[:], r=[mst], w=[d_mix[6 + j]])
                        if l == 0 and j == 0:
                            ck(10, mst.ap[:, :])
            b.barrier()

            xs = contextlib.ExitStack()
            xTt = sb("xT", [128, 8, T], F32, xs)
            ms = contextlib.ExitStack()
            mixT = sb("mixT", [128, 8, T], BF16, ms)
            for ft in range(8):
                b.dma("sp", mixT[:, ft, :], mixscr[ft], r=[d_mix[ft]], w=[d_mix[ft]])
            for bi in range(4):
                b.dma("sp", xTt[:, :, blk(bi)], xscr[:, :, blk(bi)], w=[d_x[bi]])
            wts = [load_w(w_out_d[l], slice(dj * 128, (dj + 1) * 128)) for dj in range(3)]
            for dj in range(8):
                wt = wts[dj % 3]
                for bi in range(4):
                    ps = psA.next()
                    for ft in range(8):
                        b.op("pe", lambda e, ft=ft, ps=ps, wt=wt: e.matmul(ps.ap[:, :], lhsT=wt.ap[:, ft, :], rhs=mixT[:, ft, blk(bi)], start=(ft == 0), stop=(ft == 7)),
                             r=[wt] + d_mix, w=[ps])
                    b.op("dve", lambda e, ps=ps: e.scalar_tensor_tensor(out=xTt[:, dj, blk(bi)], in0=ps.ap[:, :], scalar=modt[:, l, 16 + dj:17 + dj], in1=xTt[:, dj, blk(bi)],
                                                                    op0=ALU.mult, op1=ALU.add), r=[ps, d_x[bi][dj]], w=[d_x[bi][dj]])
                if dj + 3 < 8:
                    wts[dj % 3] = load_w(w_out_d[l], slice((dj + 3) * 128, (dj + 4) * 128))
            b.barrier()
            if l == 0:
                ck(11, xTt[:, 0, :])
            ms.close()
            fpw = contextlib.ExitStack()
            wgt = rot("wgt", 2, [128, 8, 512], BF16, fpw)
            wut = rot("wut", 2, [128, 8, 512], BF16, fpw)
            wdt = rot("wdt", 3, [128, 4, D], BF16, fpw)

            def ffn_wload(c0g, ng):
                wg_, wu_, wd_ = wgt.next(), wut.next(), wdt.next()
                cs = slice(c0g * 128, (c0g + ng) * 128)
                b.dma("pool", wg_.ap[:, :, 0:ng * 128], wg_d[l].rearrange("(j p) c -> p j c", p=128)[:, :, cs], w=[wg_])
                b.dma("pool", wu_.ap[:, :, 0:ng * 128], wu_d[l].rearrange("(j p) c -> p j c", p=128)[:, :, cs], w=[wu_])
                b.dma("pool", wd_.ap[:, 0:ng, :], wd_d[l, c0g * 128:(c0g + ng) * 128, :].rearrange("(c p) d -> p c d", p=128), w=[wd_])
                return wg_, wu_, wd_
            pre_w = ffn_wload(0, 4)
            with contextlib.ExitStack() as ph:
                norm_phase(xTt, ph, None, None, to_hT(l, 1))
            b.barrier()
            with contextlib.ExitStack() as fp:
                gpad = rot("gpad", 3, [128, 8, 66], F32, fp)
                acc = rot("acc", 3, [128, 512], F32, fp)
                sgt = rot("sgt", 3, [128, 512], F32, fp)
                actT = rot("actT", 3, [128, 4, 512], BF16, fp)
                for g_ in gpad.tl:
                    b.op("dve", lambda e, g_=g_: e.memset(g_.ap[:], 0.0), w=[g_])
                groups = [(0, 4), (4, 4), (8, 4), (12, 4), (16, 4), (20, 2)]
                pend = []

                def down_proj(wd_, at, ng, bi):
                    for dj in range(8):
                        ps = psA.next()
                        for ci in range(ng):
                            b.op("pe", lambda e, ci=ci, ps=ps: e.matmul(ps.ap[:, :], lhsT=wd_.ap[:, ci, dj * 128:(dj + 1) * 128], rhs=at.ap[:, ci, :], start=(ci == 0), stop=(ci == ng - 1)),
                                 r=[wd_, at], w=[ps])
                        b.op("dve", lambda e, ps=ps: e.scalar_tensor_tensor(out=xTt[:, dj, blk(bi)], in0=ps.ap[:, :], scalar=modt[:, l, 40 + dj:41 + dj], in1=xTt[:, dj, blk(bi)],
                                                                        op0=ALU.mult, op1=ALU.add), r=[ps, d_x[bi][dj]], w=[d_x[bi][dj]])

                for (c0g, ng) in groups:
                    wg_, wu_, wd_ = pre_w if c0g == 0 else ffn_wload(c0g, ng)
                    for bi in range(4):
                        at = actT.next()
                        for ci in range(ng):
                            c = c0g + ci
                            pg, pu = psA.next(), psA.next()
                            for jj in range(8):
                                b.op("pe", lambda e, jj=jj, pg=pg: e.matmul(pg.ap[:, :], lhsT=wg_.ap[:, jj, ci * 128:(ci + 1) * 128], rhs=hT[:, jj, blk(bi)], start=(jj == 0), stop=(jj == 7)),
                                     r=[wg_, d_hT[bi]], w=[pg])
                            for jj in range(8):
                                b.op("pe", lambda e, jj=jj, pu=pu: e.matmul(pu.ap[:, :], lhsT=wu_.ap[:, jj, ci * 128:(ci + 1) * 128], rhs=hT[:, jj, blk(bi)], start=(jj == 0), stop=(jj == 7)),
                                     r=[wu_, d_hT[bi]], w=[pu])
                            gp = gpad.next()
                            b.op("act", lambda e, gp=gp, pg=pg: e.activation(out=gp.ap[:, :, 1:65], in_=pg.ap[:, :].rearrange("p (r t) -> p r t", r=8), func=AF.Copy), r=[pg], w=[gp])
                            b.op("dve", lambda e, gp=gp: e.tensor_tensor(out=gp.ap[:, 1:8, 0], in0=gp.ap[:, 0:7, 64], in1=fmask[:, :], op=ALU.mult), r=[gp], w=[gp])
                            b.op("dve", lambda e, gp=gp: e.tensor_tensor(out=gp.ap[:, 0:7, 65], in0=gp.ap[:, 1:8, 1], in1=fmask[:, :], op=ALU.mult), r=[gp], w=[gp])
                            ac = acc.next()
                            a3 = ac.ap[:].rearrange("p (r t) -> p r t", r=8)
                            b.op("act", lambda e, gp=gp, a3=a3: e.activation(out=a3, in_=gp.ap[:, :, 0:64], func=AF.Identity, scale=pvl(PV_FW + c)), r=[gp], w=[ac])
                            b.op("dve", lambda e, gp=gp, a3=a3: e.scalar_tensor_tensor(out=a3, in0=gp.ap[:, :, 1:65], scalar=pvl(PV_FW + 22 + c), in1=a3, op0=ALU.mult, op1=ALU.add), r=[gp, ac], w=[ac])
                            b.op("dve", lambda e, gp=gp, a3=a3: e.scalar_tensor_tensor(out=a3, in0=gp.ap[:, :, 2:66], scalar=pvl(PV_FW + 44 + c), in1=a3, op0=ALU.mult, op1=ALU.add), r=[gp, ac], w=[ac])
                            sg = sgt.next()
                            b.op("act", lambda e, sg=sg, ac=ac: e.activation(out=sg.ap[:], in_=ac.ap[:], func=AF.Silu, bias=pvl(PV_FB + c)), r=[ac], w=[sg])
                            b.op("dve", lambda e, sg=sg, pu=pu: e.tensor_tensor(out=at.ap[:, ci, :], in0=pu.ap[:, :], in1=sg.ap[:], op=ALU.mult), r=[pu, sg], w=[at])
                            if ci == 1 and pend:
                                down_proj(*pend.pop(0))
                        pend.append((wd_, at, ng, bi))
                while pend:
                    down_proj(*pend.pop(0))
            b.barrier()
            fpw.close()
            if l == 0:
                ck(13, xTt[:, 0, :])
                for bi in range(4):
                    b.dma("sp", xscr[:, :, blk(bi)], xTt[:, :, blk(bi)], r=[d_x[bi]])
            else:
                with contextlib.ExitStack() as ph:
                    yT = sb("yT", [128, 8, 512], F32, ph)
                    d_y = [Dep() for _ in range(8)]
                    yo = rot("yo", 3, [128, D], F32, ph)
                    cur = {}

                    def fin(bi, j, tm):
                        b.op("act", lambda e: e.activation(out=yT[:, j, :], in_=tm.ap[:], func=AF.Identity, scale=pv[:, 0, 64 + j:65 + j]), r=[tm], w=[d_y[j]])
                        if j == 7:
                            for tt in range(4):
                                yt = yo.next()
                                for half in range(2):
                                    ps = psA.next()
                                    for q in range(4):
                                        jj = half * 4 + q
                                        b.op("pe", lambda e, jj=jj, q=q, ps=ps: e.transpose(ps.ap[:, q * 128:(q + 1) * 128], yT[:, jj, tt * 128:(tt + 1) * 128], ident), r=[d_y[jj]], w=[ps])
                                    b.op("act", lambda e, ps=ps, yt=yt, half=half: e.activation(out=yt.ap[:, half * 512:(half + 1) * 512], in_=ps.ap[:, :], func=AF.Copy), r=[ps], w=[yt])
                                t0 = bi * 512 + tt * 128
                                b.dma("sp", y_d[t0:t0 + 128, :], yt.ap[:], r=[yt])
                    norm_phase(xTt, ph, None, None, fin)
                    b.barrier()
                xs.close()
        for l in range(2):
            b.dma("sp", flru_d[l].rearrange("j p s -> p j s"), flru[:, l, :, :], r=[d_flru])
        b.barrier()
        DBG["nins"] = b.nins


_NC = None


def _consts():
    c = np.zeros((128, NCONST), np.float32)
    c[:, 0:128] = np.eye(128, dtype=np.float32)
    p = np.arange(128)
    c[:, 128:256] = (p[:, None] // 64 == p[None, :] // 64).astype(np.float32)
    s = np.arange(64)[:, None]
    t = np.arange(64)[None, :]
    c[0:64, 256:320] = -1.0 * (t > s)
    c[0:64, 320:384] = -1.0 * (t >= s)
    c[0:64, 384:448] = (t > s)
    c[0:64, 448:512] = (t >= s)
    c[0:64, 512:576] = -1.0 * (t < s)
    s2 = np.arange(32)[:, None]
    t2 = np.arange(32)[None, :]
    c[0:32, 576:608] = (t2 >= s2)
    c[0:32, 608:640] = (t2 >= s2)
    return c


def prep(inp):
    f = lambda k: np.ascontiguousarray(np.asarray(inp[k], dtype=np.float32))
    xp, xsm = f("x_prompt"), f("x_sample")
    L = 2
    pv = np.zeros((L, 128, NPV), np.float32)
    colT = lambda v: v.reshape(-1, 128).T
    for l in range(L):
        pv[l, :, 0:8] = colT(f("norm_mix_g")[l])
        pv[l, :, 8:16] = colT(f("norm_ffn_g")[l])
        pv[l, :, 16:64] = colT(f("ada_b")[l])
        pv[l, :, 64:72] = colT(f("final_g"))
        for j in range(4):
            c0 = PV_RW + j * 9
            sl = slice(j * 128, (j + 1) * 128)
            pv[l, :, c0 + 0] = f("rwkv_w0")[l, 0, sl]
            pv[l, :, c0 + 1] = f("rwkv_w0")[l, 1, sl]
            pv[l, :, c0 + 2] = f("rwkv_a0")[l, 0, sl]
            pv[l, :, c0 + 3] = f("rwkv_a0")[l, 1, sl]
            pv[l, :, c0 + 4] = f("rwkv_k_k")[l, sl]
            pv[l, :, c0 + 5] = f("rwkv_k_a")[l, sl]
            pv[l, :, c0 + 6] = f("rwkv_r_k")[l].reshape(-1)[sl]
            pv[l, :, c0 + 7] = f("rwkv_ln_w")[l, sl]
            pv[l, :, c0 + 8] = f("rwkv_ln_b")[l, sl]
        for j in range(2):
            c0 = PV_LRU + j * 11
            sl = slice(j * 128, (j + 1) * 128)
            for k in range(4):
                pv[l, :, c0 + k] = f("lru_conv_w")[l, k, sl]
            pv[l, :, c0 + 4] = f("lru_conv_b")[l, sl]
            for d in range(2):
                pv[l, :, c0 + 5 + d] = f("lru_ba")[l, d, sl]
                pv[l, :, c0 + 7 + d] = f("lru_bx")[l, d, sl]
                pv[l, :, c0 + 9 + d] = f("lru_lambda")[l, d, sl]
            c0 = PV_HG + j * 5
            for d in range(2):
                for l2 in range(2):
                    pv[l, :, c0 + d * 2 + l2] = f("hgrn_lb_logits")[d, l2, sl]
            pv[l, :, c0 + 4] = f("hgrn_norm_g")[l, sl]
        for k in range(3):
            pv[l, :, PV_FW + k * 22:PV_FW + (k + 1) * 22] = colT(f("ffn_conv_w")[l, k])
        pv[l, :, PV_FB:PV_FB + 22] = colT(f("ffn_conv_b")[l])
    consts = _consts()
    shared = dict(
        consts=consts, pv=pv, ada_w=f("ada_w"), w_in=f("w_in"), w_out=f("w_out"),
        rwkv_w_up=f("rwkv_w_up").reshape(2, 128, 512), rwkv_a_up=f("rwkv_a_up").reshape(2, 128, 512),
        rwkv_g_up=f("rwkv_g_up"), lru_wa=f("lru_wa"), lru_wx=f("lru_wx"),
        ffn_w_gate=f("ffn_w_gate"), ffn_w_up=f("ffn_w_up"), ffn_w_down=f("ffn_w_down"))
    srw, slru, shg = f("state_rwkv"), f("state_rglru"), f("state_hgrn")
    in_maps = []
    for core in range(8):
        m = dict(shared)
        irw = np.zeros((2, 2, 8, 4, 128, 64), np.float32)
        ilru = np.zeros((2, 2, 128, 16), np.float32)
        ihg = np.zeros((2, 2, 8, 2, 128, 64), np.float32)
        if core < 4:
            bb = core
            m["x"] = xsm[bb]
            m["cond"] = np.ascontiguousarray(f("c")[bb].reshape(8, 128).T)
            m["carry"] = np.ones((128, 1), np.float32)
            m["fmask"] = np.zeros((128, 7), np.float32)
            for l in range(2):
                for d in range(2):
                    st = srw[bb, l, d].transpose(0, 2, 1).reshape(4, 128, 64)
                    irw[l, d, 0] = st
                    ihg[l, d, 0] = shg[bb, l, d].reshape(2, 128, 64)
                    ilru[l, :, :, 0 * 2 + d] = slru[bb, l, d].reshape(2, 128)
        else:
            m["x"] = np.ascontiguousarray(xp[(core - 4) * 8:(core - 3) * 8].reshape(T, D))
            m["cond"] = np.ascontiguousarray(f("c_ctx").reshape(8, 128).T)
            m["carry"] = np.zeros((128, 1), np.float32)
            fm = np.ones((128, 7), np.float32)
            fm[:, 3] = 0.0
            m["fmask"] = fm
        m["irw"], m["ilru"], m["ihg"] = irw, ilru, ihg
        in_maps.append(m)
    return in_maps


def kernel(**inp):
    global _NC
    in_maps = prep(inp)
    xp = inp["x_prompt"]
    if _NC is None:
        _NC = build_nc()
    res = run_bass_kernel_spmd(_NC, in_maps, core_ids=list(range(8)))
    R = res.results
    y_prompt = np.concatenate([R[c]["y"].reshape(8, 256, D) for c in range(4, 8)], axis=0)
    y_sample = np.stack([R[c]["y"] for c in range(4)], axis=0)
    new_rwkv = np.zeros((32, 2, 2, 8, 64, 64), np.float32)
    new_lru = np.zeros((32, 2, 2, 256), np.float32)
    new_hg = np.zeros((32, 2, 2, 4, 64, 64), np.float32)
    for c in range(4, 8):
        frw, flr, fhg = R[c]["frw"], R[c]["flru"], R[c]["fhg"]
        for n in range(8):
            sq = (c - 4) * 8 + n
            for l in range(2):
                for d in range(2):
                    s = (7 - n) if d else n
                    new_rwkv[sq, l, d] = frw[l, d, s].reshape(8, 64, 64).transpose(0, 2, 1)
                    new_hg[sq, l, d] = fhg[l, d, s].reshape(4, 64, 64)
                    new_lru[sq, l, d] = flr[l, :, :, s * 2 + d].reshape(256)
    return (y_prompt, y_sample, new_rwkv, new_lru, new_hg)
```
